# Optimizing a Trainium2 kernel written in Bass

```python
import math
import jax, jax.numpy as jnp
from jax import lax
import numpy as np

D_MODEL = 1024
BATCH = 8
SEQ = 2048
DEPTH = 1
DEC_BATCH = 128
DEC_SEQ = 1
PAST_LEN = 16384
PAGE_SIZE = 128

GLA_HEADS = 4
GLA_DK = 64
GLA_DV = 128
GLA_QK = GLA_HEADS * GLA_DK
GLA_V = GLA_HEADS * GLA_DV
GLA_RANK = 16
GLA_GATE_TEMP = 16.0
SSD_HEADS = 8
SSD_HEADDIM = 64
SSD_INNER = SSD_HEADS * SSD_HEADDIM
SSD_GROUPS = 2
SSD_STATE = 64
CONV_W = 4
SSD_CONV_DIM = SSD_INNER + 2 * SSD_GROUPS * SSD_STATE
CHUNK = 64
EPS = 1e-6
PROJ_SIZES = (GLA_QK, GLA_QK, GLA_V, GLA_V, GLA_RANK, SSD_INNER, SSD_CONV_DIM, SSD_HEADS, D_MODEL, D_MODEL)
IN_DIM = sum(PROJ_SIZES)

kernel_name = 'gla_ssd_gated_hybrid_step'


def rmsnorm(x, g):
    xf = x.astype(jnp.float32)
    y = xf * lax.rsqrt(jnp.mean(xf * xf, axis=-1, keepdims=True) + EPS)
    return (y * g.astype(jnp.float32)).astype(x.dtype)


def _split_points(sizes):
    return [int(v) for v in np.cumsum(sizes)[:-1]]


def _chunking(L):
    c = min(CHUNK, L)
    return c, -(-L // c)


def _to_chunks(a, c, n):
    L = a.shape[1]
    a = jnp.pad(a, [(0, 0), (0, n * c - L)] + [(0, 0)] * (a.ndim - 2))
    a = a.reshape(a.shape[0], n, c, *a.shape[2:])
    return jnp.moveaxis(a, 1, 0)


def _from_chunks(a, L):
    a = jnp.moveaxis(a, 0, 1)
    a = a.reshape(a.shape[0], a.shape[1] * a.shape[2], *a.shape[3:])
    return a[:, :L]


def gla_recurrence(q, k, v, log_a, S0):
    L = q.shape[1]
    c, n = _chunking(L)
    mask = jnp.tril(jnp.ones((c, c), dtype=bool))[None, :, :, None, None]

    def step(S, inp):
        qc, kc, vc, ac = inp
        b = jnp.cumsum(ac, axis=1)
        decay = jnp.exp(jnp.where(mask, b[:, :, None] - b[:, None, :], -jnp.inf))
        A = jnp.einsum('bihd,bjhd,bijhd->bhij', qc, kc, decay)
        o = jnp.einsum('bhij,bjhe->bihe', A, vc) + jnp.einsum('bihd,bhde->bihe', qc * jnp.exp(b), S)
        b_last = b[:, -1]
        S_new = jnp.exp(b_last)[..., None] * S + jnp.einsum(
            'bjhd,bjhe->bhde', kc * jnp.exp(b_last[:, None] - b), vc)
        return S_new, o

    xs = tuple(_to_chunks(a, c, n) for a in (q, k, v, log_a))
    S_fin, o = lax.scan(step, S0, xs)
    return _from_chunks(o, L), S_fin


def ssd_recurrence(x, dt, A, Bh, Ch, h0):
    L = x.shape[1]
    c, n = _chunking(L)
    mask = jnp.tril(jnp.ones((c, c), dtype=bool))[None, :, :, None]
    la = dt * A

    def step(h, inp):
        xc, dtc, lac, bc, cc = inp
        cum = jnp.cumsum(lac, axis=1)
        decay = jnp.exp(jnp.where(mask, cum[:, :, None] - cum[:, None, :], -jnp.inf))
        scores = jnp.einsum('bihn,bjhn->bijh', cc, bc) * decay
        xdt = xc * dtc[..., None]
        y = jnp.einsum('bijh,bjhp->bihp', scores, xdt) + jnp.einsum(
            'bihn,bhpn->bihp', cc * jnp.exp(cum)[..., None], h)
        c_last = cum[:, -1]
        h_new = jnp.exp(c_last)[:, :, None, None] * h + jnp.einsum(
            'bjhp,bjhn->bhpn', xdt * jnp.exp(c_last[:, None] - cum)[..., None], bc)
        return h_new, y

    xs = tuple(_to_chunks(a, c, n) for a in (x, dt, la, Bh, Ch))
    h_fin, y = lax.scan(step, h0, xs)
    return _from_chunks(y, L), h_fin


def causal_conv(u, buf, w, bias):
    full = jnp.concatenate([buf.astype(u.dtype), u], axis=1)
    out = lax.conv_general_dilated(full, w[:, None, :].astype(u.dtype), window_strides=(1,), padding='VALID',
                                   dimension_numbers=('NWC', 'WIO', 'NWC'), feature_group_count=u.shape[-1])
    return out + bias.astype(u.dtype), full[:, -(CONV_W - 1):]


def mixer_layer(x, S0, h0, buf, norm_g, w_in, w_a2, b_a2, gla_norm_g, conv_w, conv_b, dt_bias, a_log,
                d_skip, ssd_norm_g, w_br_gla, w_br_ssd, w_out):
    f32 = jnp.float32
    Bsz, L, _ = x.shape
    hn = rmsnorm(x, norm_g)
    proj = hn @ w_in
    q, k, v, g, a_lr, z, xbc, dt_raw, gate_a, gate_b = jnp.split(proj, _split_points(PROJ_SIZES), axis=-1)

    qh = q.reshape(Bsz, L, GLA_HEADS, GLA_DK).astype(f32) * (GLA_DK ** -0.5)
    kh = k.reshape(Bsz, L, GLA_HEADS, GLA_DK).astype(f32)
    vh = v.reshape(Bsz, L, GLA_HEADS, GLA_DV).astype(f32)
    log_a = (jax.nn.log_sigmoid((a_lr @ w_a2 + b_a2).astype(f32)) / GLA_GATE_TEMP).reshape(Bsz, L, GLA_HEADS, GLA_DK)
    o, S_new = gla_recurrence(qh, kh, vh, log_a, S0.astype(f32))
    o = rmsnorm(o, gla_norm_g.reshape(GLA_HEADS, GLA_DV)).reshape(Bsz, L, GLA_V)
    o = (o * jax.nn.silu(g.astype(f32))).astype(x.dtype)

    xbc_c, buf_new = causal_conv(xbc, buf, conv_w, conv_b)
    xbc_c = jax.nn.silu(xbc_c.astype(f32))
    xs, Bm, Cm = jnp.split(xbc_c, [SSD_INNER, SSD_INNER + SSD_GROUPS * SSD_STATE], axis=-1)
    dt = jax.nn.softplus(dt_raw.astype(f32) + dt_bias.astype(f32))
    A = -jnp.exp(a_log.astype(f32))
    xs_h = xs.reshape(Bsz, L, SSD_HEADS, SSD_HEADDIM)
    rep = SSD_HEADS // SSD_GROUPS
    Bh = jnp.repeat(Bm.reshape(Bsz, L, SSD_GROUPS, SSD_STATE), rep, axis=2)
    Ch = jnp.repeat(Cm.reshape(Bsz, L, SSD_GROUPS, SSD_STATE), rep, axis=2)
    y, h_new = ssd_recurrence(xs_h, dt, A, Bh, Ch, h0.astype(f32))
    y = y + d_skip.astype(f32)[:, None] * xs_h
    y = y.reshape(Bsz, L, SSD_INNER) * jax.nn.silu(z.astype(f32))
    y = rmsnorm(y.reshape(Bsz, L, SSD_GROUPS, SSD_INNER // SSD_GROUPS),
                ssd_norm_g.reshape(SSD_GROUPS, SSD_INNER // SSD_GROUPS)).reshape(Bsz, L, SSD_INNER).astype(x.dtype)

    merged = jax.nn.sigmoid(gate_a) * (o @ w_br_gla) + jax.nn.sigmoid(gate_b) * (y @ w_br_ssd)
    out = x + merged @ w_out
    return out, S_new.astype(x.dtype), h_new.astype(x.dtype), buf_new.astype(x.dtype)


def trunk(x, s_gla, s_ssm, s_conv, params, final_norm_g):
    new_g, new_s, new_c = [], [], []
    for l in range(DEPTH):
        x, sg, ss, sc = mixer_layer(x, s_gla[l], s_ssm[l], s_conv[l], *[p[l] for p in params])
        new_g.append(sg)
        new_s.append(ss)
        new_c.append(sc)
    return rmsnorm(x, final_norm_g), jnp.stack(new_g), jnp.stack(new_s), jnp.stack(new_c)


def setup_inputs(seed: int = 0) -> dict:
    key = jax.random.key(seed)
    ks = jax.random.split(key, 20)
    f = jnp.float32

    def nrm(k, shape, s):
        return jax.random.normal(k, shape, f) * s

    dt0 = jnp.exp(jax.random.uniform(ks[12], (DEPTH, SSD_HEADS), f, math.log(1e-3), math.log(1e-1)))
    return {
        'x_prompt': nrm(ks[0], (BATCH, SEQ, D_MODEL), 1.0),
        'x_sample': nrm(ks[1], (DEC_BATCH, DEC_SEQ, D_MODEL), 1.0),
        'state_gla': nrm(ks[2], (DEPTH, DEC_BATCH, GLA_HEADS, GLA_DK, GLA_DV), 1.0),
        'state_ssm': nrm(ks[3], (DEPTH, DEC_BATCH, SSD_HEADS, SSD_HEADDIM, SSD_STATE), 0.5),
        'state_conv': nrm(ks[4], (DEPTH, DEC_BATCH, CONV_W - 1, SSD_CONV_DIM), 1.0),
        'norm_g': 1.0 + nrm(ks[5], (DEPTH, D_MODEL), 0.02),
        'w_in': nrm(ks[6], (DEPTH, D_MODEL, IN_DIM), D_MODEL ** -0.5),
        'w_a2': nrm(ks[7], (DEPTH, GLA_RANK, GLA_QK), GLA_RANK ** -0.5),
        'b_a2': nrm(ks[8], (DEPTH, GLA_QK), 0.1),
        'gla_norm_g': 1.0 + nrm(ks[9], (DEPTH, GLA_V), 0.02),
        'conv_w': nrm(ks[10], (DEPTH, CONV_W, SSD_CONV_DIM), CONV_W ** -0.5),
        'conv_b': nrm(ks[11], (DEPTH, SSD_CONV_DIM), 0.01),
        'dt_bias': dt0 + jnp.log(-jnp.expm1(-dt0)),
        'a_log': jnp.log(jax.random.uniform(ks[13], (DEPTH, SSD_HEADS), f, 1.0, 16.0)),
        'd_skip': 1.0 + nrm(ks[14], (DEPTH, SSD_HEADS), 0.1),
        'ssd_norm_g': 1.0 + nrm(ks[15], (DEPTH, SSD_INNER), 0.02),
        'w_br_gla': nrm(ks[16], (DEPTH, GLA_V, D_MODEL), GLA_V ** -0.5),
        'w_br_ssd': nrm(ks[17], (DEPTH, SSD_INNER, D_MODEL), SSD_INNER ** -0.5),
        'w_out': nrm(ks[18], (DEPTH, D_MODEL, D_MODEL), D_MODEL ** -0.5),
        'final_norm_g': 1.0 + nrm(ks[19], (D_MODEL,), 0.02),
    }


def reference(x_prompt, x_sample, state_gla, state_ssm, state_conv, norm_g, w_in, w_a2, b_a2, gla_norm_g,
              conv_w, conv_b, dt_bias, a_log, d_skip, ssd_norm_g, w_br_gla, w_br_ssd, w_out, final_norm_g):
    params = (norm_g, w_in, w_a2, b_a2, gla_norm_g, conv_w, conv_b, dt_bias, a_log, d_skip, ssd_norm_g,
              w_br_gla, w_br_ssd, w_out)
    bp = x_prompt.shape[0]
    dtp = x_prompt.dtype
    zero_gla = jnp.zeros((DEPTH, bp, GLA_HEADS, GLA_DK, GLA_DV), dtp)
    zero_ssm = jnp.zeros((DEPTH, bp, SSD_HEADS, SSD_HEADDIM, SSD_STATE), dtp)
    zero_conv = jnp.zeros((DEPTH, bp, CONV_W - 1, SSD_CONV_DIM), dtp)
    y_prompt, gla_p, ssm_p, conv_p = trunk(x_prompt, zero_gla, zero_ssm, zero_conv, params, final_norm_g)
    y_sample, gla_s, ssm_s, conv_s = trunk(x_sample, state_gla, state_ssm, state_conv, params, final_norm_g)
    return (y_prompt, y_sample, gla_p, ssm_p, conv_p, gla_s, ssm_s, conv_s)
```

```python
import numpy as np
from contextlib import ExitStack
import concourse.bass as bass
import concourse.mybir as mybir
from concourse.bass_utils import run_bass_kernel_spmd

F32 = mybir.dt.float32
BF16 = mybir.dt.bfloat16
I32 = mybir.dt.int32
AF = mybir.ActivationFunctionType
ALU = mybir.AluOpType
AX = mybir.AxisListType

ENGS = ("pe", "act", "dve", "pool", "sp")


class Op:
    __slots__ = ("eng", "fn", "waits", "dma_key", "dma_n", "ev")

    def __init__(self, eng, fn):
        self.eng = eng
        self.fn = fn
        self.waits = []
        self.dma_key = None
        self.dma_n = 0
        self.ev = None


class Prog:
    def __init__(self, nc):
        self.nc = nc
        self.ops = {e: [] for e in ENGS}
        self.res = {}
        self.waited = {e: {} for e in ENGS}
        self.dma_count = {}
        self.all_dma_keys = []
        self.nprog = {e: 0 for e in ENGS}
        import os
        self.limit = int(os.environ.get("MK_MAXOPS", "100000000"))
        self.count = 0
        self.log = []

    def _deps(self, reads, writes):
        deps = []
        for r in reads:
            st = self.res.get(r)
            if st and st[0] is not None:
                deps.append(st[0])
            if st and (r.startswith("pb") or r == "pt"):
                deps.extend(st[1])
        for w in writes:
            st = self.res.get(w)
            if st:
                if st[0] is not None:
                    deps.append(st[0])
                deps.extend(st[1])
        return deps

    def _commit(self, ev, reads, writes):
        for r in reads:
            st = self.res.setdefault(r, [None, []])
            st[1].append(ev)
        for w in writes:
            self.res[w] = [ev, []]

    def _add_waits(self, op, deps):
        eng = op.eng
        best = {}
        for (k, v) in deps:
            if k == eng and eng == "pe":
                continue
            if best.get(k, 0) < v:
                best[k] = v
        for k, v in best.items():
            if self.waited[eng].get(k, 0) >= v:
                continue
            self.waited[eng][k] = v
            op.waits.append((k, v))

    def op(self, eng, fn, reads=(), writes=()):
        self.count += 1
        if self.count > self.limit:
            return None
        self.log.append((self.count, eng, tuple(reads), tuple(writes)))
        o = Op(eng, fn)
        deps = self._deps(reads, writes)
        self._add_waits(o, deps)
        self.ops[eng].append(o)
        self.nprog[eng] += 1
        ev = (eng, self.nprog[eng])
        o.ev = ev
        self._commit(ev, reads, writes)
        return o

    def dma(self, queue, key, fn, reads=(), writes=(), n=1):
        self.count += 1
        if self.count > self.limit:
            return None
        self.log.append((self.count, "dma:" + queue, tuple(reads), tuple(writes)))
        o = Op(queue, fn)
        deps = self._deps(reads, writes)
        self._add_waits(o, deps)
        semkey = "dma:" + key
        if semkey not in self.dma_count:
            self.dma_count[semkey] = 0
            self.all_dma_keys.append(semkey)
        self.dma_count[semkey] += 16 * n
        o.dma_key = semkey
        o.dma_n = n
        self.ops[queue].append(o)
        ev = (semkey, self.dma_count[semkey])
        o.ev = ev
        self._commit(ev, reads, writes)
        return o

    def barrier(self):
        evs = []
        for e in ENGS:
            if self.nprog[e]:
                evs.append((e, self.nprog[e]))
        for k in self.all_dma_keys:
            evs.append((k, self.dma_count[k]))
        for e in ENGS:
            o = Op(e, None)
            self._add_waits(o, [ev for ev in evs if not (ev[0] == e)])
            self.ops[e].append(o)
            self.nprog[e] += 1
            o.ev = (e, self.nprog[e])
        self.res = {}

    def finish(self):
        evs = []
        for e in ENGS:
            if e != "sp" and self.nprog[e]:
                evs.append((e, self.nprog[e]))
        for k in self.all_dma_keys:
            evs.append((k, self.dma_count[k]))
        o = Op("sp", None)
        self._add_waits(o, evs)
        self.ops["sp"].append(o)
        self.nprog["sp"] += 1
        o.ev = ("sp", self.nprog["sp"])

    def emit(self, stack):
        nc = self.nc
        sems = {}
        for e in ENGS:
            sems[e] = stack.enter_context(nc.semaphore("s_" + e))
        for i, k in enumerate(self.all_dma_keys):
            sems[k] = stack.enter_context(nc.semaphore("d%d" % i))
        block = stack.enter_context(nc.Block())
        ops = self.ops

        def replay(ename, eng):
            mysem = sems[ename]
            for o in ops[ename]:
                for (k, v) in o.waits:
                    eng.wait_ge(sems[k], v)
                if o.dma_key is not None:
                    instrs = o.fn(eng)
                    assert len(instrs) == o.dma_n, (len(instrs), o.dma_n)
                    for ins in instrs:
                        ins.then_inc(sems[o.dma_key], 16)
                elif o.fn is None:
                    eng.nop().then_inc(mysem, 1)
                else:
                    ins = o.fn(eng)
                    ins.then_inc(mysem, 1)

        @block.tensor
        def _(eng):
            replay("pe", eng)

        @block.scalar
        def _(eng):
            replay("act", eng)

        @block.vector
        def _(eng):
            replay("dve", eng)

        @block.gpsimd
        def _(eng):
            replay("pool", eng)

        @block.sync
        def _(eng):
            replay("sp", eng)


D = 1024
T = 2048
NS = 16
IN_DIM = 4888
C_Q, C_K, C_V, C_G, C_A, C_Z, C_X, C_DT, C_GA, C_GB = 0, 256, 512, 1024, 1536, 1552, 2064, 2832, 2840, 3864
EPS = 1e-6
SC = 256
NSC = T // SC
CH = 128
NEG = -30000.0


class K:
    def __init__(self, nc, P, st):
        self.nc, self.P, self.st = nc, P, st
        self.bank_i = 0

    def sb(self, name, shape, dt):
        return self.st.enter_context(self.nc.sbuf_tensor(name, shape, dt))

    def ps(self, name, shape, dt):
        return self.st.enter_context(self.nc.psum_tensor(name, shape, dt))

    def mmg(self, mms, reads, writes):
        def fn(e, mms=mms):
            ins = None
            for (o, l, r, s0, s1) in mms:
                ins = e.matmul(o, lhsT=l, rhs=r, start=s0, stop=s1)
            return ins
        return self.P.op("pe", fn, reads, writes)

    def trg(self, trs, reads, writes):
        def fn(e, trs=trs):
            ins = None
            for (o, i, idn) in trs:
                ins = e.transpose(o, i, idn)
            return ins
        return self.P.op("pe", fn, reads, writes)

    def act(self, out, in_, func, reads, writes, **kw):
        def fn(e):
            return e.activation(out=out, in_=in_, func=func, **kw)
        return self.P.op("act", fn, reads, writes)

    def tt(self, eng, out, in0, in1, op, reads, writes):
        def fn(e):
            return e.tensor_tensor(out=out, in0=in0, in1=in1, op=op)
        return self.P.op(eng, fn, reads, writes)

    def ts(self, eng, out, in0, s1, s2, op0, op1, reads, writes):
        def fn(e):
            if op1 is None and eng == "pool" and op0 == ALU.mult:
                return e.tensor_scalar(out=out, in0=in0, scalar1=s1, scalar2=0.0, op0=ALU.mult, op1=ALU.add)
            if op1 is None:
                return e.tensor_scalar(out=out, in0=in0, scalar1=s1, scalar2=None, op0=op0)
            return e.tensor_scalar(out=out, in0=in0, scalar1=s1, scalar2=s2, op0=op0, op1=op1)
        return self.P.op(eng, fn, reads, writes)

    def stt(self, out, in0, scalar, in1, op0, op1, reads, writes):
        def fn(e):
            return e.scalar_tensor_tensor(out=out, in0=in0, scalar=scalar, in1=in1, op0=op0, op1=op1)
        return self.P.op("dve", fn, reads, writes)

    def cp(self, eng, out, in_, reads, writes):
        if eng == "act":
            return self.act(out, in_, AF.Identity, reads, writes)
        def fn(e):
            return e.tensor_copy(out=out, in_=in_)
        return self.P.op(eng, fn, reads, writes)

    def memset(self, eng, ap, val, writes):
        def fn(e):
            return e.memset(ap, val)
        return self.P.op(eng, fn, (), writes)

    def dma(self, q, key, pairs, reads, writes, slow=False):
        def fn(e, pairs=pairs):
            if slow:
                return [e.dma_start(out=o, in_=i, allow_slow_non_contiguous=True) for (o, i) in pairs]
            return [e.dma_start(out=o, in_=i) for (o, i) in pairs]
        return self.P.dma(q, key, fn, reads, writes, n=len(pairs))


def bc_rows(ap1d, nparts, n, off=0):
    return bass.AP(ap1d.tensor, off, [[0, nparts], [1, n]])


def build_program(with_sample=True, stop=None):
    nc = bass.Bass("TRN2", target_bir_lowering=False)
    P = Prog(nc)

    def dr(n, s, kind="ExternalInput", dt=F32):
        return nc.dram_tensor(n, s, dt, kind=kind).ap()

    xp = dr("xp", [T, D]); xs_d = dr("xs", [NS, D])
    sgla = dr("sgla", [64, 2, 4096]); sssm = dr("sssm", [32, 4, 4096]); sconv = dr("sconv", [NS, 3, 768])
    norm_g = dr("norm_g", [D]); w_in = dr("w_in", [D, IN_DIM]); w_a2 = dr("w_a2", [16, 256]); b_a2 = dr("b_a2", [256])
    gla_ng = dr("gla_norm_g", [512]); conv_w = dr("conv_w", [4, 768]); conv_b = dr("conv_b", [768])
    dt_bias = dr("dt_bias", [8]); a_log = dr("a_log", [8]); d_skip = dr("d_skip", [8]); ssd_ng = dr("ssd_norm_g", [512])
    w_brg_d = dr("w_br_gla", [512, D]); w_brs_d = dr("w_br_ssd", [512, D]); w_out_d = dr("w_out", [D, D]); fng = dr("final_norm_g", [D])
    yp = dr("yp", [T, D], "ExternalOutput"); ys_o = dr("ys", [NS, D], "ExternalOutput")
    glap = dr("glap", [256, 128], "ExternalOutput"); ssmp = dr("ssmp", [512, 64], "ExternalOutput")
    convp = dr("convp", [3, 768], "ExternalOutput")
    glas = dr("glas", [64, 2, 4096], "ExternalOutput"); ssms = dr("ssms", [32, 4, 4096], "ExternalOutput")
    convs = dr("convs", [NS, 3, 768], "ExternalOutput")

    with ExitStack() as st:
        k = K(nc, P, st)
        sb, ps = k.sb, k.ps
        w_in_bf = sb("w_in_bf", [128, 8, IN_DIM], BF16)
        wbrg = sb("wbrg", [128, 4, D], BF16)
        wbrs = sb("wbrs", [128, 4, D], BF16)
        wout = sb("wout", [128, 8, D], BF16)
        wa2 = sb("wa2", [16, 256], BF16)
        ident_bf = sb("ident_bf", [128, 128], BF16)
        ident_f = sb("ident_f", [128, 128], F32)
        tri_f = sb("tri_f", [128, 128], F32)
        mask_bf = sb("mask_bf", [128, 128], BF16)
        ones_f = sb("ones_f", [128, 128], F32)
        ones_bf = sb("ones_bf", [128, 128], BF16)
        negh = sb("negh", [128, 1], F32)
        stg = sb("stg", [48, 128], F32)
        prm = sb("prm", [128, 48], F32)
        nba2 = sb("nba2", [128, 2], F32)
        dtb_rep = sb("dtb_rep", [128, 8], F32)
        A_rep = sb("A_rep", [128, 8], F32)
        dsk_rep = sb("dsk_rep", [128, 8], F32)
        dskc = sb("dskc", [128, 4], F32)
        fng_rep = sb("fng_rep", [128, D], F32)
        pt = ps("pt", [128, 1024], BF16)
        banks = [ps("pb%d" % i, [128, 512], F32) for i in range(7)]

        held = set()

        pool_i = {}

        def bank(hold=False, pool=(0, 1, 2, 3, 4, 5, 6)):
            n = len(pool)
            i0 = pool_i.get(pool, 0)
            for t in range(n):
                i = pool[(i0 + t) % n]
                if i not in held:
                    pool_i[pool] = (i0 + t + 1) % n
                    if hold:
                        held.add(i)
                    return banks[i], "pb%d" % i
            raise RuntimeError("no free PSUM bank in pool %r" % (pool,))

        def release(bn):
            held.discard(int(bn[2:]))

        def v4(ap, h):
            return ap.rearrange("p (h i) -> p h i", h=h)

        k.memset("pool", ones_f[:], 1.0, ["ones_f"])
        k.memset("pool", ones_bf[:], 1.0, ["ones_bf"])
        k.memset("pool", negh[:], -0.5, ["negh"])
        P.op("pool", lambda e: e.affine_select(out=tri_f[:], in_=ones_f[:], pattern=[[1, 128]], compare_op=ALU.is_ge,
                                               fill=0.0, base=0, channel_multiplier=-1), ["ones_f"], ["tri_f"])
        P.op("pool", lambda e: e.affine_select(out=ident_f[:], in_=ones_f[:], pattern=[[1, 128]], compare_op=ALU.is_equal,
                                               fill=0.0, base=0, channel_multiplier=-1), ["ones_f"], ["ident_f"])
        k.cp("pool", mask_bf[:], tri_f[:], ["tri_f"], ["mask_bf"])
        k.cp("pool", ident_bf[:], ident_f[:], ["ident_f"], ["ident_bf"])

        lvl = stop[1] if (stop is not None and stop[0] == 0) else 99
        ng = prm[:, 0:8]
        cb = prm[:, 10:16]

        def cw(w, cc):
            return prm[:, 24 + w * 6 + cc: 25 + w * 6 + cc]

        xt = [sb("xt%d" % i, [128, D], F32) for i in range(2)]
        xr = [sb("xr%d" % i, [128, D], F32) for i in range(2)]
        xn = sb("xn", [128, D], BF16)
        scrB = sb("scrB", [128, 1024], BF16)
        ATm = scrB[:, 0:512].rearrange("p (h i) -> p h i", h=4)
        Mg = scrB[:, 512:1024].rearrange("p (h i) -> p h i", h=4)
        ss = sb("ss", [128, 8], F32)
        xnT = [sb("xnT%d" % i, [128, 8, SC], BF16) for i in range(2)]
        alr = sb("alr", [16, SC], BF16)
        bpos = sb("bpos", [128, 2, SC], F32)
        eb = sb("eb", [128, 2, SC], F32)
        enb = sb("enb", [128, 2, SC], F32)
        ebl = [sb("ebl%d" % i, [128, 2, 2], F32) for i in range(2)]
        qtT = [sb("qtT%d" % i, [128, 2, SC], BF16) for i in range(2)]
        ktT = [sb("ktT%d" % i, [128, 2, SC], BF16) for i in range(2)]
        gs = [sb("gs%d" % i, [128, 4, SC], BF16) for i in range(2)]
        zs = [sb("zs%d" % i, [128, 4, SC], BF16) for i in range(2)]
        xbc_raw = sb("xbc_raw", [128, 6, SC + 3], F32)
        cacc = sb("cacc", [128, SC], F32)
        xsT = [sb("xsT%d" % i, [128, 4, SC], F32) for i in range(2)]
        BT = [sb("BT%d" % i, [128, SC], BF16) for i in range(2)]
        CT = [sb("CT%d" % i, [128, SC], BF16) for i in range(2)]
        vtok = sb("vtok", [128, 2, 512], BF16)
        dtt = [sb("dtt%d" % i, [128, 2, 8], F32) for i in range(2)]
        latok = [sb("latok%d" % i, [128, 2, 8], F32) for i in range(2)]
        ktok = sb("ktok", [128, 256], BF16)
        S_f = sb("S_f", [128, 2, 128], F32)
        S_bf = sb("S_bf", [128, 2, 128], BF16)
        hT_f = sb("hT_f", [128, 4, 64], F32)
        hT_bf = sb("hT_bf", [128, 4, 64], BF16)
        tmp4 = sb("tmp4", [128, 4, 128], F32)
        diff = tmp4
        y1 = tmp4
        sq = sb("sq", [128, 4, 128], BF16)
        rs = sb("rs", [128, 4, 128], F32)
        expcum = rs
        ccol = sb("ccol", [128, 8], F32)
        decay = sb("decay", [128, 8, 128], BF16)
        scm = sb("scm", [128, 2, 128], BF16)
        CsT = sb("CsT", [128, 4, 128], BF16)
        xdt = sb("xdt", [128, 8, 64], BF16)
        xw = sb("xw", [128, 8, 64], BF16)
        Btok = sb("Btok", [128, 128], BF16)
        ogT = sb("ogT", [128, 4, SC], BF16)
        ygT = sb("ygT", [128, 4, SC], BF16)
        tg = sb("tg", [128, 2, SC], F32)
        mrgT = sb("mrgT", [128, 8, SC], BF16)
        sm = sb("sm", [128, 64], F32)

        w_in_v = w_in.rearrange("(kc p) c -> p kc c", p=128)
        col_blocks = [(C_A, C_A + 16), (C_DT, C_DT + 8), (C_K, C_K + 256), (C_Q, C_Q + 256), (C_X, C_X + 384), (C_X + 384, C_X + 768),
                      (C_G, C_G + 512), (C_Z, C_Z + 512), (C_V, C_V + 512),
                      (C_GA, C_GA + 512), (C_GA + 512, C_GA + 1024), (C_GB, C_GB + 512), (C_GB + 512, C_GB + 1024)]
        if lvl >= 2:
            k.dma("pool", "wa2", [(wa2[:, :], w_a2)], [], ["wa2"])
        wthr = [0]

        def wdma(key, out_ap, in_ap):
            k.dma("pool", key, [(out_ap, in_ap)], [], [key, "wthr%d" % (wthr[0] % 6)])
            wthr[0] += 1

        N_EARLY = 9
        if lvl >= 2:
            for (c0, c1) in col_blocks[:N_EARLY]:
                wdma("w_in_%d" % c0, w_in_bf[:, :, c0:c1], w_in_v[:, :, c0:c1])

        def wres(c0, c1):
            return ["w_in_%d" % a for (a, b) in col_blocks if a < c1 and b > c0]

        k.memset("dve", S_f[:], 0.0, ["S_f"])
        k.memset("dve", S_bf[:], 0.0, ["S_bf"])
        k.memset("dve", hT_f[:], 0.0, ["hT_f"])
        k.memset("dve", hT_bf[:], 0.0, ["hT_bf"])
        k.memset("dve", xbc_raw[:], 0.0, ["xbc_raw%d" % _c for _c in range(6)])

        def load_x(tile_idx):
            slot = tile_idx % 2
            k.dma("sp", "xt%d" % slot, [(xt[slot][:], xp[tile_idx * 128:(tile_idx + 1) * 128, :])], [], ["xt%d" % slot])

        if stop is None or stop[0] > 0:
            load_x(0)
            load_x(1)
        if lvl >= 1:
            k.dma("sp", "stg", [
                (stg[0:8, :], norm_g.rearrange("(a p) -> a p", p=128)),
                (stg[8:10, :], b_a2.rearrange("(a p) -> a p", p=128)),
                (stg[10:16, :], conv_b.rearrange("(a p) -> a p", p=128)),
                (stg[16:20, :], gla_ng.rearrange("(a p) -> a p", p=128)),
                (stg[20:24, :], ssd_ng.rearrange("(a p) -> a p", p=128)),
                (stg[24:48, :], conv_w.rearrange("w (a p) -> (w a) p", p=128)),
            ], [], ["stg"])
            k.dma("sp", "prm2", [
                (dtb_rep[:], bc_rows(dt_bias, 128, 8)),
                (A_rep[:], bc_rows(a_log, 128, 8)),
                (dsk_rep[:], bc_rows(d_skip, 128, 8)),
                (fng_rep[:], bc_rows(fng, 128, D)),
            ], [], ["dtb_rep", "A_rep", "dsk_rep", "fng_rep"])
            b0, b0n = bank()
            k.trg([(b0[:, 0:48], stg[:, :], ident_f[0:48, 0:48])], ["stg", "ident_f"], [b0n])
            k.cp("dve", prm[:], b0[:, 0:48], [b0n], ["prm"])
            k.ts("dve", nba2[:], prm[:, 8:10], -1.0, None, ALU.mult, None, ["prm"], ["nba2"])
            k.act(A_rep[:], A_rep[:], AF.Exp, ["A_rep"], ["A_rep"])
            k.ts("dve", A_rep[:], A_rep[:], -1.0, None, ALU.mult, None, ["A_rep"], ["A_rep"])
            dsk2 = dsk_rep[:, :].rearrange("p (c two) -> p c two", two=2)
            k.cp("dve", dskc[0:64, :], dsk2[0:64, :, 0], ["dsk_rep"], ["dskc"])
            k.cp("dve", dskc[64:128, :], dsk2[64:128, :, 1], ["dsk_rep", "dskc"], ["dskc"])
        def rstd_from_ss(col, n, out_col):
            rn = "ss%d" % col
            k.ts("pool", sm[:, out_col:out_col + 1], ss[:, col:col + 1], 1.0 / n, EPS, ALU.mult, ALU.add, [rn], ["sm%d" % out_col])
            k.tt("pool", sm[:, out_col:out_col + 1], sm[:, out_col:out_col + 1], negh[:, 0:1], ALU.pow, ["sm%d" % out_col, "negh"],
                 ["sm%d" % out_col])

        def staged_weight(dst3, src2d, nkc, scale_cols, const_scale, name):
            for kc in range(nkc):
                slot = kc % 2
                k.dma("sp", "xr%d" % slot, [(xr[slot][:], src2d[kc * 128:(kc + 1) * 128, :])], [], ["xr%d" % slot])
                if scale_cols is not None:
                    k.act(dst3[:, kc, :], xr[slot][:], AF.Identity, ["xr%d" % slot, "prm"], [name], scale=scale_cols[:, kc:kc + 1])
                else:
                    k.act(dst3[:, kc, :], xr[slot][:], AF.Identity, ["xr%d" % slot], [name], scale=const_scale)

        POOL_A = (0, 1, 2)
        POOL_B = (3, 4, 5, 6)

        def stage1a(s):
            pb = s % 2
            xs_ = xnT[pb]
            xsn = "xnT%d" % pb
            sfx = str(pb)
            for ti in range(2):
                tile_idx = 2 * s + ti
                slot = tile_idx % 2
                xtn = "xt%d" % slot
                k.act(xn[:], xt[slot][:], AF.Square, [xtn], ["xn", "ss0"], accum_out=ss[:, 0:1])
                yield
                rstd_from_ss(0, D, 0)
                k.ts("pool", xn[:], xt[slot][:], sm[:, 0:1], None, ALU.mult, None, [xtn, "sm0"], ["xn"])
                yield
                bxt, bxtn = bank(pool=POOL_A)
                ptv = bxt[:, :].bitcast(BF16).rearrange("p (kc t) -> p kc t", kc=8)
                k.trg([(ptv[:, kc, :], xn[:, kc * 128:(kc + 1) * 128], ident_bf[:]) for kc in range(8)], ["xn", "ident_bf"], [bxtn])
                k.tt("dve", xs_[:, :, ti * 128:(ti + 1) * 128], ptv, ng.unsqueeze(2).broadcast_to([128, 8, 128]), ALU.mult,
                     [bxtn, "prm"], [xsn])
                if tile_idx + 2 < 2 * NSC:
                    load_x(tile_idx + 2)
                yield

            def proj_fm(c0, m):
                b, bn = bank(pool=POOL_A)
                k.mmg([(b[0:m, 0:SC], w_in_bf[:, kc, c0:c0 + m], xs_[:, kc, :], kc == 0, kc == 7) for kc in range(8)],
                      wres(c0, c0 + m) + [xsn], [bn])
                return b, bn

            b, bn = proj_fm(C_A, 16)
            k.cp("act", alr[:, :], b[0:16, 0:SC], [bn], ["alr"])
            yield
            b, bn = bank(pool=POOL_A)
            bv = v4(b[:, :], 2)
            k.mmg([(bv[:, cc, :], wa2[:, cc * 128:(cc + 1) * 128], alr[:, :], True, True) for cc in range(2)], ["wa2", "alr"], [bn])
            yield
            for cc in range(2):
                k.act(eb[:, cc, :], bv[:, cc, :], AF.Exp, [bn, "nba2"], ["eb"], scale=-1.0, bias=nba2[:, cc:cc + 1])
            k.act(eb[:, :, :], eb[:, :, :], AF.Ln, ["eb"], ["eb"], bias=1.0)
            yield
            b, bn = bank(pool=POOL_A)
            bdt = b[:, 0:16].rearrange("p (t h) -> p t h", t=2)
            for ti in range(2):
                k.mmg([(bdt[:, ti, :], xs_[:, kc, ti * 128:(ti + 1) * 128], w_in_bf[:, kc, C_DT:C_DT + 8], kc == 0, kc == 7)
                       for kc in range(8)], wres(C_DT, C_DT + 8) + [xsn], [bn])
            dtn, lan = "dtt" + sfx, "latok" + sfx
            k.tt("dve", dtt[pb][:, :, :], bdt, dtb_rep[:, :].unsqueeze(1).broadcast_to([128, 2, 8]), ALU.add, [bn, "dtb_rep"], [dtn])
            yield
            k.act(dtt[pb][:, :, :], dtt[pb][:, :, :], AF.Exp, [dtn], [dtn])
            k.act(dtt[pb][:, :, :], dtt[pb][:, :, :], AF.Ln, [dtn], [dtn], bias=1.0)
            yield
            k.tt("dve", latok[pb][:, :, :], dtt[pb][:, :, :], A_rep[:, :].unsqueeze(1).broadcast_to([128, 2, 8]), ALU.mult,
                 [dtn, "A_rep"], [lan])
            for cc in range(2):
                for ci in range(2):
                    sl = slice(ci * 128, (ci + 1) * 128)

                    def fn(e, cc=cc, sl=sl):
                        return e.tensor_tensor_scan(out=bpos[:, cc, sl], data0=ones_f[:, :], data1=eb[:, cc, sl], initial=0.0,
                                                    op0=ALU.mult, op1=ALU.add)
                    P.op("dve", fn, ["eb", "ones_f"], ["bpos"])
            k.act(eb[:, :, :], bpos[:, :, :], AF.Exp, ["bpos"], ["eb"], scale=-1.0 / 16)
            k.act(enb[:, :, :], bpos[:, :, :], AF.Exp, ["bpos"], ["enb"], scale=1.0 / 16)
            yield
            k.cp("pool", ebl[pb][:, :, :], eb[:, :, 127:SC:128], ["eb"], ["ebl" + sfx])
            yield
            for cc in range(2):
                b, bn = proj_fm(C_K + cc * 128, 128)
                yield
                k.tt("dve", ktT[pb][:, cc, :], b[:, 0:SC], enb[:, cc, :], ALU.mult, [bn, "enb"], ["ktT" + sfx])
                yield
            for cc in range(2):
                b, bn = proj_fm(C_Q + cc * 128, 128)
                yield
                k.stt(qtT[pb][:, cc, :], b[:, 0:SC], 0.125, eb[:, cc, :], ALU.mult, ALU.mult, [bn, "eb"], ["qtT" + sfx])
                yield


        def stage1x(s):
            pb = s % 2
            xs_ = xnT[pb]
            xsn = "xnT%d" % pb
            sfx = str(pb)
            def proj_fm(c0, m):
                b, bn = bank(pool=POOL_A)
                k.mmg([(b[0:m, 0:SC], w_in_bf[:, kc, c0:c0 + m], xs_[:, kc, :], kc == 0, kc == 7) for kc in range(8)],
                      wres(c0, c0 + m) + [xsn], [bn])
                return b, bn

            for cc in range(6):
                b, bn = proj_fm(C_X + cc * 128, 128)
                yield
                k.cp("act", xbc_raw[:, cc, 3:SC + 3], b[:, 0:SC], [bn], ["xbc_raw%d" % cc])
                yield

        def stage1b(s):
            pb = s % 2
            sfx = str(pb)
            for cc in range(6):
                xr_ = "xbc_raw%d" % cc
                k.ts("dve", cacc[:, :], xbc_raw[:, cc, 0:SC], cw(0, cc), cb[:, cc:cc + 1], ALU.mult, ALU.add, [xr_, "prm"], ["cacc"])
                for w in range(1, 4):
                    k.stt(cacc[:, :], xbc_raw[:, cc, w:w + SC], cw(w, cc), cacc[:, :], ALU.mult, ALU.add, [xr_, "prm", "cacc"], ["cacc"])
                yield
                if cc < 4:
                    k.act(xsT[pb][:, cc, :], cacc[:, :], AF.Silu, ["cacc"], ["xsT" + sfx])
                elif cc == 4:
                    k.act(BT[pb][:, :], cacc[:, :], AF.Silu, ["cacc"], ["BT" + sfx])
                else:
                    k.act(CT[pb][:, :], cacc[:, :], AF.Silu, ["cacc"], ["CT" + sfx])
                yield
            XR = ["xbc_raw%d" % cc for cc in range(6)]
            k.cp("pool", xbc_raw[:, :, 0:3], xbc_raw[:, :, SC:SC + 3], XR, XR)

        def stage1gz(s):
            pb = s % 2
            xs_ = xnT[pb]
            xsn = "xnT%d" % pb
            sfx = str(pb)
            def proj_fm(c0, m):
                b, bn = bank(pool=POOL_A)
                k.mmg([(b[0:m, 0:SC], w_in_bf[:, kc, c0:c0 + m], xs_[:, kc, :], kc == 0, kc == 7) for kc in range(8)],
                      wres(c0, c0 + m) + [xsn], [bn])
                return b, bn

            for cc in range(4):
                b, bn = proj_fm(C_G + cc * 128, 128)
                yield
                k.cp("act", gs[pb][:, cc, :], b[:, 0:SC], [bn], ["gs" + sfx])
                yield
            for cc in range(4):
                b, bn = proj_fm(C_Z + cc * 128, 128)
                yield
                k.cp("act", zs[pb][:, cc, :], b[:, 0:SC], [bn], ["zs" + sfx])
                yield


        def silu_gz(s):
            pb = s % 2
            sfx = str(pb)
            k.act(gs[pb][:, :, :], gs[pb][:, :, :], AF.Silu, ["gs" + sfx], ["gs" + sfx])
            k.act(zs[pb][:, :, :], zs[pb][:, :, :], AF.Silu, ["zs" + sfx], ["zs" + sfx])

        def stage2(s):
            pb = s % 2
            sfx = str(pb)
            xs_ = xnT[pb]
            xsn = "xnT" + sfx
            qn, kn, gn, zn, xsn_, Bn, Cn, dtn, lan = ("qtT" + sfx, "ktT" + sfx, "gs" + sfx, "zs" + sfx, "xsT" + sfx, "BT" + sfx,
                                                        "CT" + sfx, "dtt" + sfx, "latok" + sfx)
            q_, k_, g_, z_, x_, B_, C_, dt_, la_ = qtT[pb], ktT[pb], gs[pb], zs[pb], xsT[pb], BT[pb], CT[pb], dtt[pb], latok[pb]
            Mgs = [Mg, ATm]
            Mgn = ["Mg", "ATm"]
            t1 = tmp4
            y1_ = tg[:, :, :].rearrange("p a (b c) -> p (a b) c", c=128)
            diffs = [tmp4[:, :, :], y1_]
            diffn = ["tmp4", "tg"]
            def chunk_gen(ci):
                sl = slice(ci * 128, (ci + 1) * 128)
                vn = "vtok%d" % ci
                b, bn = bank(pool=POOL_B)
                k.mmg([(b[:, :], xs_[:, kc, sl], w_in_bf[:, kc, C_V:C_V + 512], kc == 0, kc == 7) for kc in range(8)],
                      wres(C_V, C_V + 512) + [xsn], [bn])
                k.cp("act", vtok[:, ci, :], b[:, :], [bn], [vn])
                ptk = pt[:, 0:256]
                k.trg([(ptk[:, cc * 128:(cc + 1) * 128], k_[:, cc, sl], ident_bf[:]) for cc in range(2)], [kn, "ident_bf"], ["pt"])
                k.cp("act", ktok[:, :], ptk, ["pt"], ["ktok"])
                bc_, bcn = bank(pool=POOL_B)
                k.mmg([(bc_[:, 0:8], tri_f[:, :], la_[:, ci, :], True, True)], ["tri_f", lan], [bcn])
                k.ts("dve", ccol[:, :], bc_[:, 0:8], -1.0, None, ALU.mult, None, [bcn], ["ccol"])
                yield
                for g in range(2):
                    pr = slice(64 * g, 64 * g + 64)
                    bcu, bcun = bank(pool=POOL_B)
                    bcuv = v4(bcu[:, :], 4)
                    k.mmg([(bcuv[:, hh, :], la_[:, ci, 4 * g + hh:4 * g + hh + 1].broadcast_to([128, 128]), tri_f[:, :], True, True)
                           for hh in range(4)], [lan, "tri_f"], [bcun])
                    for hh in range(4):
                        k.ts("dve", diffs[g][:, hh, :], bcuv[:, hh, :], ccol[:, 4 * g + hh:4 * g + hh + 1], 0.0, ALU.add, ALU.min,
                             [bcun, "ccol"], [diffn[g]])
                    k.act(decay[:, 4 * g:4 * g + 4, :], diffs[g], AF.Exp, [diffn[g]], ["decay%d" % g])
                    k.act(expcum[pr, :, :], bcuv[pr, :, :], AF.Exp, [bcun], ["rs%d" % g])
                yield
                bx, bxn = bank(pool=POOL_B)
                k.trg([(bx[:, cc * 128:(cc + 1) * 128], x_[:, cc, sl], ident_f[:]) for cc in range(4)], [xsn_, "ident_f"], [bxn])
                k.tt("dve", xdt[:, :, :], bx[:, :].rearrange("p (h q) -> p h q", h=8),
                     dt_[:, ci, :].unsqueeze(2).broadcast_to([128, 8, 64]), ALU.mult, [bxn, dtn], ["xdt"])
                ptb = pt[:, 256:384]
                k.trg([(ptb, B_[:, sl], ident_bf[:])], [Bn, "ident_bf"], ["pt"])
                k.cp("act", Btok[:, :], ptb, ["pt"], ["Btok"])
                for g in range(2):
                    pr = slice(64 * g, 64 * g + 64)
                    bsc, bscn = bank(pool=POOL_B)
                    k.mmg([(bsc[:, 0:128], B_[pr, sl], C_[pr, sl], True, True)], [Bn, Cn], [bscn])
                    k.tt("dve", scm[:, g, :], bsc[:, 0:128], mask_bf[:, :], ALU.mult, [bscn, "mask_bf"], ["scm%d" % g])
                yield
                ATv = ATm.rearrange("p (cc hh) i -> p hh cc i", hh=2)
                for hh in range(2):
                    pr = slice(64 * hh, 64 * hh + 64)
                    ba, ban = bank(pool=POOL_B)
                    bav = ba[:, 0:256].rearrange("p (c i) -> p c i", c=2)
                    k.mmg([(bav[:, cc, :], k_[pr, cc, sl], q_[pr, cc, sl], True, True) for cc in range(2)], [kn, qn], [ban])
                    k.tt("dve", ATv[:, hh, :, :], bav, mask_bf[:, :].unsqueeze(1).broadcast_to([128, 2, 128]), ALU.mult,
                         [ban, "mask_bf"], ["ATm"])
                yield
                for hh in range(2):
                    pr = slice(64 * hh, 64 * hh + 64)
                    bo, bon = bank(pool=POOL_B)
                    bov = bo[:, 0:256].rearrange("p (c i) -> p c i", c=2)
                    mms = []
                    for cc in range(2):
                        h = 2 * cc + hh
                        mms.append((bov[:, cc, :], vtok[:, ci, h * 128:(h + 1) * 128], ATm[:, h, :], True, False))
                        mms.append((bov[:, cc, :], S_bf[pr, cc, :], q_[pr, cc, sl], False, True))
                    k.mmg(mms, [vn, "ATm", "S_bf", qn], [bon])
                    k.act(sq[:, 2 * hh:2 * hh + 2, :], bov, AF.Square, [bon], ["sq"])
                    for cc in range(2):
                        k.tt("dve", t1[:, 2 * hh + cc, :], g_[:, 2 * cc + hh, sl], bov[:, cc, :], ALU.mult, [bon, gn], ["tmp4"])
                bk, bkn = bank(pool=POOL_B)
                bkv = bk[:, 0:256].rearrange("p (c e) -> p c e", c=2)
                mms = []
                for hh in range(2):
                    for cc in range(2):
                        h = 2 * cc + hh
                        mms.append((bkv[64 * hh:64 * hh + 64, cc, :], ktok[:, cc * 128 + hh * 64:cc * 128 + hh * 64 + 64],
                                    vtok[:, ci, h * 128:(h + 1) * 128], True, True))
                k.mmg(mms, ["ktok", vn], [bkn])
                k.tt("dve", S_f[:, :, :], S_f[:, :, :], bkv, ALU.add, ["S_f", bkn], ["S_f"])
                k.tt("dve", S_f[:, :, :], S_f[:, :, :], ebl[pb][:, :, ci:ci + 1].broadcast_to([128, 2, 128]), ALU.mult,
                     ["S_f", "ebl" + sfx], ["S_f"])
                k.cp("act", S_bf[:, :, :], S_f[:, :, :], ["S_f"], ["S_bf"])
                bsg, bsgn = bank(hold=True, pool=POOL_B)
                k.mmg([(bsg[:, :], ones_bf[:, :], sq[:, :, :].rearrange("p h i -> p (h i)"), True, True)], ["ones_bf", "sq"], [bsgn])
                k.act(bsg[:, :], bsg[:, :], AF.Ln, [bsgn], [bsgn], scale=1.0 / 128, bias=EPS)
                k.act(bsg[:, :], bsg[:, :], AF.Exp, [bsgn], [bsgn], scale=-0.5)
                yield
                k.tt("dve", CsT[:, :, :], expcum[:, :, :], C_[:, sl].unsqueeze(1).broadcast_to([128, 4, 128]), ALU.mult,
                     ["rs0", "rs1", Cn], ["CsT"])
                for g in range(2):
                    k.tt("dve", Mgs[g], decay[:, 4 * g:4 * g + 4, :], scm[:, g, :].unsqueeze(1).broadcast_to([128, 4, 128]), ALU.mult,
                         ["decay%d" % g, "scm%d" % g], [Mgn[g]])
                k.tt("dve", xw[:, :, :], xdt[:, :, :], decay[:, :, 127:128].broadcast_to([128, 8, 64]), ALU.mult,
                     ["xdt", "decay0", "decay1"], ["xw"])
                k.tt("dve", ogT[:, :, sl].rearrange("p (cc hh) i -> p hh cc i", hh=2),
                     t1[:, :, :].rearrange("p (hh cc) i -> p hh cc i", hh=2),
                     v4(bsg[:, :], 4).rearrange("p (hh cc) i -> p hh cc i", hh=2), ALU.mult, ["tmp4", bsgn], ["ogT"])
                release(bsgn)
                yield
                bh, bhn = bank(hold=True, pool=POOL_B)
                bhv = bh[:, 0:256].rearrange("p (h q) -> p h q", h=4)
                for g in range(2):
                    pr = slice(64 * g, 64 * g + 64)
                    by, byn = bank(pool=POOL_B)
                    byv = by[:, 0:256].rearrange("p (c i) -> p c i", c=2)
                    mms = []
                    for hh in range(4):
                        h = 4 * g + hh
                        po = byv[64 * (h % 2):64 * (h % 2) + 64, hh // 2, :]
                        mms.append((po, xdt[:, h, :], Mgs[g][:, hh, :], True, False))
                        mms.append((po, hT_bf[pr, hh, :], CsT[pr, hh, :], False, True))
                    k.mmg(mms, ["xdt", Mgn[g], "hT_bf", "CsT"], [byn])
                    for c2 in range(2):
                        cc = 2 * g + c2
                        k.stt(y1_[:, cc, :], x_[:, cc, sl], dskc[:, cc:cc + 1], byv[:, c2, :], ALU.mult, ALU.add, [xsn_, "dskc", byn], ["tg"])
                    k.mmg([(bh[pr, 0:256], Btok[:, 64 * g:64 * g + 64], xw[:, 4 * g:4 * g + 4, :].rearrange("p h q -> p (h q)"), True, True)],
                          ["Btok", "xw"], [bhn])
                k.tt("dve", hT_f[:, :, :], hT_f[:, :, :], expcum[:, :, 127:128].broadcast_to([128, 4, 64]), ALU.mult,
                     ["hT_f", "rs0", "rs1"], ["hT_f"])
                k.tt("dve", hT_f[:, :, :], hT_f[:, :, :], bhv, ALU.add, ["hT_f", bhn], ["hT_f"])
                k.cp("act", hT_bf[:, :, :], hT_f[:, :, :], ["hT_f"], ["hT_bf"])
                release(bhn)
                yield
                k.tt("dve", y1_, y1_, z_[:, :, sl], ALU.mult, ["tg", zn], ["tg"])
                k.act(sq[:, :, :], y1_, AF.Square, ["tg"], ["sq"])
                bs, bsn = bank(hold=True, pool=POOL_B)
                bsv = bs[:, 0:256].rearrange("p (g i) -> p g i", g=2)
                mms = []
                for g in range(2):
                    mms.append((bsv[:, g, :], ones_bf[:, :], sq[:, 2 * g, :], True, False))
                    mms.append((bsv[:, g, :], ones_bf[:, :], sq[:, 2 * g + 1, :], False, True))
                k.mmg(mms, ["ones_bf", "sq"], [bsn])
                k.act(bsv, bsv, AF.Ln, [bsn], [bsn], scale=1.0 / 256, bias=EPS)
                k.act(bsv, bsv, AF.Exp, [bsn], [bsn], scale=-0.5)
                k.tt("dve", ygT[:, :, sl].rearrange("p (g t) i -> p g t i", g=2), y1_.rearrange("p (g t) i -> p g t i", g=2),
                     bsv.unsqueeze(2).broadcast_to([128, 2, 2, 128]), ALU.mult, ["tg", bsn], ["ygT"])
                release(bsn)
                yield

            for ci in range(2):
                yield from chunk_gen(ci)

        def stage3_m(s):
            pb = s % 2
            xs_ = xnT[pb]
            xsn = "xnT%d" % pb
            for ti in range(2):
                tile_idx = 2 * s + ti
                k.dma("sp", "xr%d" % ti, [(xr[ti][:], xp[tile_idx * 128:(tile_idx + 1) * 128, :])], [], ["xr%d" % ti])
            for m in range(8):
                ms = slice(m * 128, (m + 1) * 128)
                bg, bgn = bank(pool=POOL_B)
                bgv = v4(bg[:, :], 2)
                mms = [(bgv[:, 0, :], w_in_bf[:, kc, C_GA + m * 128:C_GA + (m + 1) * 128], xs_[:, kc, :], kc == 0, kc == 7) for kc in range(8)]
                mms += [(bgv[:, 1, :], w_in_bf[:, kc, C_GB + m * 128:C_GB + (m + 1) * 128], xs_[:, kc, :], kc == 0, kc == 7) for kc in range(8)]
                k.mmg(mms, wres(C_GA + m * 128, C_GA + (m + 1) * 128) + wres(C_GB + m * 128, C_GB + (m + 1) * 128) + [xsn], [bgn])
                bab, babn = bank(pool=POOL_B)
                babv = v4(bab[:, :], 2)
                mms = [(babv[:, 0, :], wbrg[:, cc, ms], ogT[:, cc, :], cc == 0, cc == 3) for cc in range(4)]
                mms += [(babv[:, 1, :], wbrs[:, cc, ms], ygT[:, cc, :], cc == 0, cc == 3) for cc in range(4)]
                k.mmg(mms, ["wbrg", "wbrs", "ogT", "ygT"], [babn])
                k.act(tg[:, :, :], bgv, AF.Tanh, [bgn], ["tg"], scale=0.5)
                k.stt(tg[:, :, :], tg[:, :, :], 1.0, babv, ALU.add, ALU.mult, ["tg", babn], ["tg"])
                k.tt("pool", mrgT[:, m, :], tg[:, 0, :], tg[:, 1, :], ALU.add, ["tg"], ["mrgT"])
                yield
        def stage3_tail(s):
            for ti in range(2):
                tile_idx = 2 * s + ti
                slot = tile_idx % 2
                xrn = "xr%d" % slot
                for half in range(2):
                    b, bn = bank(pool=POOL_B)
                    hs = slice(half * 512, (half + 1) * 512)
                    k.mmg([(b[:, :], mrgT[:, kc, ti * 128:(ti + 1) * 128], wout[:, kc, hs], kc == 0, kc == 7) for kc in range(8)],
                          ["mrgT", "wout0", "wout1"], [bn])
                    k.stt(xr[slot][:, hs], b[:, :], 0.5, xr[slot][:, hs], ALU.mult, ALU.add, [bn, xrn], [xrn])
                yield
            for ti in range(2):
                tile_idx = 2 * s + ti
                slot = tile_idx % 2
                xrn = "xr%d" % slot
                rows = slice(tile_idx * 128, (tile_idx + 1) * 128)
                k.act(mrgT[:, :, ti * 128:(ti + 1) * 128], xr[slot][:].rearrange("p (a b) -> p a b", a=8), AF.Square, [xrn], ["mrgT", "ss1"],
                      accum_out=ss[:, 1:2])
                rstd_from_ss(1, D, 1)
                yield
                k.stt(xr[slot][:], xr[slot][:], sm[:, 1:2], fng_rep[:], ALU.mult, ALU.mult, [xrn, "sm1", "fng_rep"], [xrn])
                k.dma("pool", xrn + "_st", [(yp[rows, :], xr[slot][:])], [xrn], [])
                yield

        def late_weights():
            for (c0, c1) in col_blocks[N_EARLY:]:
                wdma("w_in_%d" % c0, w_in_bf[:, :, c0:c1], w_in_v[:, :, c0:c1])
            wdma("wbrg", wbrg[:, :, :], w_brg_d.rearrange("(kc p) c -> p kc c", p=128))
            wdma("wbrs", wbrs[:, :, :], w_brs_d.rearrange("(kc p) c -> p kc c", p=128))
            wov = w_out_d.rearrange("(kc p) c -> p kc c", p=128)
            wdma("wout0", wout[:, 0:4, :], wov[:, 0:4, :])
            wdma("wout1", wout[:, 4:8, :], wov[:, 4:8, :])

        def run(gen):
            for _ in gen:
                pass

        def chain(*gens):
            for gg in gens:
                yield from gg

        def interleave(gp, gf, rp=2, rf=1):
            dp = df = False
            while not (dp and df):
                for _ in range(rp):
                    if not dp:
                        try:
                            next(gp)
                        except StopIteration:
                            dp = True
                for _ in range(rf):
                    if not df:
                        try:
                            next(gf)
                        except StopIteration:
                            df = True

        nsc = NSC if stop is None else stop[0]
        if nsc > 0 and (stop is None or stop[1] >= 1):
            g1a = stage1a(0)
            for _ in range(6):
                next(g1a)
            interleave(g1a, chain(stage1x(0), stage1b(0), stage1gz(0)), 2, 1)
            silu_gz(0)
            late_weights()
            def fold_gains():
                for cc in range(4):
                    k.ts("pool", wbrg[:, cc, :], wbrg[:, cc, :], prm[:, 16 + cc:17 + cc], None, ALU.mult, None, ["wbrg", "prm"], ["wbrg"])
                    k.ts("pool", wbrs[:, cc, :], wbrs[:, cc, :], prm[:, 20 + cc:21 + cc], None, ALU.mult, None, ["wbrs", "prm"], ["wbrs"])

            tail_prev = iter(())
            for s in range(nsc):
                nxt = s + 1 < nsc
                if stop is None or stop[1] >= 2:
                    interleave(stage2(s), chain(tail_prev, chain(stage1a(s + 1), stage1gz(s + 1), stage1x(s + 1)) if nxt else iter(())), 2, 7)
                tail_prev = iter(())
                if stop is None or stop[1] >= 3:
                    if s == 0:
                        fold_gains()
                    if nxt:
                        silu_gz(s + 1)
                    interleave(stage3_m(s), stage1b(s + 1) if nxt else iter(()), 1, 2)
                    tail_prev = stage3_tail(s)
            run(tail_prev)

        if lvl >= 3:
            k.dma("sp", "S_f", [(glap.rearrange("(cc p) e -> p cc e", p=128), S_f[:, :, :])], ["S_f"], [])
            bt_, btn = bank()
            btv = bt_[0:64, :].rearrange("p (hh g n) -> p hh g n", hh=4, g=2)
            k.trg([(bt_[0:64, hh * 128:(hh + 1) * 128], hT_f[:, hh, :], ident_f[:, :]) for hh in range(4)], ["hT_f", "ident_f"], [btn])
            hout = tmp4[0:64, :, :].rearrange("p a b -> p (a b)").rearrange("p (h n) -> p h n", h=8)
            k.cp("dve", hout.rearrange("p (g hh) n -> p hh g n", g=2), btv, [btn], ["tmp4"])
            k.dma("sp", "tmp4", [(ssmp.rearrange("(h p) n -> p h n", p=64), hout)], ["tmp4"], [])
        if lvl >= 4:
            bcv, bcvn = bank()
            k.mmg([(bcv[0:4, cc * 128:(cc + 1) * 128], xbc_raw[:, cc, 0:4], ident_f[:], True, True) for cc in range(4)], ["xbc_raw%d" % _c for _c in range(6)] + ["ident_f"], [bcvn])
            bcw, bcwn = bank()
            k.mmg([(bcw[0:4, (cc - 4) * 128:(cc - 3) * 128], xbc_raw[:, cc, 0:4], ident_f[:], True, True) for cc in (4, 5)], ["xbc_raw%d" % _c for _c in range(6)] + ["ident_f"], [bcwn])
            cst = diff[0:3, :, :].rearrange("p a b -> p (a b)")
            cst2 = expcum[0:3, 0:2, :].rearrange("p a b -> p (a b)")
            k.cp("dve", cst, bcv[0:3, :], [bcvn], ["tmp4"])
            k.cp("dve", cst2, bcw[0:3, 0:256], [bcwn], ["rs"])
            k.dma("sp", "tmp4", [(convp[:, 0:512], cst)], ["tmp4"], [])
            k.dma("sp", "rs", [(convp[:, 512:768], cst2)], ["rs"], [])

        if with_sample:
            sample_phase(nc, P, k, locals())

        P.finish()
        P.emit(st)
    return nc


def sample_phase(nc, P, k, L):
    g = lambda n: L[n]
    bank, release = g("bank"), g("release")
    xt, xr, xn, ss, sm, pt = g("xt"), g("xr"), g("xn"), g("ss"), g("sm"), g("pt")
    junk = g("scrB")
    ident_f, ident_bf, ones_f, negh = g("ident_f"), g("ident_bf"), g("ones_f"), g("negh")
    w_in_bf, wbrg, wbrs, wout, wa2 = g("w_in_bf"), g("wbrg"), g("wbrs"), g("wout"), g("wa2")
    prm, dtb_rep, A_rep, dsk_rep, fng_rep = g("prm"), g("dtb_rep"), g("A_rep"), g("dsk_rep"), g("fng_rep")
    tg, rs, tmp4, alr, cacc, bpos, eb, enb, xbc_raw, Btok = (g(n) for n in
        ("tg", "rs", "tmp4", "alr", "cacc", "bpos", "eb", "enb", "xbc_raw", "Btok"))
    xsT = g("xsT")[0]
    xs_d, sgla, sssm, sconv, b_a2 = g("xs_d"), g("sgla"), g("sssm"), g("sconv"), g("b_a2")
    ys_o, glas, ssms, convs = g("ys_o"), g("glas"), g("ssms"), g("convs")
    wres = g("wres")
    ng = prm[:, 0:8]
    cb = prm[:, 10:16]
    cw = g("cw")

    def dscr(n, shape):
        return nc.dram_tensor(n, shape, F32, kind="Internal").ap()
    d_q, d_k, d_a = dscr("d_q", [NS, 256]), dscr("d_k", [NS, 256]), dscr("d_a", [NS, 256])
    d_v, d_g, d_z = dscr("d_v", [NS, 512]), dscr("d_g", [NS, 512]), dscr("d_z", [NS, 512])
    d_xs, d_xd = dscr("d_xs", [NS, 512]), dscr("d_xd", [NS, 512])
    d_B, d_C = dscr("d_B", [NS, 128]), dscr("d_C", [NS, 128])
    d_dt, d_dA = dscr("d_dt", [NS, 8]), dscr("d_dA", [NS, 8])

    def dap(t, off, pat):
        return bass.AP(t.tensor, off, pat)

    fx = g("xsT")[1][:, :, :].rearrange("p a b -> p (a b)")
    q1, k1, a1 = fx[:, 0:32], fx[:, 32:64], fx[:, 64:96]
    v1 = fx[:, 96:224]
    g3 = fx[0:64, 224:352]
    x2, xd2, z2, B2, C2 = fx[:, 352:416], fx[:, 416:480], fx[:, 480:544], fx[:, 544:608], fx[:, 608:672]
    dtA2 = fx[:, 672:674]
    xdt2, y2s = fx[:, 674:738], fx[:, 738:802]
    Pm = fx[:, 802:866]
    Gm = fx[:, 866:994]
    Em = g("S_f")[0:32, :, :].rearrange("p a b -> p (a b)")[:, 0:128]
    bq = g("qtT")[0][:, :, :].rearrange("p a b -> p (a b)")
    xnTs = bq[:, 0:128].rearrange("p (k s) -> p k s", k=8)
    ynd = bq[:, 128:256]
    ynT2 = bq[:, 256:384]
    ogTs = bq[:, 384:448].rearrange("p (h s) -> p h s", h=4)
    mTs = g("ktT")[0][:, 0, 0:128].rearrange("p (k s) -> p k s", k=8)

    P.barrier()

    xnTl = g("xnT")
    Sslot = [xt[0][:, :], xt[1][:, :],
             xnTl[0][:, :, :].rearrange("p a b -> p (a b)").bitcast(F32), xnTl[1][:, :, :].rearrange("p a b -> p (a b)").bitcast(F32)]
    Sname = ["xt0", "xt1", "xnT0", "xnT1"]
    for dq in range(4):
        k.dma("sp", Sname[dq], [(Sslot[dq][64 * dhi:64 * dhi + 64, :], sgla[:, dhi, dq * 1024:(dq + 1) * 1024]) for dhi in range(2)],
              [], [Sname[dq]])

    x_s = xbc_raw[0:NS, :, :].rearrange("p a b -> p (a b)")[:, 0:D]
    stg_tiles = [tg[0:NS, :, :].rearrange("p a b -> p (a b)"), rs[0:NS, 0:4, :].rearrange("p a b -> p (a b)")]
    stg_names = ["tg", "rs"]
    ustage = cacc
    u_s = xsT[0:NS, :, :].rearrange("p a b -> p (a b)")[:, 0:768]
    xn_s = xn[0:NS, :]

    k.dma("sp", "xbc_raw", [(x_s, xs_d)], [], ["xbc_raw"])
    k.act(junk[0:NS, :], x_s, AF.Square, ["xbc_raw"], ["junk", "ss"], accum_out=ss[0:NS, 2:3])
    k.ts("pool", sm[0:NS, 2:3], ss[0:NS, 2:3], 1.0 / D, EPS, ALU.mult, ALU.add, ["ss"], ["sm"])
    k.tt("pool", sm[0:NS, 2:3], sm[0:NS, 2:3], negh[0:NS, 0:1], ALU.pow, ["sm", "negh"], ["sm"])
    k.ts("pool", xn_s, x_s, sm[0:NS, 2:3], None, ALU.mult, None, ["xbc_raw", "sm"], ["xn"])
    ptv = pt[:, 0:8 * NS].rearrange("p (kc t) -> p kc t", kc=8)
    k.trg([(ptv[:, kc, :], xn[0:NS, kc * 128:(kc + 1) * 128], ident_bf[0:NS, 0:NS]) for kc in range(8)], ["xn", "ident_bf"], ["pt"])
    k.tt("dve", xnTs[:, :, :], ptv, ng.unsqueeze(2).broadcast_to([128, 8, NS]), ALU.mult, ["pt", "prm"], ["xnTs"])

    si = [0]

    def proj_tm(c0, w):
        b, bn = bank()
        k.mmg([(b[0:NS, 0:w], xnTs[:, kc, :], w_in_bf[:, kc, c0:c0 + w], kc == 0, kc == 7) for kc in range(8)],
              wres(c0, c0 + w) + ["xnTs"], [bn])
        return b, bn

    def proj_to_dram(c0, w, dsts):
        b, bn = proj_tm(c0, w)
        i = si[0] % 2
        si[0] += 1
        k.cp("act", stg_tiles[i][:, 0:w], b[0:NS, 0:w], [bn], [stg_names[i]])
        off = 0
        for (d, dw) in dsts:
            k.dma("sp", d.tensor.name, [(d, stg_tiles[i][:, off:off + dw])], [stg_names[i]], [d.tensor.name])
            off += dw

    proj_to_dram(C_Q, 512, [(d_q, 256), (d_k, 256)])
    proj_to_dram(C_V, 512, [(d_v, 512)])
    proj_to_dram(C_G, 512, [(d_g, 512)])
    proj_to_dram(C_Z, 512, [(d_z, 512)])
    for (c0, w, o) in ((C_X, 512, 0), (C_X + 512, 256, 512)):
        b, bn = proj_tm(c0, w)
        k.cp("act", u_s[:, o:o + w], b[0:NS, 0:w], [bn], ["xsT"])
    k.dma("sp", "xsT", [(convs[:, 2, :], u_s)], ["xsT"], [])
    k.dma("sp", "convs01", [(convs[:, 0:2, :], sconv[:, 1:3, :])], [], [])
    b, bn = proj_tm(C_A, 16)
    k.cp("dve", sm[0:NS, 8:24], b[0:NS, 0:16], [bn], ["sm_a"])
    b, bn = proj_tm(C_DT, 8)
    k.tt("dve", sm[0:NS, 24:32], b[0:NS, 0:8], dtb_rep[0:NS, :], ALU.add, [bn, "dtb_rep"], ["sm_dt"])
    k.act(sm[0:NS, 24:32], sm[0:NS, 24:32], AF.Exp, ["sm_dt"], ["sm_dt"])
    k.act(sm[0:NS, 24:32], sm[0:NS, 24:32], AF.Ln, ["sm_dt"], ["sm_dt"], bias=1.0)
    k.tt("dve", sm[0:NS, 32:40], sm[0:NS, 24:32], A_rep[0:NS, :], ALU.mult, ["sm_dt", "A_rep"], ["sm_dA"])
    k.act(sm[0:NS, 32:40], sm[0:NS, 32:40], AF.Exp, ["sm_dA"], ["sm_dA"])
    k.dma("sp", "d_dt", [(d_dt, sm[0:NS, 24:32])], ["sm_dt"], ["d_dt"])
    k.dma("sp", "d_dA", [(d_dA, sm[0:NS, 32:40])], ["sm_dA"], ["d_dA"])

    b, bn = bank()
    k.trg([(b[0:16, 0:NS], sm[0:NS, 8:24], ident_f[0:NS, 0:NS])], ["sm_a", "ident_f"], [bn])
    k.cp("act", alr[0:16, 0:NS], b[0:16, 0:NS], [bn], ["alr"])
    b, bn = bank()
    k.mmg([(b[0:NS, 0:256], alr[0:16, 0:NS], wa2[0:16, :], True, True)], ["alr", "wa2"], [bn])
    ba2r = cacc[0:NS, 0:256]
    k.dma("sp", "cacc", [(ba2r, bc_rows(b_a2, NS, 256))], [], ["cacc"])
    a_s = bpos[0:NS, 0, :]
    k.tt("dve", a_s, b[0:NS, 0:256], ba2r, ALU.add, [bn, "cacc"], ["bpos"])
    k.act(a_s, a_s, AF.Exp, ["bpos"], ["bpos"], scale=-1.0)
    k.act(a_s, a_s, AF.Ln, ["bpos"], ["bpos"], bias=1.0)
    k.act(a_s, a_s, AF.Exp, ["bpos"], ["bpos"], scale=-1.0 / 16)
    k.dma("sp", "d_a", [(d_a, a_s)], ["bpos"], ["d_a"])

    bufrows = xr[0][0:48, 0:768]
    k.dma("sp", "xr0", [(bufrows, sconv.rearrange("s r c -> (s r) c"))], [], ["xr0"])
    b, bn = bank()
    k.trg([(b[:, cc * 48:(cc + 1) * 48], xr[0][0:48, cc * 128:(cc + 1) * 128], ident_f[0:48, 0:48]) for cc in range(6)],
          ["xr0", "ident_f"], [bn])
    bufT = rs[:, 0:3, :].rearrange("p a b -> p (a b)")[:, 0:288]
    k.cp("dve", bufT, b[:, 0:288], [bn], ["rs"])
    bufT4 = bufT.rearrange("p (c s r) -> p c s r", c=6, s=NS)
    bu, bun = bank()
    buv = bu[:, 0:6 * NS].rearrange("p (c s) -> p c s", c=6)
    for cc in range(6):
        k.mmg([(buv[:, cc, :], w_in_bf[:, kc, C_X + cc * 128:C_X + (cc + 1) * 128], xnTs[:, kc, :], kc == 0, kc == 7) for kc in range(8)],
              wres(C_X + cc * 128, C_X + (cc + 1) * 128) + ["xnTs"], [bun])
    accs = eb[:, 0, 0:6 * NS].rearrange("p (c s) -> p c s", c=6)
    xcT = enb[:, 0, 0:6 * NS].rearrange("p (c s) -> p c s", c=6)
    for cc in range(6):
        k.ts("dve", accs[:, cc, :], bufT4[:, cc, :, 0], cw(0, cc), cb[:, cc:cc + 1], ALU.mult, ALU.add, ["rs", "prm"], ["eb"])
        for w in (1, 2):
            k.stt(accs[:, cc, :], bufT4[:, cc, :, w], cw(w, cc), accs[:, cc, :], ALU.mult, ALU.add, ["rs", "prm", "eb"], ["eb"])
        k.stt(accs[:, cc, :], buv[:, cc, :], cw(3, cc), accs[:, cc, :], ALU.mult, ALU.add, [bun, "prm", "eb"], ["eb"])
    k.act(xcT, accs, AF.Silu, ["eb"], ["enb"])
    bA, bAn = bank()
    k.trg([(bA[0:NS, cc * 128:(cc + 1) * 128], xcT[:, cc, :], ident_f[:, :]) for cc in range(4)], ["enb", "ident_f"], [bAn])
    bB, bBn = bank()
    k.trg([(bB[0:NS, (cc - 4) * 128:(cc - 3) * 128], xcT[:, cc, :], ident_f[:, :]) for cc in (4, 5)], ["enb", "ident_f"], [bBn])
    xcs = xr[1][0:NS, 0:768]
    k.cp("act", xcs[:, 0:512], bA[0:NS, 0:512], [bAn], ["xr1"])
    k.cp("act", xcs[:, 512:768], bB[0:NS, 0:256], [bBn], ["xr1"])
    xd_s = xr[1][0:NS, 768:1024]
    xdfull = tmp4[0:NS, :, :].rearrange("p a b -> p (a b)")
    k.tt("dve", xdfull.rearrange("p (h q) -> p h q", h=8), xcs[:, 0:512].rearrange("p (h q) -> p h q", h=8),
         dsk_rep[0:NS, :].unsqueeze(2).broadcast_to([NS, 8, 64]), ALU.mult, ["xr1", "dsk_rep"], ["tmp4"])
    k.dma("sp", "d_xs", [(d_xs, xcs[:, 0:512])], ["xr1"], ["d_xs"])
    k.dma("sp", "d_B", [(d_B, xcs[:, 512:640])], ["xr1"], ["d_B"])
    k.dma("sp", "d_C", [(d_C, xcs[:, 640:768])], ["xr1"], ["d_C"])
    k.dma("sp", "d_xd", [(d_xd, xdfull)], ["tmp4"], ["d_xd"])

    for dhi in range(2):
        pr = slice(64 * dhi, 64 * dhi + 64)
        k.dma("sp", "q1", [(q1[pr, :], dap(d_q, dhi * 32, [[64, 64], [1, 32]])),
                           (k1[pr, :], dap(d_k, dhi * 32, [[64, 64], [1, 32]])),
                           (a1[pr, :], dap(d_a, dhi * 32, [[64, 64], [1, 32]])),
                           (v1[pr, :], dap(d_v, 0, [[128, 64], [1, 128]]))],
              ["d_q", "d_k", "d_a", "d_v"], ["q1"])
    k.dma("sp", "g3", [(g3[:, :], dap(d_g, 0, [[128, 64], [1, 128]]))], ["d_g"], ["g3"])
    k.memset("dve", Pm[:, :], 0.0, ["Pm"])
    k.cp("dve", Pm[0:64, :], ident_f[0:64, 0:64], ["ident_f", "Pm"], ["Pm"])
    k.cp("dve", Pm[64:128, :], ident_f[64:128, 64:128], ["ident_f", "Pm"], ["Pm"])
    pacc = rs
    for dq in range(4):
        sl_ = dq % 2
        Sn, Tn = Sname[dq], "xr%d" % sl_
        Sf = Sslot[dq]
        S3 = Sf.rearrange("p (d e) -> p d e", d=8)
        T3 = xr[sl_][:, :].rearrange("p (d e) -> p d e", d=8)
        dsl = slice(dq * 8, dq * 8 + 8)
        k.tt("dve", T3, k1[:, dsl].unsqueeze(2).broadcast_to([128, 8, 128]), v1[:, :].unsqueeze(1).broadcast_to([128, 8, 128]),
             ALU.mult, ["q1"], [Tn])
        k.tt("dve", S3, S3, a1[:, dsl].unsqueeze(2).broadcast_to([128, 8, 128]), ALU.mult, [Sn, "q1"], [Sn])
        k.tt("dve", Sf, Sf, xr[sl_][:, :], ALU.add, [Sn, Tn], [Sn])
        k.dma("act", Sn + "_st", [(glas[:, dhi, dq * 1024:(dq + 1) * 1024], Sf[64 * dhi:64 * dhi + 64, :]) for dhi in range(2)], [Sn], [])
        k.tt("dve", T3, S3, q1[:, dsl].unsqueeze(2).broadcast_to([128, 8, 128]), ALU.mult, [Sn, "q1"], [Tn])

        def red(e, T3=T3, dq=dq):
            return e.tensor_reduce(out=pacc[:, dq, :], in_=T3.rearrange("p d e -> p e d"), axis=AX.X, op=ALU.add)
        P.op("dve", red, [Tn], ["rs"])
    po1 = tmp4[:, 0, :]

    def red2(e):
        return e.tensor_reduce(out=po1, in_=pacc[:, :, :].rearrange("p q e -> p e q"), axis=AX.X, op=ALU.add)
    P.op("dve", red2, ["rs"], ["tmp4"])
    b, bn = bank()
    k.mmg([(b[0:64, 0:128], Pm[:, :], po1, True, True)], ["Pm", "tmp4"], [bn])
    o3s = tmp4[0:64, 1, :]
    k.act(o3s, b[0:64, 0:128], AF.Identity, [bn], ["tmp4b"], scale=0.125)
    k.act(junk[0:64, 0:128], o3s, AF.Square, ["tmp4b"], ["junk", "ss"], accum_out=ss[0:64, 3:4])
    k.ts("pool", sm[0:64, 3:4], ss[0:64, 3:4], 1.0 / 128, EPS, ALU.mult, ALU.add, ["ss"], ["sm"])
    k.tt("pool", sm[0:64, 3:4], sm[0:64, 3:4], negh[0:64, 0:1], ALU.pow, ["sm", "negh"], ["sm"])
    k.act(g3[:, :], g3[:, :], AF.Silu, ["g3"], ["g3"])
    k.stt(Btok[0:64, :], o3s, sm[0:64, 3:4], g3[:, :], ALU.mult, ALU.mult, ["tmp4b", "sm", "g3"], ["Btok"])
    k.trg([(pt[:, 0:64], Btok[0:64, :], ident_bf[0:64, 0:64])], ["Btok", "ident_bf"], ["pt"])
    k.cp("dve", ogTs[:, :, :], pt[:, 0:64].rearrange("p (s h) -> p h s", h=4), ["pt"], ["ogTs"])

    for hh in range(4):
        pr = slice(32 * hh, 32 * hh + 32)
        k.dma("sp", "x2", [(x2[pr, :], dap(d_xs, hh * 64, [[256, 32], [1, 64]])),
                           (xd2[pr, :], dap(d_xd, hh * 64, [[256, 32], [1, 64]])),
                           (z2[pr, :], dap(d_z, hh * 64, [[256, 32], [1, 64]])),
                           (B2[pr, :], dap(d_B, 0, [[64, 32], [1, 64]])),
                           (C2[pr, :], dap(d_C, 0, [[64, 32], [1, 64]])),
                           (dtA2[pr, 0:1], dap(d_dt, hh, [[4, 32], [1, 1]])),
                           (dtA2[pr, 1:2], dap(d_dA, hh, [[4, 32], [1, 1]]))],
              ["d_xs", "d_xd", "d_z", "d_B", "d_C", "d_dt", "d_dA"], ["x2"], slow=True)
    k.ts("dve", xdt2[:, :], x2[:, :], dtA2[:, 0:1], None, ALU.mult, None, ["x2"], ["xdt2"])
    for pq in range(4):
        k.dma("sp", Sname[pq], [(Sslot[pq][32 * hh:32 * hh + 32, :], sssm[:, hh, pq * 1024:(pq + 1) * 1024]) for hh in range(4)],
              [], [Sname[pq]])
    for pq in range(4):
        sl_ = pq % 2
        Hn, Tn = Sname[pq], "xr%d" % sl_
        Hf = Sslot[pq]
        H3 = Hf.rearrange("p (q n) -> p q n", q=16)
        T3 = xr[sl_][:, :].rearrange("p (q n) -> p q n", q=16)
        psl = slice(pq * 16, pq * 16 + 16)
        k.tt("dve", T3, xdt2[:, psl].unsqueeze(2).broadcast_to([128, 16, 64]), B2[:, :].unsqueeze(1).broadcast_to([128, 16, 64]),
             ALU.mult, ["xdt2", "x2"], [Tn])
        k.stt(Hf, Hf, dtA2[:, 1:2], xr[sl_][:, :], ALU.mult, ALU.add, [Hn, Tn, "x2"], [Hn])
        k.dma("act", Hn + "_st", [(ssms[:, hh, pq * 1024:(pq + 1) * 1024], Hf[32 * hh:32 * hh + 32, :]) for hh in range(4)], [Hn], [])
        k.tt("dve", T3, H3, C2[:, :].unsqueeze(1).broadcast_to([128, 16, 64]), ALU.mult, [Hn, "x2"], [Tn])

        def redy(e, T3=T3, psl=psl):
            return e.tensor_reduce(out=y2s[:, psl], in_=T3, axis=AX.X, op=ALU.add)
        P.op("dve", redy, [Tn], ["y2s"])
    k.tt("dve", y2s[:, :], y2s[:, :], xd2[:, :], ALU.add, ["y2s", "x2"], ["y2s"])
    k.act(z2[:, :], z2[:, :], AF.Silu, ["x2"], ["z2"])
    k.tt("dve", y2s[:, :], y2s[:, :], z2[:, :], ALU.mult, ["y2s", "z2"], ["y2s"])
    k.memset("dve", ss[:, 4:6], 0.0, ["ss"])
    k.act(junk[:, 0:64], y2s[:, :], AF.Square, ["y2s"], ["junk", "ss"], accum_out=ss[:, 4:5])
    for j in range(4):
        k.cp("dve", Em[:, j * 32:(j + 1) * 32], ident_f[0:32, 0:32], ["ident_f"], ["Em"])
    b, bn = bank()
    k.mmg([(b[:, 0:128], Em[:, :], Em[:, :], True, True)], ["Em"], [bn])
    k.cp("dve", Gm[:, :], b[:, 0:128], [bn], ["Gm"])
    b, bn = bank()
    k.mmg([(b[:, 0:2], Gm[:, :], ss[:, 4:6], True, True)], ["Gm", "ss"], [bn])
    k.cp("dve", ss[:, 6:8], b[:, 0:2], [bn], ["ss2"])
    k.ts("pool", sm[:, 4:5], ss[:, 6:7], 1.0 / 256, EPS, ALU.mult, ALU.add, ["ss2"], ["sm"])
    k.tt("pool", sm[:, 4:5], sm[:, 4:5], negh[:, 0:1], ALU.pow, ["sm", "negh"], ["sm"])
    for j in range(2):
        k.ts("dve", ynd[:, j * 64:(j + 1) * 64], y2s[:, :], sm[:, 4:5], None, ALU.mult, None, ["y2s", "sm"], ["ynd"])
    k.trg([(pt[:, 128:256], ynd[:, :], ident_bf[:, :])], ["ynd", "ident_bf"], ["pt"])
    k.cp("dve", ynT2[:, :], pt[:, 128:256], ["pt"], ["ynT2"])

    mrg_s = xn[0:NS, :]
    for half in range(2):
        hs = slice(half * 512, (half + 1) * 512)
        bA, bAn = bank()
        k.mmg([(bA[0:NS, :], ogTs[:, h, :], wbrg[:, h, hs], h == 0, h == 3) for h in range(4)], ["ogTs", "wbrg"], [bAn])
        bBs = []
        for par in range(2):
            pr = slice(64 * par, 64 * par + 64)
            bB, bBn = bank()
            heads = [(g_, hh) for g_ in range(2) for hh in range(4) if hh % 2 == par]
            mms = []
            for i, (g_, hh) in enumerate(heads):
                c0 = hh * 32 + g_
                mms.append((bB[0:NS, :], ynT2[pr, c0:c0 + 31:2], wbrs[pr, 2 * g_ + hh // 2, hs], i == 0, i == len(heads) - 1))
            k.mmg(mms, ["ynT2", "wbrs"], [bBn])
            bBs.append((bB, bBn))
        gts = []
        for (cg, nm) in ((C_GA, "tg"), (C_GB, "rs")):
            bG, bGn = bank()
            c0 = cg + half * 512
            k.mmg([(bG[0:NS, :], xnTs[:, kc, :], w_in_bf[:, kc, c0:c0 + 512], kc == 0, kc == 7) for kc in range(8)],
                  wres(c0, c0 + 512) + ["xnTs"], [bGn])
            tl = stg_tiles[0] if nm == "tg" else stg_tiles[1]
            k.act(tl, bG[0:NS, :], AF.Tanh, [bGn], [nm], scale=0.5)
            gts.append((tl, nm))
        k.stt(gts[0][0], gts[0][0], 1.0, bA[0:NS, :], ALU.add, ALU.mult, ["tg", bAn], ["tg"])
        yB = tmp4[0:NS, :, :].rearrange("p a b -> p (a b)")
        k.cp("act", yB, bBs[1][0][0:NS, :], [bBs[1][1]], ["tmp4"])
        k.tt("dve", yB, yB, bBs[0][0][0:NS, :], ALU.add, ["tmp4", bBs[0][1]], ["tmp4"])
        k.stt(gts[1][0], gts[1][0], 1.0, yB, ALU.add, ALU.mult, ["rs", "tmp4"], ["rs"])
        k.tt("pool", mrg_s[:, hs], gts[0][0], gts[1][0], ALU.add, ["tg", "rs"], ["xn"])

    ptv = pt[:, 0:8 * NS].rearrange("p (kc t) -> p kc t", kc=8)
    k.trg([(ptv[:, kc, :], xn[0:NS, kc * 128:(kc + 1) * 128], ident_bf[0:NS, 0:NS]) for kc in range(8)], ["xn", "ident_bf"], ["pt"])
    k.cp("dve", mTs[:, :, :], ptv, ["pt"], ["mTs"])
    for half in range(2):
        hs = slice(half * 512, (half + 1) * 512)
        b, bn = bank()
        k.mmg([(b[0:NS, :], mTs[:, kc, :], wout[:, kc, hs], kc == 0, kc == 7) for kc in range(8)], ["mTs", "wout0", "wout1"], [bn])
        k.stt(x_s[:, hs], b[0:NS, :], 0.5, x_s[:, hs], ALU.mult, ALU.add, [bn, "xbc_raw"], ["xbc_raw"])
    k.act(junk[0:NS, :], x_s, AF.Square, ["xbc_raw"], ["junk", "ss"], accum_out=ss[0:NS, 2:3])
    k.ts("pool", sm[0:NS, 2:3], ss[0:NS, 2:3], 1.0 / D, EPS, ALU.mult, ALU.add, ["ss"], ["sm"])
    k.tt("pool", sm[0:NS, 2:3], sm[0:NS, 2:3], negh[0:NS, 0:1], ALU.pow, ["sm", "negh"], ["sm"])
    k.stt(x_s, x_s, sm[0:NS, 2:3], fng_rep[0:NS, :], ALU.mult, ALU.mult, ["xbc_raw", "sm", "fng_rep"], ["xbc_raw"])
    k.dma("sp", "xbc_raw", [(ys_o, x_s)], ["xbc_raw"], [])


_CACHE = {}


def _in_maps(inputs):
    f = lambda a: np.ascontiguousarray(np.asarray(a, dtype=np.float32))
    shared = {
        "norm_g": f(inputs["norm_g"][0]), "w_in": f(inputs["w_in"][0]), "w_a2": f(inputs["w_a2"][0]), "b_a2": f(inputs["b_a2"][0]),
        "gla_norm_g": f(inputs["gla_norm_g"][0]), "conv_w": f(inputs["conv_w"][0]), "conv_b": f(inputs["conv_b"][0]),
        "dt_bias": f(inputs["dt_bias"][0]), "a_log": f(inputs["a_log"][0]), "d_skip": f(inputs["d_skip"][0]),
        "ssd_norm_g": f(inputs["ssd_norm_g"][0]), "w_br_gla": f(inputs["w_br_gla"][0]), "w_br_ssd": f(inputs["w_br_ssd"][0]),
        "w_out": f(inputs["w_out"][0]), "final_norm_g": f(inputs["final_norm_g"]),
    }
    maps = []
    for c in range(8):
        m = dict(shared)
        m["xp"] = f(inputs["x_prompt"][c])
        m["xs"] = f(inputs["x_sample"][16 * c:16 * c + 16, 0])
        m["sgla"] = f(inputs["state_gla"][0, 16 * c:16 * c + 16]).reshape(64, 2, 4096)
        m["sssm"] = f(inputs["state_ssm"][0, 16 * c:16 * c + 16]).reshape(32, 4, 4096)
        m["sconv"] = f(inputs["state_conv"][0, 16 * c:16 * c + 16])
        maps.append(m)
    return maps


def kernel(**inputs):
    if "nc" not in _CACHE:
        _CACHE["nc"] = build_program()
    nc = _CACHE["nc"]
    res = run_bass_kernel_spmd(nc, _in_maps(inputs), core_ids=list(range(8)))
    r = res.results
    y_prompt = np.stack([r[c]["yp"] for c in range(8)], 0)
    y_sample = np.concatenate([r[c]["ys"] for c in range(8)], 0).reshape(128, 1, D)
    gla_p = np.stack([r[c]["glap"].reshape(4, 64, 128) for c in range(8)], 0)[None]
    ssm_p = np.stack([r[c]["ssmp"].reshape(8, 64, 64) for c in range(8)], 0)[None]
    conv_p = np.stack([r[c]["convp"] for c in range(8)], 0)[None]
    gla_s = np.concatenate([r[c]["glas"].reshape(16, 4, 64, 128) for c in range(8)], 0)[None]
    ssm_s = np.concatenate([r[c]["ssms"].reshape(16, 8, 64, 64) for c in range(8)], 0)[None]
    conv_s = np.concatenate([r[c]["convs"] for c in range(8)], 0)[None]
    outs = (y_prompt, y_sample, gla_p, ssm_p, conv_p, gla_s, ssm_s, conv_s)
    return tuple(np.ascontiguousarray(o, dtype=np.float32) for o in outs)
```

```python
import numpy as np
from contextlib import ExitStack
import concourse.bass as bass
import concourse.mybir as mybir
from concourse.bass_utils import run_bass_kernel_spmd

F32 = mybir.dt.float32
BF16 = mybir.dt.bfloat16
I32 = mybir.dt.int32
AF = mybir.ActivationFunctionType
ALU = mybir.AluOpType
AX = mybir.AxisListType

ENGS = ("pe", "act", "dve", "pool", "sp")


class Op:
    __slots__ = ("eng", "fn", "waits", "dma_key", "dma_n", "ev")

    def __init__(self, eng, fn):
        self.eng = eng
        self.fn = fn
        self.waits = []
        self.dma_key = None
        self.dma_n = 0
        self.ev = None


class Prog:
    def __init__(self, nc):
        self.nc = nc
        self.ops = {e: [] for e in ENGS}
        self.res = {}
        self.waited = {e: {} for e in ENGS}
        self.dma_count = {}
        self.all_dma_keys = []
        self.nprog = {e: 0 for e in ENGS}
        import os
        self.limit = int(os.environ.get("MK_MAXOPS", "100000000"))
        self.count = 0
        self.log = []

    def _deps(self, reads, writes):
        deps = []
        for r in reads:
            st = self.res.get(r)
            if st and st[0] is not None:
                deps.append(st[0])
            if st and (r.startswith("pb") or r == "pt"):
                deps.extend(st[1])
        for w in writes:
            st = self.res.get(w)
            if st:
                if st[0] is not None:
                    deps.append(st[0])
                deps.extend(st[1])
        return deps

    def _commit(self, ev, reads, writes):
        for r in reads:
            st = self.res.setdefault(r, [None, []])
            st[1].append(ev)
        for w in writes:
            self.res[w] = [ev, []]

    def _add_waits(self, op, deps):
        eng = op.eng
        best = {}
        for (k, v) in deps:
            if k == eng and eng == "pe":
                continue
            if best.get(k, 0) < v:
                best[k] = v
        for k, v in best.items():
            if self.waited[eng].get(k, 0) >= v:
                continue
            self.waited[eng][k] = v
            op.waits.append((k, v))

    def op(self, eng, fn, reads=(), writes=()):
        self.count += 1
        if self.count > self.limit:
            return None
        self.log.append((self.count, eng, tuple(reads), tuple(writes)))
        o = Op(eng, fn)
        deps = self._deps(reads, writes)
        self._add_waits(o, deps)
        self.ops[eng].append(o)
        self.nprog[eng] += 1
        ev = (eng, self.nprog[eng])
        o.ev = ev
        self._commit(ev, reads, writes)
        return o

    def dma(self, queue, key, fn, reads=(), writes=(), n=1):
        self.count += 1
        if self.count > self.limit:
            return None
        self.log.append((self.count, "dma:" + queue, tuple(reads), tuple(writes)))
        o = Op(queue, fn)
        deps = self._deps(reads, writes)
        self._add_waits(o, deps)
        semkey = "dma:" + key
        if semkey not in self.dma_count:
            self.dma_count[semkey] = 0
            self.all_dma_keys.append(semkey)
        self.dma_count[semkey] += 16 * n
        o.dma_key = semkey
        o.dma_n = n
        self.ops[queue].append(o)
        ev = (semkey, self.dma_count[semkey])
        o.ev = ev
        self._commit(ev, reads, writes)
        return o

    def barrier(self):
        evs = []
        for e in ENGS:
            if self.nprog[e]:
                evs.append((e, self.nprog[e]))
        for k in self.all_dma_keys:
            evs.append((k, self.dma_count[k]))
        for e in ENGS:
            o = Op(e, None)
            self._add_waits(o, [ev for ev in evs if not (ev[0] == e)])
            self.ops[e].append(o)
            self.nprog[e] += 1
            o.ev = (e, self.nprog[e])
        self.res = {}

    def finish(self):
        evs = []
        for e in ENGS:
            if e != "sp" and self.nprog[e]:
                evs.append((e, self.nprog[e]))
        for k in self.all_dma_keys:
            evs.append((k, self.dma_count[k]))
        o = Op("sp", None)
        self._add_waits(o, evs)
        self.ops["sp"].append(o)
        self.nprog["sp"] += 1
        o.ev = ("sp", self.nprog["sp"])

    def emit(self, stack):
        nc = self.nc
        sems = {}
        for e in ENGS:
            sems[e] = stack.enter_context(nc.semaphore("s_" + e))
        for i, k in enumerate(self.all_dma_keys):
            sems[k] = stack.enter_context(nc.semaphore("d%d" % i))
        block = stack.enter_context(nc.Block())
        ops = self.ops

        def replay(ename, eng):
            mysem = sems[ename]
            for o in ops[ename]:
                for (k, v) in o.waits:
                    eng.wait_ge(sems[k], v)
                if o.dma_key is not None:
                    instrs = o.fn(eng)
                    assert len(instrs) == o.dma_n, (len(instrs), o.dma_n)
                    for ins in instrs:
                        ins.then_inc(sems[o.dma_key], 16)
                elif o.fn is None:
                    eng.nop().then_inc(mysem, 1)
                else:
                    ins = o.fn(eng)
                    ins.then_inc(mysem, 1)

        @block.tensor
        def _(eng):
            replay("pe", eng)

        @block.scalar
        def _(eng):
            replay("act", eng)

        @block.vector
        def _(eng):
            replay("dve", eng)

        @block.gpsimd
        def _(eng):
            replay("pool", eng)

        @block.sync
        def _(eng):
            replay("sp", eng)


D = 1024
T = 2048
NS = 16
IN_DIM = 4888
C_Q, C_K, C_V, C_G, C_A, C_Z, C_X, C_DT, C_GA, C_GB = 0, 256, 512, 1024, 1536, 1552, 2064, 2832, 2840, 3864
EPS = 1e-6
SC = 256
NSC = T // SC
CH = 128
NEG = -30000.0


class K:
    def __init__(self, nc, P, st):
        self.nc, self.P, self.st = nc, P, st
        self.bank_i = 0

    def sb(self, name, shape, dt):
        return self.st.enter_context(self.nc.sbuf_tensor(name, shape, dt))

    def ps(self, name, shape, dt):
        return self.st.enter_context(self.nc.psum_tensor(name, shape, dt))

    def mmg(self, mms, reads, writes):
        def fn(e, mms=mms):
            ins = None
            for (o, l, r, s0, s1) in mms:
                ins = e.matmul(o, lhsT=l, rhs=r, start=s0, stop=s1)
            return ins
        return self.P.op("pe", fn, reads, writes)

    def trg(self, trs, reads, writes):
        def fn(e, trs=trs):
            ins = None
            for (o, i, idn) in trs:
                ins = e.transpose(o, i, idn)
            return ins
        return self.P.op("pe", fn, reads, writes)

    def act(self, out, in_, func, reads, writes, **kw):
        def fn(e):
            return e.activation(out=out, in_=in_, func=func, **kw)
        return self.P.op("act", fn, reads, writes)

    def tt(self, eng, out, in0, in1, op, reads, writes):
        def fn(e):
            return e.tensor_tensor(out=out, in0=in0, in1=in1, op=op)
        return self.P.op(eng, fn, reads, writes)

    def ts(self, eng, out, in0, s1, s2, op0, op1, reads, writes):
        def fn(e):
            if op1 is None and eng == "pool" and op0 == ALU.mult:
                return e.tensor_scalar(out=out, in0=in0, scalar1=s1, scalar2=0.0, op0=ALU.mult, op1=ALU.add)
            if op1 is None:
                return e.tensor_scalar(out=out, in0=in0, scalar1=s1, scalar2=None, op0=op0)
            return e.tensor_scalar(out=out, in0=in0, scalar1=s1, scalar2=s2, op0=op0, op1=op1)
        return self.P.op(eng, fn, reads, writes)

    def stt(self, out, in0, scalar, in1, op0, op1, reads, writes):
        def fn(e):
            return e.scalar_tensor_tensor(out=out, in0=in0, scalar=scalar, in1=in1, op0=op0, op1=op1)
        return self.P.op("dve", fn, reads, writes)

    def cp(self, eng, out, in_, reads, writes):
        if eng == "act":
            return self.act(out, in_, AF.Identity, reads, writes)
        def fn(e):
            return e.tensor_copy(out=out, in_=in_)
        return self.P.op(eng, fn, reads, writes)

    def memset(self, eng, ap, val, writes):
        def fn(e):
            return e.memset(ap, val)
        return self.P.op(eng, fn, (), writes)

    def dma(self, q, key, pairs, reads, writes, slow=False):
        def fn(e, pairs=pairs):
            if slow:
                return [e.dma_start(out=o, in_=i, allow_slow_non_contiguous=True) for (o, i) in pairs]
            return [e.dma_start(out=o, in_=i) for (o, i) in pairs]
        return self.P.dma(q, key, fn, reads, writes, n=len(pairs))


def bc_rows(ap1d, nparts, n, off=0):
    return bass.AP(ap1d.tensor, off, [[0, nparts], [1, n]])


def build_program(with_sample=True, stop=None):
    nc = bass.Bass("TRN2", target_bir_lowering=False)
    P = Prog(nc)

    def dr(n, s, kind="ExternalInput", dt=F32):
        return nc.dram_tensor(n, s, dt, kind=kind).ap()

    xp = dr("xp", [T, D]); xs_d = dr("xs", [NS, D])
    sgla = dr("sgla", [64, 2, 4096]); sssm = dr("sssm", [32, 4, 4096]); sconv = dr("sconv", [NS, 3, 768])
    norm_g = dr("norm_g", [D]); w_in = dr("w_in", [D, IN_DIM]); w_a2 = dr("w_a2", [16, 256]); b_a2 = dr("b_a2", [256])
    gla_ng = dr("gla_norm_g", [512]); conv_w = dr("conv_w", [4, 768]); conv_b = dr("conv_b", [768])
    dt_bias = dr("dt_bias", [8]); a_log = dr("a_log", [8]); d_skip = dr("d_skip", [8]); ssd_ng = dr("ssd_norm_g", [512])
    w_brg_d = dr("w_br_gla", [512, D]); w_brs_d = dr("w_br_ssd", [512, D]); w_out_d = dr("w_out", [D, D]); fng = dr("final_norm_g", [D])
    yp = dr("yp", [T, D], "ExternalOutput"); ys_o = dr("ys", [NS, D], "ExternalOutput")
    glap = dr("glap", [256, 128], "ExternalOutput"); ssmp = dr("ssmp", [512, 64], "ExternalOutput")
    convp = dr("convp", [3, 768], "ExternalOutput")
    glas = dr("glas", [64, 2, 4096], "ExternalOutput"); ssms = dr("ssms", [32, 4, 4096], "ExternalOutput")
    convs = dr("convs", [NS, 3, 768], "ExternalOutput")

    with ExitStack() as st:
        k = K(nc, P, st)
        sb, ps = k.sb, k.ps
        w_in_bf = sb("w_in_bf", [128, 8, IN_DIM], BF16)
        wbrg = sb("wbrg", [128, 4, D], BF16)
        wbrs = sb("wbrs", [128, 4, D], BF16)
        wout = sb("wout", [128, 8, D], BF16)
        wa2 = sb("wa2", [16, 256], BF16)
        ident_bf = sb("ident_bf", [128, 128], BF16)
        ident_f = sb("ident_f", [128, 128], F32)
        tri_f = sb("tri_f", [128, 128], F32)
        mask_bf = sb("mask_bf", [128, 128], BF16)
        ones_f = sb("ones_f", [128, 128], F32)
        ones_bf = sb("ones_bf", [128, 128], BF16)
        negh = sb("negh", [128, 1], F32)
        stg = sb("stg", [48, 128], F32)
        prm = sb("prm", [128, 48], F32)
        nba2 = sb("nba2", [128, 2], F32)
        dtb_rep = sb("dtb_rep", [128, 8], F32)
        A_rep = sb("A_rep", [128, 8], F32)
        dsk_rep = sb("dsk_rep", [128, 8], F32)
        dskc = sb("dskc", [128, 4], F32)
        fng_rep = sb("fng_rep", [128, D], F32)
        pt = ps("pt", [128, 1024], BF16)
        banks = [ps("pb%d" % i, [128, 512], F32) for i in range(7)]

        held = set()

        pool_i = {}

        def bank(hold=False, pool=(0, 1, 2, 3, 4, 5, 6)):
            n = len(pool)
            i0 = pool_i.get(pool, 0)
            for t in range(n):
                i = pool[(i0 + t) % n]
                if i not in held:
                    pool_i[pool] = (i0 + t + 1) % n
                    if hold:
                        held.add(i)
                    return banks[i], "pb%d" % i
            raise RuntimeError("no free PSUM bank in pool %r" % (pool,))

        def release(bn):
            held.discard(int(bn[2:]))

        def v4(ap, h):
            return ap.rearrange("p (h i) -> p h i", h=h)

        k.memset("pool", ones_f[:], 1.0, ["ones_f"])
        k.memset("pool", ones_bf[:], 1.0, ["ones_bf"])
        k.memset("pool", negh[:], -0.5, ["negh"])
        P.op("pool", lambda e: e.affine_select(out=tri_f[:], in_=ones_f[:], pattern=[[1, 128]], compare_op=ALU.is_ge,
                                               fill=0.0, base=0, channel_multiplier=-1), ["ones_f"], ["tri_f"])
        P.op("pool", lambda e: e.affine_select(out=ident_f[:], in_=ones_f[:], pattern=[[1, 128]], compare_op=ALU.is_equal,
                                               fill=0.0, base=0, channel_multiplier=-1), ["ones_f"], ["ident_f"])
        k.cp("pool", mask_bf[:], tri_f[:], ["tri_f"], ["mask_bf"])
        k.cp("pool", ident_bf[:], ident_f[:], ["ident_f"], ["ident_bf"])

        lvl = stop[1] if (stop is not None and stop[0] == 0) else 99
        ng = prm[:, 0:8]
        cb = prm[:, 10:16]

        def cw(w, cc):
            return prm[:, 24 + w * 6 + cc: 25 + w * 6 + cc]

        xt = [sb("xt%d" % i, [128, D], F32) for i in range(2)]
        xr = [sb("xr%d" % i, [128, D], F32) for i in range(2)]
        xn = sb("xn", [128, D], BF16)
        scrB = sb("scrB", [128, 1024], BF16)
        ATm = scrB[:, 0:512].rearrange("p (h i) -> p h i", h=4)
        Mg = scrB[:, 512:1024].rearrange("p (h i) -> p h i", h=4)
        ss = sb("ss", [128, 8], F32)
        xnT = [sb("xnT%d" % i, [128, 8, SC], BF16) for i in range(2)]
        alr = sb("alr", [16, SC], BF16)
        bpos = sb("bpos", [128, 2, SC], F32)
        eb = sb("eb", [128, 2, SC], F32)
        enb = sb("enb", [128, 2, SC], F32)
        ebl = [sb("ebl%d" % i, [128, 2, 2], F32) for i in range(2)]
        qtT = [sb("qtT%d" % i, [128, 2, SC], BF16) for i in range(2)]
        ktT = [sb("ktT%d" % i, [128, 2, SC], BF16) for i in range(2)]
        gs = [sb("gs%d" % i, [128, 4, SC], BF16) for i in range(2)]
        zs = [sb("zs%d" % i, [128, 4, SC], BF16) for i in range(2)]
        xbc_raw = sb("xbc_raw", [128, 6, SC + 3], F32)
        cacc = sb("cacc", [128, SC], F32)
        xsT = [sb("xsT%d" % i, [128, 4, SC], F32) for i in range(2)]
        BT = [sb("BT%d" % i, [128, SC], BF16) for i in range(2)]
        CT = [sb("CT%d" % i, [128, SC], BF16) for i in range(2)]
        vtok = sb("vtok", [128, 2, 512], BF16)
        dtt = [sb("dtt%d" % i, [128, 2, 8], F32) for i in range(2)]
        latok = [sb("latok%d" % i, [128, 2, 8], F32) for i in range(2)]
        ktok = sb("ktok", [128, 256], BF16)
        S_f = sb("S_f", [128, 2, 128], F32)
        S_bf = sb("S_bf", [128, 2, 128], BF16)
        hT_f = sb("hT_f", [128, 4, 64], F32)
        hT_bf = sb("hT_bf", [128, 4, 64], BF16)
        tmp4 = sb("tmp4", [128, 4, 128], F32)
        diff = tmp4
        y1 = tmp4
        sq = sb("sq", [128, 4, 128], BF16)
        rs = sb("rs", [128, 4, 128], F32)
        expcum = rs
        ccol = sb("ccol", [128, 8], F32)
        decay = sb("decay", [128, 8, 128], BF16)
        scm = sb("scm", [128, 2, 128], BF16)
        CsT = sb("CsT", [128, 4, 128], BF16)
        xdt = sb("xdt", [128, 8, 64], BF16)
        xw = sb("xw", [128, 8, 64], BF16)
        Btok = sb("Btok", [128, 128], BF16)
        ogT = sb("ogT", [128, 4, SC], BF16)
        ygT = sb("ygT", [128, 4, SC], BF16)
        tg = sb("tg", [128, 2, SC], F32)
        mrgT = sb("mrgT", [128, 8, SC], BF16)
        sm = sb("sm", [128, 64], F32)

        w_in_v = w_in.rearrange("(kc p) c -> p kc c", p=128)
        col_blocks = [(C_A, C_A + 16), (C_DT, C_DT + 8), (C_K, C_K + 256), (C_Q, C_Q + 256), (C_X, C_X + 384), (C_X + 384, C_X + 768),
                      (C_G, C_G + 512), (C_Z, C_Z + 512), (C_V, C_V + 512),
                      (C_GA, C_GA + 512), (C_GA + 512, C_GA + 1024), (C_GB, C_GB + 512), (C_GB + 512, C_GB + 1024)]
        if lvl >= 2:
            k.dma("pool", "wa2", [(wa2[:, :], w_a2)], [], ["wa2"])
        wthr = [0]

        def wdma(key, out_ap, in_ap):
            k.dma("pool", key, [(out_ap, in_ap)], [], [key, "wthr%d" % (wthr[0] % 6)])
            wthr[0] += 1

        N_EARLY = 9
        if lvl >= 2:
            for (c0, c1) in col_blocks[:N_EARLY]:
                wdma("w_in_%d" % c0, w_in_bf[:, :, c0:c1], w_in_v[:, :, c0:c1])

        def wres(c0, c1):
            return ["w_in_%d" % a for (a, b) in col_blocks if a < c1 and b > c0]

        k.memset("dve", S_f[:], 0.0, ["S_f"])
        k.memset("dve", S_bf[:], 0.0, ["S_bf"])
        k.memset("dve", hT_f[:], 0.0, ["hT_f"])
        k.memset("dve", hT_bf[:], 0.0, ["hT_bf"])
        k.memset("dve", xbc_raw[:], 0.0, ["xbc_raw"])

        def load_x(tile_idx):
            slot = tile_idx % 2
            k.dma("sp", "xt%d" % slot, [(xt[slot][:], xp[tile_idx * 128:(tile_idx + 1) * 128, :])], [], ["xt%d" % slot])

        if stop is None or stop[0] > 0:
            load_x(0)
            load_x(1)
        if lvl >= 1:
            k.dma("sp", "stg", [
                (stg[0:8, :], norm_g.rearrange("(a p) -> a p", p=128)),
                (stg[8:10, :], b_a2.rearrange("(a p) -> a p", p=128)),
                (stg[10:16, :], conv_b.rearrange("(a p) -> a p", p=128)),
                (stg[16:20, :], gla_ng.rearrange("(a p) -> a p", p=128)),
                (stg[20:24, :], ssd_ng.rearrange("(a p) -> a p", p=128)),
                (stg[24:48, :], conv_w.rearrange("w (a p) -> (w a) p", p=128)),
            ], [], ["stg"])
            k.dma("sp", "prm2", [
                (dtb_rep[:], bc_rows(dt_bias, 128, 8)),
                (A_rep[:], bc_rows(a_log, 128, 8)),
                (dsk_rep[:], bc_rows(d_skip, 128, 8)),
                (fng_rep[:], bc_rows(fng, 128, D)),
            ], [], ["dtb_rep", "A_rep", "dsk_rep", "fng_rep"])
            b0, b0n = bank()
            k.trg([(b0[:, 0:48], stg[:, :], ident_f[0:48, 0:48])], ["stg", "ident_f"], [b0n])
            k.cp("dve", prm[:], b0[:, 0:48], [b0n], ["prm"])
            k.ts("dve", nba2[:], prm[:, 8:10], -1.0, None, ALU.mult, None, ["prm"], ["nba2"])
            k.act(A_rep[:], A_rep[:], AF.Exp, ["A_rep"], ["A_rep"])
            k.ts("dve", A_rep[:], A_rep[:], -1.0, None, ALU.mult, None, ["A_rep"], ["A_rep"])
            dsk2 = dsk_rep[:, :].rearrange("p (c two) -> p c two", two=2)
            k.cp("dve", dskc[0:64, :], dsk2[0:64, :, 0], ["dsk_rep"], ["dskc"])
            k.cp("dve", dskc[64:128, :], dsk2[64:128, :, 1], ["dsk_rep", "dskc"], ["dskc"])
        def rstd_from_ss(col, n, out_col):
            rn = "ss%d" % col
            k.ts("pool", sm[:, out_col:out_col + 1], ss[:, col:col + 1], 1.0 / n, EPS, ALU.mult, ALU.add, [rn], ["sm%d" % out_col])
            k.tt("pool", sm[:, out_col:out_col + 1], sm[:, out_col:out_col + 1], negh[:, 0:1], ALU.pow, ["sm%d" % out_col, "negh"],
                 ["sm%d" % out_col])

        def staged_weight(dst3, src2d, nkc, scale_cols, const_scale, name):
            for kc in range(nkc):
                slot = kc % 2
                k.dma("sp", "xr%d" % slot, [(xr[slot][:], src2d[kc * 128:(kc + 1) * 128, :])], [], ["xr%d" % slot])
                if scale_cols is not None:
                    k.act(dst3[:, kc, :], xr[slot][:], AF.Identity, ["xr%d" % slot, "prm"], [name], scale=scale_cols[:, kc:kc + 1])
                else:
                    k.act(dst3[:, kc, :], xr[slot][:], AF.Identity, ["xr%d" % slot], [name], scale=const_scale)

        POOL_A = (0, 1, 2)
        POOL_B = (3, 4, 5, 6)

        def stage1a(s):
            pb = s % 2
            xs_ = xnT[pb]
            xsn = "xnT%d" % pb
            sfx = str(pb)
            for ti in range(2):
                tile_idx = 2 * s + ti
                slot = tile_idx % 2
                xtn = "xt%d" % slot
                k.act(xn[:], xt[slot][:], AF.Square, [xtn], ["xn", "ss0"], accum_out=ss[:, 0:1])
                yield
                rstd_from_ss(0, D, 0)
                k.ts("pool", xn[:], xt[slot][:], sm[:, 0:1], None, ALU.mult, None, [xtn, "sm0"], ["xn"])
                yield
                bxt, bxtn = bank(pool=POOL_A)
                ptv = bxt[:, :].bitcast(BF16).rearrange("p (kc t) -> p kc t", kc=8)
                k.trg([(ptv[:, kc, :], xn[:, kc * 128:(kc + 1) * 128], ident_bf[:]) for kc in range(8)], ["xn", "ident_bf"], [bxtn])
                k.tt("dve", xs_[:, :, ti * 128:(ti + 1) * 128], ptv, ng.unsqueeze(2).broadcast_to([128, 8, 128]), ALU.mult,
                     [bxtn, "prm"], [xsn])
                if tile_idx + 2 < 2 * NSC:
                    load_x(tile_idx + 2)
                yield

            def proj_fm(c0, m):
                b, bn = bank(pool=POOL_A)
                k.mmg([(b[0:m, 0:SC], w_in_bf[:, kc, c0:c0 + m], xs_[:, kc, :], kc == 0, kc == 7) for kc in range(8)],
                      wres(c0, c0 + m) + [xsn], [bn])
                return b, bn

            b, bn = proj_fm(C_A, 16)
            k.cp("act", alr[:, :], b[0:16, 0:SC], [bn], ["alr"])
            yield
            b, bn = bank(pool=POOL_A)
            bv = v4(b[:, :], 2)
            k.mmg([(bv[:, cc, :], wa2[:, cc * 128:(cc + 1) * 128], alr[:, :], True, True) for cc in range(2)], ["wa2", "alr"], [bn])
            yield
            for cc in range(2):
                k.act(eb[:, cc, :], bv[:, cc, :], AF.Exp, [bn, "nba2"], ["eb"], scale=-1.0, bias=nba2[:, cc:cc + 1])
            k.act(eb[:, :, :], eb[:, :, :], AF.Ln, ["eb"], ["eb"], bias=1.0)
            yield
            b, bn = bank(pool=POOL_A)
            bdt = b[:, 0:16].rearrange("p (t h) -> p t h", t=2)
            for ti in range(2):
                k.mmg([(bdt[:, ti, :], xs_[:, kc, ti * 128:(ti + 1) * 128], w_in_bf[:, kc, C_DT:C_DT + 8], kc == 0, kc == 7)
                       for kc in range(8)], wres(C_DT, C_DT + 8) + [xsn], [bn])
            dtn, lan = "dtt" + sfx, "latok" + sfx
            k.tt("dve", dtt[pb][:, :, :], bdt, dtb_rep[:, :].unsqueeze(1).broadcast_to([128, 2, 8]), ALU.add, [bn, "dtb_rep"], [dtn])
            yield
            k.act(dtt[pb][:, :, :], dtt[pb][:, :, :], AF.Exp, [dtn], [dtn])
            k.act(dtt[pb][:, :, :], dtt[pb][:, :, :], AF.Ln, [dtn], [dtn], bias=1.0)
            yield
            k.tt("dve", latok[pb][:, :, :], dtt[pb][:, :, :], A_rep[:, :].unsqueeze(1).broadcast_to([128, 2, 8]), ALU.mult,
                 [dtn, "A_rep"], [lan])
            for cc in range(2):
                for ci in range(2):
                    sl = slice(ci * 128, (ci + 1) * 128)

                    def fn(e, cc=cc, sl=sl):
                        return e.tensor_tensor_scan(out=bpos[:, cc, sl], data0=ones_f[:, :], data1=eb[:, cc, sl], initial=0.0,
                                                    op0=ALU.mult, op1=ALU.add)
                    P.op("dve", fn, ["eb", "ones_f"], ["bpos"])
            k.act(eb[:, :, :], bpos[:, :, :], AF.Exp, ["bpos"], ["eb"], scale=-1.0 / 16)
            k.act(enb[:, :, :], bpos[:, :, :], AF.Exp, ["bpos"], ["enb"], scale=1.0 / 16)
            yield
            k.cp("pool", ebl[pb][:, :, :], eb[:, :, 127:SC:128], ["eb"], ["ebl" + sfx])
            yield
            for cc in range(2):
                b, bn = proj_fm(C_K + cc * 128, 128)
                yield
                k.tt("dve", ktT[pb][:, cc, :], b[:, 0:SC], enb[:, cc, :], ALU.mult, [bn, "enb"], ["ktT" + sfx])
                yield
            for cc in range(2):
                b, bn = proj_fm(C_Q + cc * 128, 128)
                yield
                k.stt(qtT[pb][:, cc, :], b[:, 0:SC], 0.125, eb[:, cc, :], ALU.mult, ALU.mult, [bn, "eb"], ["qtT" + sfx])
                yield


        def stage1b(s):
            pb = s % 2
            xs_ = xnT[pb]
            xsn = "xnT%d" % pb
            sfx = str(pb)
            def proj_fm(c0, m):
                b, bn = bank(pool=POOL_A)
                k.mmg([(b[0:m, 0:SC], w_in_bf[:, kc, c0:c0 + m], xs_[:, kc, :], kc == 0, kc == 7) for kc in range(8)],
                      wres(c0, c0 + m) + [xsn], [bn])
                return b, bn

            for cc in range(6):
                b, bn = proj_fm(C_X + cc * 128, 128)
                yield
                k.cp("act", xbc_raw[:, cc, 3:SC + 3], b[:, 0:SC], [bn], ["xbc_raw"])
                yield
                k.ts("dve", cacc[:, :], xbc_raw[:, cc, 0:SC], cw(0, cc), cb[:, cc:cc + 1], ALU.mult, ALU.add, ["xbc_raw", "prm"], ["cacc"])
                for w in range(1, 4):
                    k.stt(cacc[:, :], xbc_raw[:, cc, w:w + SC], cw(w, cc), cacc[:, :], ALU.mult, ALU.add, ["xbc_raw", "prm", "cacc"], ["cacc"])
                yield
                if cc < 4:
                    k.act(xsT[pb][:, cc, :], cacc[:, :], AF.Silu, ["cacc"], ["xsT" + sfx])
                elif cc == 4:
                    k.act(BT[pb][:, :], cacc[:, :], AF.Silu, ["cacc"], ["BT" + sfx])
                else:
                    k.act(CT[pb][:, :], cacc[:, :], AF.Silu, ["cacc"], ["CT" + sfx])
                yield
            k.cp("pool", xbc_raw[:, :, 0:3], xbc_raw[:, :, SC:SC + 3], ["xbc_raw"], ["xbc_raw"])

        def stage1gz(s):
            pb = s % 2
            xs_ = xnT[pb]
            xsn = "xnT%d" % pb
            sfx = str(pb)
            def proj_fm(c0, m):
                b, bn = bank(pool=POOL_A)
                k.mmg([(b[0:m, 0:SC], w_in_bf[:, kc, c0:c0 + m], xs_[:, kc, :], kc == 0, kc == 7) for kc in range(8)],
                      wres(c0, c0 + m) + [xsn], [bn])
                return b, bn

            for cc in range(4):
                b, bn = proj_fm(C_G + cc * 128, 128)
                yield
                k.cp("act", gs[pb][:, cc, :], b[:, 0:SC], [bn], ["gs" + sfx])
                yield
            for cc in range(4):
                b, bn = proj_fm(C_Z + cc * 128, 128)
                yield
                k.cp("act", zs[pb][:, cc, :], b[:, 0:SC], [bn], ["zs" + sfx])
                yield


        def silu_gz(s):
            pb = s % 2
            sfx = str(pb)
            k.act(gs[pb][:, :, :], gs[pb][:, :, :], AF.Silu, ["gs" + sfx], ["gs" + sfx])
            k.act(zs[pb][:, :, :], zs[pb][:, :, :], AF.Silu, ["zs" + sfx], ["zs" + sfx])

        def stage2(s):
            pb = s % 2
            sfx = str(pb)
            xs_ = xnT[pb]
            xsn = "xnT" + sfx
            qn, kn, gn, zn, xsn_, Bn, Cn, dtn, lan = ("qtT" + sfx, "ktT" + sfx, "gs" + sfx, "zs" + sfx, "xsT" + sfx, "BT" + sfx,
                                                        "CT" + sfx, "dtt" + sfx, "latok" + sfx)
            q_, k_, g_, z_, x_, B_, C_, dt_, la_ = qtT[pb], ktT[pb], gs[pb], zs[pb], xsT[pb], BT[pb], CT[pb], dtt[pb], latok[pb]
            Mgs = [Mg, ATm]
            Mgn = ["Mg", "ATm"]
            t1 = tmp4
            y1_ = tg[:, :, :].rearrange("p a (b c) -> p (a b) c", c=128)
            diffs = [tmp4[:, :, :], y1_]
            diffn = ["tmp4", "tg"]
            def chunk_gen(ci):
                sl = slice(ci * 128, (ci + 1) * 128)
                vn = "vtok%d" % ci
                b, bn = bank(pool=POOL_B)
                k.mmg([(b[:, :], xs_[:, kc, sl], w_in_bf[:, kc, C_V:C_V + 512], kc == 0, kc == 7) for kc in range(8)],
                      wres(C_V, C_V + 512) + [xsn], [bn])
                k.cp("act", vtok[:, ci, :], b[:, :], [bn], [vn])
                ptk = pt[:, 0:256]
                k.trg([(ptk[:, cc * 128:(cc + 1) * 128], k_[:, cc, sl], ident_bf[:]) for cc in range(2)], [kn, "ident_bf"], ["pt"])
                k.cp("act", ktok[:, :], ptk, ["pt"], ["ktok"])
                bc_, bcn = bank(pool=POOL_B)
                k.mmg([(bc_[:, 0:8], tri_f[:, :], la_[:, ci, :], True, True)], ["tri_f", lan], [bcn])
                k.ts("dve", ccol[:, :], bc_[:, 0:8], -1.0, None, ALU.mult, None, [bcn], ["ccol"])
                yield
                for g in range(2):
                    pr = slice(64 * g, 64 * g + 64)
                    bcu, bcun = bank(pool=POOL_B)
                    bcuv = v4(bcu[:, :], 4)
                    k.mmg([(bcuv[:, hh, :], la_[:, ci, 4 * g + hh:4 * g + hh + 1].broadcast_to([128, 128]), tri_f[:, :], True, True)
                           for hh in range(4)], [lan, "tri_f"], [bcun])
                    for hh in range(4):
                        k.ts("dve", diffs[g][:, hh, :], bcuv[:, hh, :], ccol[:, 4 * g + hh:4 * g + hh + 1], 0.0, ALU.add, ALU.min,
                             [bcun, "ccol"], [diffn[g]])
                    k.act(decay[:, 4 * g:4 * g + 4, :], diffs[g], AF.Exp, [diffn[g]], ["decay%d" % g])
                    k.act(expcum[pr, :, :], bcuv[pr, :, :], AF.Exp, [bcun], ["rs%d" % g])
                yield
                bx, bxn = bank(pool=POOL_B)
                k.trg([(bx[:, cc * 128:(cc + 1) * 128], x_[:, cc, sl], ident_f[:]) for cc in range(4)], [xsn_, "ident_f"], [bxn])
                k.tt("dve", xdt[:, :, :], bx[:, :].rearrange("p (h q) -> p h q", h=8),
                     dt_[:, ci, :].unsqueeze(2).broadcast_to([128, 8, 64]), ALU.mult, [bxn, dtn], ["xdt"])
                ptb = pt[:, 256:384]
                k.trg([(ptb, B_[:, sl], ident_bf[:])], [Bn, "ident_bf"], ["pt"])
                k.cp("act", Btok[:, :], ptb, ["pt"], ["Btok"])
                for g in range(2):
                    pr = slice(64 * g, 64 * g + 64)
                    bsc, bscn = bank(pool=POOL_B)
                    k.mmg([(bsc[:, 0:128], B_[pr, sl], C_[pr, sl], True, True)], [Bn, Cn], [bscn])
                    k.tt("dve", scm[:, g, :], bsc[:, 0:128], mask_bf[:, :], ALU.mult, [bscn, "mask_bf"], ["scm%d" % g])
                yield
                ATv = ATm.rearrange("p (cc hh) i -> p hh cc i", hh=2)
                for hh in range(2):
                    pr = slice(64 * hh, 64 * hh + 64)
                    ba, ban = bank(pool=POOL_B)
                    bav = ba[:, 0:256].rearrange("p (c i) -> p c i", c=2)
                    k.mmg([(bav[:, cc, :], k_[pr, cc, sl], q_[pr, cc, sl], True, True) for cc in range(2)], [kn, qn], [ban])
                    k.tt("dve", ATv[:, hh, :, :], bav, mask_bf[:, :].unsqueeze(1).broadcast_to([128, 2, 128]), ALU.mult,
                         [ban, "mask_bf"], ["ATm"])
                yield
                for hh in range(2):
                    pr = slice(64 * hh, 64 * hh + 64)
                    bo, bon = bank(pool=POOL_B)
                    bov = bo[:, 0:256].rearrange("p (c i) -> p c i", c=2)
                    mms = []
                    for cc in range(2):
                        h = 2 * cc + hh
                        mms.append((bov[:, cc, :], vtok[:, ci, h * 128:(h + 1) * 128], ATm[:, h, :], True, False))
                        mms.append((bov[:, cc, :], S_bf[pr, cc, :], q_[pr, cc, sl], False, True))
                    k.mmg(mms, [vn, "ATm", "S_bf", qn], [bon])
                    k.act(sq[:, 2 * hh:2 * hh + 2, :], bov, AF.Square, [bon], ["sq"])
                    for cc in range(2):
                        k.tt("dve", t1[:, 2 * hh + cc, :], g_[:, 2 * cc + hh, sl], bov[:, cc, :], ALU.mult, [bon, gn], ["tmp4"])
                bk, bkn = bank(pool=POOL_B)
                bkv = bk[:, 0:256].rearrange("p (c e) -> p c e", c=2)
                mms = []
                for hh in range(2):
                    for cc in range(2):
                        h = 2 * cc + hh
                        mms.append((bkv[64 * hh:64 * hh + 64, cc, :], ktok[:, cc * 128 + hh * 64:cc * 128 + hh * 64 + 64],
                                    vtok[:, ci, h * 128:(h + 1) * 128], True, True))
                k.mmg(mms, ["ktok", vn], [bkn])
                k.tt("dve", S_f[:, :, :], S_f[:, :, :], bkv, ALU.add, ["S_f", bkn], ["S_f"])
                k.tt("dve", S_f[:, :, :], S_f[:, :, :], ebl[pb][:, :, ci:ci + 1].broadcast_to([128, 2, 128]), ALU.mult,
                     ["S_f", "ebl" + sfx], ["S_f"])
                k.cp("act", S_bf[:, :, :], S_f[:, :, :], ["S_f"], ["S_bf"])
                bsg, bsgn = bank(hold=True, pool=POOL_B)
                k.mmg([(bsg[:, :], ones_bf[:, :], sq[:, :, :].rearrange("p h i -> p (h i)"), True, True)], ["ones_bf", "sq"], [bsgn])
                k.act(bsg[:, :], bsg[:, :], AF.Ln, [bsgn], [bsgn], scale=1.0 / 128, bias=EPS)
                k.act(bsg[:, :], bsg[:, :], AF.Exp, [bsgn], [bsgn], scale=-0.5)
                yield
                k.tt("dve", CsT[:, :, :], expcum[:, :, :], C_[:, sl].unsqueeze(1).broadcast_to([128, 4, 128]), ALU.mult,
                     ["rs0", "rs1", Cn], ["CsT"])
                for g in range(2):
                    k.tt("dve", Mgs[g], decay[:, 4 * g:4 * g + 4, :], scm[:, g, :].unsqueeze(1).broadcast_to([128, 4, 128]), ALU.mult,
                         ["decay%d" % g, "scm%d" % g], [Mgn[g]])
                k.tt("dve", xw[:, :, :], xdt[:, :, :], decay[:, :, 127:128].broadcast_to([128, 8, 64]), ALU.mult,
                     ["xdt", "decay0", "decay1"], ["xw"])
                k.tt("dve", ogT[:, :, sl].rearrange("p (cc hh) i -> p hh cc i", hh=2),
                     t1[:, :, :].rearrange("p (hh cc) i -> p hh cc i", hh=2),
                     v4(bsg[:, :], 4).rearrange("p (hh cc) i -> p hh cc i", hh=2), ALU.mult, ["tmp4", bsgn], ["ogT"])
                release(bsgn)
                yield
                bh, bhn = bank(hold=True, pool=POOL_B)
                bhv = bh[:, 0:256].rearrange("p (h q) -> p h q", h=4)
                for g in range(2):
                    pr = slice(64 * g, 64 * g + 64)
                    by, byn = bank(pool=POOL_B)
                    byv = by[:, 0:256].rearrange("p (c i) -> p c i", c=2)
                    mms = []
                    for hh in range(4):
                        h = 4 * g + hh
                        po = byv[64 * (h % 2):64 * (h % 2) + 64, hh // 2, :]
                        mms.append((po, xdt[:, h, :], Mgs[g][:, hh, :], True, False))
                        mms.append((po, hT_bf[pr, hh, :], CsT[pr, hh, :], False, True))
                    k.mmg(mms, ["xdt", Mgn[g], "hT_bf", "CsT"], [byn])
                    for c2 in range(2):
                        cc = 2 * g + c2
                        k.stt(y1_[:, cc, :], x_[:, cc, sl], dskc[:, cc:cc + 1], byv[:, c2, :], ALU.mult, ALU.add, [xsn_, "dskc", byn], ["tg"])
                    k.mmg([(bh[pr, 0:256], Btok[:, 64 * g:64 * g + 64], xw[:, 4 * g:4 * g + 4, :].rearrange("p h q -> p (h q)"), True, True)],
                          ["Btok", "xw"], [bhn])
                k.tt("dve", hT_f[:, :, :], hT_f[:, :, :], expcum[:, :, 127:128].broadcast_to([128, 4, 64]), ALU.mult,
                     ["hT_f", "rs0", "rs1"], ["hT_f"])
                k.tt("dve", hT_f[:, :, :], hT_f[:, :, :], bhv, ALU.add, ["hT_f", bhn], ["hT_f"])
                k.cp("act", hT_bf[:, :, :], hT_f[:, :, :], ["hT_f"], ["hT_bf"])
                release(bhn)
                yield
                k.tt("dve", y1_, y1_, z_[:, :, sl], ALU.mult, ["tg", zn], ["tg"])
                k.act(sq[:, :, :], y1_, AF.Square, ["tg"], ["sq"])
                bs, bsn = bank(hold=True, pool=POOL_B)
                bsv = bs[:, 0:256].rearrange("p (g i) -> p g i", g=2)
                mms = []
                for g in range(2):
                    mms.append((bsv[:, g, :], ones_bf[:, :], sq[:, 2 * g, :], True, False))
                    mms.append((bsv[:, g, :], ones_bf[:, :], sq[:, 2 * g + 1, :], False, True))
                k.mmg(mms, ["ones_bf", "sq"], [bsn])
                k.act(bsv, bsv, AF.Ln, [bsn], [bsn], scale=1.0 / 256, bias=EPS)
                k.act(bsv, bsv, AF.Exp, [bsn], [bsn], scale=-0.5)
                k.tt("dve", ygT[:, :, sl].rearrange("p (g t) i -> p g t i", g=2), y1_.rearrange("p (g t) i -> p g t i", g=2),
                     bsv.unsqueeze(2).broadcast_to([128, 2, 2, 128]), ALU.mult, ["tg", bsn], ["ygT"])
                release(bsn)
                yield

            for ci in range(2):
                yield from chunk_gen(ci)

        def stage3_m(s):
            pb = s % 2
            xs_ = xnT[pb]
            xsn = "xnT%d" % pb
            for ti in range(2):
                tile_idx = 2 * s + ti
                k.dma("sp", "xr%d" % ti, [(xr[ti][:], xp[tile_idx * 128:(tile_idx + 1) * 128, :])], [], ["xr%d" % ti])
            for m in range(8):
                ms = slice(m * 128, (m + 1) * 128)
                bg, bgn = bank(pool=POOL_B)
                bgv = v4(bg[:, :], 2)
                mms = [(bgv[:, 0, :], w_in_bf[:, kc, C_GA + m * 128:C_GA + (m + 1) * 128], xs_[:, kc, :], kc == 0, kc == 7) for kc in range(8)]
                mms += [(bgv[:, 1, :], w_in_bf[:, kc, C_GB + m * 128:C_GB + (m + 1) * 128], xs_[:, kc, :], kc == 0, kc == 7) for kc in range(8)]
                k.mmg(mms, wres(C_GA + m * 128, C_GA + (m + 1) * 128) + wres(C_GB + m * 128, C_GB + (m + 1) * 128) + [xsn], [bgn])
                bab, babn = bank(pool=POOL_B)
                babv = v4(bab[:, :], 2)
                mms = [(babv[:, 0, :], wbrg[:, cc, ms], ogT[:, cc, :], cc == 0, cc == 3) for cc in range(4)]
                mms += [(babv[:, 1, :], wbrs[:, cc, ms], ygT[:, cc, :], cc == 0, cc == 3) for cc in range(4)]
                k.mmg(mms, ["wbrg", "wbrs", "ogT", "ygT"], [babn])
                k.act(tg[:, :, :], bgv, AF.Tanh, [bgn], ["tg"], scale=0.5)
                k.stt(tg[:, :, :], tg[:, :, :], 1.0, babv, ALU.add, ALU.mult, ["tg", babn], ["tg"])
                k.tt("pool", mrgT[:, m, :], tg[:, 0, :], tg[:, 1, :], ALU.add, ["tg"], ["mrgT"])
                yield
        def stage3_tail(s):
            for ti in range(2):
                tile_idx = 2 * s + ti
                slot = tile_idx % 2
                xrn = "xr%d" % slot
                for half in range(2):
                    b, bn = bank(pool=POOL_B)
                    hs = slice(half * 512, (half + 1) * 512)
                    k.mmg([(b[:, :], mrgT[:, kc, ti * 128:(ti + 1) * 128], wout[:, kc, hs], kc == 0, kc == 7) for kc in range(8)],
                          ["mrgT", "wout0", "wout1"], [bn])
                    k.stt(xr[slot][:, hs], b[:, :], 0.5, xr[slot][:, hs], ALU.mult, ALU.add, [bn, xrn], [xrn])
                yield
            for ti in range(2):
                tile_idx = 2 * s + ti
                slot = tile_idx % 2
                xrn = "xr%d" % slot
                rows = slice(tile_idx * 128, (tile_idx + 1) * 128)
                k.act(mrgT[:, :, ti * 128:(ti + 1) * 128], xr[slot][:].rearrange("p (a b) -> p a b", a=8), AF.Square, [xrn], ["mrgT", "ss1"],
                      accum_out=ss[:, 1:2])
                rstd_from_ss(1, D, 1)
                yield
                k.stt(xr[slot][:], xr[slot][:], sm[:, 1:2], fng_rep[:], ALU.mult, ALU.mult, [xrn, "sm1", "fng_rep"], [xrn])
                k.dma("pool", xrn + "_st", [(yp[rows, :], xr[slot][:])], [xrn], [])
                yield

        def late_weights():
            for (c0, c1) in col_blocks[N_EARLY:]:
                wdma("w_in_%d" % c0, w_in_bf[:, :, c0:c1], w_in_v[:, :, c0:c1])
            wdma("wbrg", wbrg[:, :, :], w_brg_d.rearrange("(kc p) c -> p kc c", p=128))
            wdma("wbrs", wbrs[:, :, :], w_brs_d.rearrange("(kc p) c -> p kc c", p=128))
            wov = w_out_d.rearrange("(kc p) c -> p kc c", p=128)
            wdma("wout0", wout[:, 0:4, :], wov[:, 0:4, :])
            wdma("wout1", wout[:, 4:8, :], wov[:, 4:8, :])

        def run(gen):
            for _ in gen:
                pass

        def chain(*gens):
            for gg in gens:
                yield from gg

        def interleave(gp, gf, rp=2, rf=1):
            dp = df = False
            while not (dp and df):
                for _ in range(rp):
                    if not dp:
                        try:
                            next(gp)
                        except StopIteration:
                            dp = True
                for _ in range(rf):
                    if not df:
                        try:
                            next(gf)
                        except StopIteration:
                            df = True

        nsc = NSC if stop is None else stop[0]
        if nsc > 0 and (stop is None or stop[1] >= 1):
            g1a = stage1a(0)
            for _ in range(6):
                next(g1a)
            interleave(g1a, chain(stage1b(0), stage1gz(0)), 2, 1)
            silu_gz(0)
            late_weights()
            def fold_gains():
                for cc in range(4):
                    k.ts("pool", wbrg[:, cc, :], wbrg[:, cc, :], prm[:, 16 + cc:17 + cc], None, ALU.mult, None, ["wbrg", "prm"], ["wbrg"])
                    k.ts("pool", wbrs[:, cc, :], wbrs[:, cc, :], prm[:, 20 + cc:21 + cc], None, ALU.mult, None, ["wbrs", "prm"], ["wbrs"])

            tail_prev = iter(())
            for s in range(nsc):
                nxt = s + 1 < nsc
                if stop is None or stop[1] >= 2:
                    interleave(stage2(s), chain(tail_prev, chain(stage1a(s + 1), stage1gz(s + 1)) if nxt else iter(())), 1, 3)
                tail_prev = iter(())
                if stop is None or stop[1] >= 3:
                    if s == 0:
                        fold_gains()
                    interleave(stage3_m(s), stage1b(s + 1) if nxt else iter(()), 1, 3)
                    if nxt:
                        silu_gz(s + 1)
                    tail_prev = stage3_tail(s)
            run(tail_prev)

        if lvl >= 3:
            k.dma("sp", "S_f", [(glap.rearrange("(cc p) e -> p cc e", p=128), S_f[:, :, :])], ["S_f"], [])
            bt_, btn = bank()
            btv = bt_[0:64, :].rearrange("p (hh g n) -> p hh g n", hh=4, g=2)
            k.trg([(bt_[0:64, hh * 128:(hh + 1) * 128], hT_f[:, hh, :], ident_f[:, :]) for hh in range(4)], ["hT_f", "ident_f"], [btn])
            hout = tmp4[0:64, :, :].rearrange("p a b -> p (a b)").rearrange("p (h n) -> p h n", h=8)
            k.cp("dve", hout.rearrange("p (g hh) n -> p hh g n", g=2), btv, [btn], ["tmp4"])
            k.dma("sp", "tmp4", [(ssmp.rearrange("(h p) n -> p h n", p=64), hout)], ["tmp4"], [])
        if lvl >= 4:
            bcv, bcvn = bank()
            k.mmg([(bcv[0:4, cc * 128:(cc + 1) * 128], xbc_raw[:, cc, 0:4], ident_f[:], True, True) for cc in range(4)], ["xbc_raw", "ident_f"], [bcvn])
            bcw, bcwn = bank()
            k.mmg([(bcw[0:4, (cc - 4) * 128:(cc - 3) * 128], xbc_raw[:, cc, 0:4], ident_f[:], True, True) for cc in (4, 5)], ["xbc_raw", "ident_f"], [bcwn])
            cst = diff[0:3, :, :].rearrange("p a b -> p (a b)")
            cst2 = expcum[0:3, 0:2, :].rearrange("p a b -> p (a b)")
            k.cp("dve", cst, bcv[0:3, :], [bcvn], ["tmp4"])
            k.cp("dve", cst2, bcw[0:3, 0:256], [bcwn], ["rs"])
            k.dma("sp", "tmp4", [(convp[:, 0:512], cst)], ["tmp4"], [])
            k.dma("sp", "rs", [(convp[:, 512:768], cst2)], ["rs"], [])

        if with_sample:
            sample_phase(nc, P, k, locals())

        P.finish()
        P.emit(st)
    return nc


def sample_phase(nc, P, k, L):
    g = lambda n: L[n]
    bank, release = g("bank"), g("release")
    xt, xr, xn, ss, sm, pt = g("xt"), g("xr"), g("xn"), g("ss"), g("sm"), g("pt")
    junk = g("scrB")
    ident_f, ident_bf, ones_f, negh = g("ident_f"), g("ident_bf"), g("ones_f"), g("negh")
    w_in_bf, wbrg, wbrs, wout, wa2 = g("w_in_bf"), g("wbrg"), g("wbrs"), g("wout"), g("wa2")
    prm, dtb_rep, A_rep, dsk_rep, fng_rep = g("prm"), g("dtb_rep"), g("A_rep"), g("dsk_rep"), g("fng_rep")
    tg, rs, tmp4, alr, cacc, bpos, eb, enb, xbc_raw, Btok = (g(n) for n in
        ("tg", "rs", "tmp4", "alr", "cacc", "bpos", "eb", "enb", "xbc_raw", "Btok"))
    xsT = g("xsT")[0]
    xs_d, sgla, sssm, sconv, b_a2 = g("xs_d"), g("sgla"), g("sssm"), g("sconv"), g("b_a2")
    ys_o, glas, ssms, convs = g("ys_o"), g("glas"), g("ssms"), g("convs")
    wres = g("wres")
    ng = prm[:, 0:8]
    cb = prm[:, 10:16]
    cw = g("cw")

    def dscr(n, shape):
        return nc.dram_tensor(n, shape, F32, kind="Internal").ap()
    d_q, d_k, d_a = dscr("d_q", [NS, 256]), dscr("d_k", [NS, 256]), dscr("d_a", [NS, 256])
    d_v, d_g, d_z = dscr("d_v", [NS, 512]), dscr("d_g", [NS, 512]), dscr("d_z", [NS, 512])
    d_xs, d_xd = dscr("d_xs", [NS, 512]), dscr("d_xd", [NS, 512])
    d_B, d_C = dscr("d_B", [NS, 128]), dscr("d_C", [NS, 128])
    d_dt, d_dA = dscr("d_dt", [NS, 8]), dscr("d_dA", [NS, 8])

    def dap(t, off, pat):
        return bass.AP(t.tensor, off, pat)

    fx = g("xsT")[1][:, :, :].rearrange("p a b -> p (a b)")
    q1, k1, a1 = fx[:, 0:32], fx[:, 32:64], fx[:, 64:96]
    v1 = fx[:, 96:224]
    g3 = fx[0:64, 224:352]
    x2, xd2, z2, B2, C2 = fx[:, 352:416], fx[:, 416:480], fx[:, 480:544], fx[:, 544:608], fx[:, 608:672]
    dtA2 = fx[:, 672:674]
    xdt2, y2s = fx[:, 674:738], fx[:, 738:802]
    Pm = fx[:, 802:866]
    Gm = fx[:, 866:994]
    Em = g("S_f")[0:32, :, :].rearrange("p a b -> p (a b)")[:, 0:128]
    bq = g("qtT")[0][:, :, :].rearrange("p a b -> p (a b)")
    xnTs = bq[:, 0:128].rearrange("p (k s) -> p k s", k=8)
    ynd = bq[:, 128:256]
    ynT2 = bq[:, 256:384]
    ogTs = bq[:, 384:448].rearrange("p (h s) -> p h s", h=4)
    mTs = g("ktT")[0][:, 0, 0:128].rearrange("p (k s) -> p k s", k=8)

    P.barrier()

    xnTl = g("xnT")
    Sslot = [xt[0][:, :], xt[1][:, :],
             xnTl[0][:, :, :].rearrange("p a b -> p (a b)").bitcast(F32), xnTl[1][:, :, :].rearrange("p a b -> p (a b)").bitcast(F32)]
    Sname = ["xt0", "xt1", "xnT0", "xnT1"]
    for dq in range(4):
        k.dma("sp", Sname[dq], [(Sslot[dq][64 * dhi:64 * dhi + 64, :], sgla[:, dhi, dq * 1024:(dq + 1) * 1024]) for dhi in range(2)],
              [], [Sname[dq]])

    x_s = xbc_raw[0:NS, :, :].rearrange("p a b -> p (a b)")[:, 0:D]
    stg_tiles = [tg[0:NS, :, :].rearrange("p a b -> p (a b)"), rs[0:NS, 0:4, :].rearrange("p a b -> p (a b)")]
    stg_names = ["tg", "rs"]
    ustage = cacc
    u_s = xsT[0:NS, :, :].rearrange("p a b -> p (a b)")[:, 0:768]
    xn_s = xn[0:NS, :]

    k.dma("sp", "xbc_raw", [(x_s, xs_d)], [], ["xbc_raw"])
    k.act(junk[0:NS, :], x_s, AF.Square, ["xbc_raw"], ["junk", "ss"], accum_out=ss[0:NS, 2:3])
    k.ts("pool", sm[0:NS, 2:3], ss[0:NS, 2:3], 1.0 / D, EPS, ALU.mult, ALU.add, ["ss"], ["sm"])
    k.tt("pool", sm[0:NS, 2:3], sm[0:NS, 2:3], negh[0:NS, 0:1], ALU.pow, ["sm", "negh"], ["sm"])
    k.ts("pool", xn_s, x_s, sm[0:NS, 2:3], None, ALU.mult, None, ["xbc_raw", "sm"], ["xn"])
    ptv = pt[:, 0:8 * NS].rearrange("p (kc t) -> p kc t", kc=8)
    k.trg([(ptv[:, kc, :], xn[0:NS, kc * 128:(kc + 1) * 128], ident_bf[0:NS, 0:NS]) for kc in range(8)], ["xn", "ident_bf"], ["pt"])
    k.tt("dve", xnTs[:, :, :], ptv, ng.unsqueeze(2).broadcast_to([128, 8, NS]), ALU.mult, ["pt", "prm"], ["xnTs"])

    si = [0]

    def proj_tm(c0, w):
        b, bn = bank()
        k.mmg([(b[0:NS, 0:w], xnTs[:, kc, :], w_in_bf[:, kc, c0:c0 + w], kc == 0, kc == 7) for kc in range(8)],
              wres(c0, c0 + w) + ["xnTs"], [bn])
        return b, bn

    def proj_to_dram(c0, w, dsts):
        b, bn = proj_tm(c0, w)
        i = si[0] % 2
        si[0] += 1
        k.cp("act", stg_tiles[i][:, 0:w], b[0:NS, 0:w], [bn], [stg_names[i]])
        off = 0
        for (d, dw) in dsts:
            k.dma("sp", d.tensor.name, [(d, stg_tiles[i][:, off:off + dw])], [stg_names[i]], [d.tensor.name])
            off += dw

    proj_to_dram(C_Q, 512, [(d_q, 256), (d_k, 256)])
    proj_to_dram(C_V, 512, [(d_v, 512)])
    proj_to_dram(C_G, 512, [(d_g, 512)])
    proj_to_dram(C_Z, 512, [(d_z, 512)])
    for (c0, w, o) in ((C_X, 512, 0), (C_X + 512, 256, 512)):
        b, bn = proj_tm(c0, w)
        k.cp("act", u_s[:, o:o + w], b[0:NS, 0:w], [bn], ["xsT"])
    k.dma("sp", "xsT", [(convs[:, 2, :], u_s)], ["xsT"], [])
    k.dma("sp", "convs01", [(convs[:, 0:2, :], sconv[:, 1:3, :])], [], [])
    b, bn = proj_tm(C_A, 16)
    k.cp("dve", sm[0:NS, 8:24], b[0:NS, 0:16], [bn], ["sm_a"])
    b, bn = proj_tm(C_DT, 8)
    k.tt("dve", sm[0:NS, 24:32], b[0:NS, 0:8], dtb_rep[0:NS, :], ALU.add, [bn, "dtb_rep"], ["sm_dt"])
    k.act(sm[0:NS, 24:32], sm[0:NS, 24:32], AF.Exp, ["sm_dt"], ["sm_dt"])
    k.act(sm[0:NS, 24:32], sm[0:NS, 24:32], AF.Ln, ["sm_dt"], ["sm_dt"], bias=1.0)
    k.tt("dve", sm[0:NS, 32:40], sm[0:NS, 24:32], A_rep[0:NS, :], ALU.mult, ["sm_dt", "A_rep"], ["sm_dA"])
    k.act(sm[0:NS, 32:40], sm[0:NS, 32:40], AF.Exp, ["sm_dA"], ["sm_dA"])
    k.dma("sp", "d_dt", [(d_dt, sm[0:NS, 24:32])], ["sm_dt"], ["d_dt"])
    k.dma("sp", "d_dA", [(d_dA, sm[0:NS, 32:40])], ["sm_dA"], ["d_dA"])

    b, bn = bank()
    k.trg([(b[0:16, 0:NS], sm[0:NS, 8:24], ident_f[0:NS, 0:NS])], ["sm_a", "ident_f"], [bn])
    k.cp("act", alr[0:16, 0:NS], b[0:16, 0:NS], [bn], ["alr"])
    b, bn = bank()
    k.mmg([(b[0:NS, 0:256], alr[0:16, 0:NS], wa2[0:16, :], True, True)], ["alr", "wa2"], [bn])
    ba2r = cacc[0:NS, 0:256]
    k.dma("sp", "cacc", [(ba2r, bc_rows(b_a2, NS, 256))], [], ["cacc"])
    a_s = bpos[0:NS, 0, :]
    k.tt("dve", a_s, b[0:NS, 0:256], ba2r, ALU.add, [bn, "cacc"], ["bpos"])
    k.act(a_s, a_s, AF.Exp, ["bpos"], ["bpos"], scale=-1.0)
    k.act(a_s, a_s, AF.Ln, ["bpos"], ["bpos"], bias=1.0)
    k.act(a_s, a_s, AF.Exp, ["bpos"], ["bpos"], scale=-1.0 / 16)
    k.dma("sp", "d_a", [(d_a, a_s)], ["bpos"], ["d_a"])

    bufrows = xr[0][0:48, 0:768]
    k.dma("sp", "xr0", [(bufrows, sconv.rearrange("s r c -> (s r) c"))], [], ["xr0"])
    b, bn = bank()
    k.trg([(b[:, cc * 48:(cc + 1) * 48], xr[0][0:48, cc * 128:(cc + 1) * 128], ident_f[0:48, 0:48]) for cc in range(6)],
          ["xr0", "ident_f"], [bn])
    bufT = rs[:, 0:3, :].rearrange("p a b -> p (a b)")[:, 0:288]
    k.cp("dve", bufT, b[:, 0:288], [bn], ["rs"])
    bufT4 = bufT.rearrange("p (c s r) -> p c s r", c=6, s=NS)
    bu, bun = bank()
    buv = bu[:, 0:6 * NS].rearrange("p (c s) -> p c s", c=6)
    for cc in range(6):
        k.mmg([(buv[:, cc, :], w_in_bf[:, kc, C_X + cc * 128:C_X + (cc + 1) * 128], xnTs[:, kc, :], kc == 0, kc == 7) for kc in range(8)],
              wres(C_X + cc * 128, C_X + (cc + 1) * 128) + ["xnTs"], [bun])
    accs = eb[:, 0, 0:6 * NS].rearrange("p (c s) -> p c s", c=6)
    xcT = enb[:, 0, 0:6 * NS].rearrange("p (c s) -> p c s", c=6)
    for cc in range(6):
        k.ts("dve", accs[:, cc, :], bufT4[:, cc, :, 0], cw(0, cc), cb[:, cc:cc + 1], ALU.mult, ALU.add, ["rs", "prm"], ["eb"])
        for w in (1, 2):
            k.stt(accs[:, cc, :], bufT4[:, cc, :, w], cw(w, cc), accs[:, cc, :], ALU.mult, ALU.add, ["rs", "prm", "eb"], ["eb"])
        k.stt(accs[:, cc, :], buv[:, cc, :], cw(3, cc), accs[:, cc, :], ALU.mult, ALU.add, [bun, "prm", "eb"], ["eb"])
    k.act(xcT, accs, AF.Silu, ["eb"], ["enb"])
    bA, bAn = bank()
    k.trg([(bA[0:NS, cc * 128:(cc + 1) * 128], xcT[:, cc, :], ident_f[:, :]) for cc in range(4)], ["enb", "ident_f"], [bAn])
    bB, bBn = bank()
    k.trg([(bB[0:NS, (cc - 4) * 128:(cc - 3) * 128], xcT[:, cc, :], ident_f[:, :]) for cc in (4, 5)], ["enb", "ident_f"], [bBn])
    xcs = xr[1][0:NS, 0:768]
    k.cp("act", xcs[:, 0:512], bA[0:NS, 0:512], [bAn], ["xr1"])
    k.cp("act", xcs[:, 512:768], bB[0:NS, 0:256], [bBn], ["xr1"])
    xd_s = xr[1][0:NS, 768:1024]
    xdfull = tmp4[0:NS, :, :].rearrange("p a b -> p (a b)")
    k.tt("dve", xdfull.rearrange("p (h q) -> p h q", h=8), xcs[:, 0:512].rearrange("p (h q) -> p h q", h=8),
         dsk_rep[0:NS, :].unsqueeze(2).broadcast_to([NS, 8, 64]), ALU.mult, ["xr1", "dsk_rep"], ["tmp4"])
    k.dma("sp", "d_xs", [(d_xs, xcs[:, 0:512])], ["xr1"], ["d_xs"])
    k.dma("sp", "d_B", [(d_B, xcs[:, 512:640])], ["xr1"], ["d_B"])
    k.dma("sp", "d_C", [(d_C, xcs[:, 640:768])], ["xr1"], ["d_C"])
    k.dma("sp", "d_xd", [(d_xd, xdfull)], ["tmp4"], ["d_xd"])

    for dhi in range(2):
        pr = slice(64 * dhi, 64 * dhi + 64)
        k.dma("sp", "q1", [(q1[pr, :], dap(d_q, dhi * 32, [[64, 64], [1, 32]])),
                           (k1[pr, :], dap(d_k, dhi * 32, [[64, 64], [1, 32]])),
                           (a1[pr, :], dap(d_a, dhi * 32, [[64, 64], [1, 32]])),
                           (v1[pr, :], dap(d_v, 0, [[128, 64], [1, 128]]))],
              ["d_q", "d_k", "d_a", "d_v"], ["q1"])
    k.dma("sp", "g3", [(g3[:, :], dap(d_g, 0, [[128, 64], [1, 128]]))], ["d_g"], ["g3"])
    k.memset("dve", Pm[:, :], 0.0, ["Pm"])
    k.cp("dve", Pm[0:64, :], ident_f[0:64, 0:64], ["ident_f", "Pm"], ["Pm"])
    k.cp("dve", Pm[64:128, :], ident_f[64:128, 64:128], ["ident_f", "Pm"], ["Pm"])
    pacc = rs
    for dq in range(4):
        sl_ = dq % 2
        Sn, Tn = Sname[dq], "xr%d" % sl_
        Sf = Sslot[dq]
        S3 = Sf.rearrange("p (d e) -> p d e", d=8)
        T3 = xr[sl_][:, :].rearrange("p (d e) -> p d e", d=8)
        dsl = slice(dq * 8, dq * 8 + 8)
        k.tt("dve", T3, k1[:, dsl].unsqueeze(2).broadcast_to([128, 8, 128]), v1[:, :].unsqueeze(1).broadcast_to([128, 8, 128]),
             ALU.mult, ["q1"], [Tn])
        k.tt("dve", S3, S3, a1[:, dsl].unsqueeze(2).broadcast_to([128, 8, 128]), ALU.mult, [Sn, "q1"], [Sn])
        k.tt("dve", Sf, Sf, xr[sl_][:, :], ALU.add, [Sn, Tn], [Sn])
        k.dma("act", Sn + "_st", [(glas[:, dhi, dq * 1024:(dq + 1) * 1024], Sf[64 * dhi:64 * dhi + 64, :]) for dhi in range(2)], [Sn], [])
        k.tt("dve", T3, S3, q1[:, dsl].unsqueeze(2).broadcast_to([128, 8, 128]), ALU.mult, [Sn, "q1"], [Tn])

        def red(e, T3=T3, dq=dq):
            return e.tensor_reduce(out=pacc[:, dq, :], in_=T3.rearrange("p d e -> p e d"), axis=AX.X, op=ALU.add)
        P.op("dve", red, [Tn], ["rs"])
    po1 = tmp4[:, 0, :]

    def red2(e):
        return e.tensor_reduce(out=po1, in_=pacc[:, :, :].rearrange("p q e -> p e q"), axis=AX.X, op=ALU.add)
    P.op("dve", red2, ["rs"], ["tmp4"])
    b, bn = bank()
    k.mmg([(b[0:64, 0:128], Pm[:, :], po1, True, True)], ["Pm", "tmp4"], [bn])
    o3s = tmp4[0:64, 1, :]
    k.act(o3s, b[0:64, 0:128], AF.Identity, [bn], ["tmp4b"], scale=0.125)
    k.act(junk[0:64, 0:128], o3s, AF.Square, ["tmp4b"], ["junk", "ss"], accum_out=ss[0:64, 3:4])
    k.ts("pool", sm[0:64, 3:4], ss[0:64, 3:4], 1.0 / 128, EPS, ALU.mult, ALU.add, ["ss"], ["sm"])
    k.tt("pool", sm[0:64, 3:4], sm[0:64, 3:4], negh[0:64, 0:1], ALU.pow, ["sm", "negh"], ["sm"])
    k.act(g3[:, :], g3[:, :], AF.Silu, ["g3"], ["g3"])
    k.stt(Btok[0:64, :], o3s, sm[0:64, 3:4], g3[:, :], ALU.mult, ALU.mult, ["tmp4b", "sm", "g3"], ["Btok"])
    k.trg([(pt[:, 0:64], Btok[0:64, :], ident_bf[0:64, 0:64])], ["Btok", "ident_bf"], ["pt"])
    k.cp("dve", ogTs[:, :, :], pt[:, 0:64].rearrange("p (s h) -> p h s", h=4), ["pt"], ["ogTs"])

    for hh in range(4):
        pr = slice(32 * hh, 32 * hh + 32)
        k.dma("sp", "x2", [(x2[pr, :], dap(d_xs, hh * 64, [[256, 32], [1, 64]])),
                           (xd2[pr, :], dap(d_xd, hh * 64, [[256, 32], [1, 64]])),
                           (z2[pr, :], dap(d_z, hh * 64, [[256, 32], [1, 64]])),
                           (B2[pr, :], dap(d_B, 0, [[64, 32], [1, 64]])),
                           (C2[pr, :], dap(d_C, 0, [[64, 32], [1, 64]])),
                           (dtA2[pr, 0:1], dap(d_dt, hh, [[4, 32], [1, 1]])),
                           (dtA2[pr, 1:2], dap(d_dA, hh, [[4, 32], [1, 1]]))],
              ["d_xs", "d_xd", "d_z", "d_B", "d_C", "d_dt", "d_dA"], ["x2"], slow=True)
    k.ts("dve", xdt2[:, :], x2[:, :], dtA2[:, 0:1], None, ALU.mult, None, ["x2"], ["xdt2"])
    for pq in range(4):
        k.dma("sp", Sname[pq], [(Sslot[pq][32 * hh:32 * hh + 32, :], sssm[:, hh, pq * 1024:(pq + 1) * 1024]) for hh in range(4)],
              [], [Sname[pq]])
    for pq in range(4):
        sl_ = pq % 2
        Hn, Tn = Sname[pq], "xr%d" % sl_
        Hf = Sslot[pq]
        H3 = Hf.rearrange("p (q n) -> p q n", q=16)
        T3 = xr[sl_][:, :].rearrange("p (q n) -> p q n", q=16)
        psl = slice(pq * 16, pq * 16 + 16)
        k.tt("dve", T3, xdt2[:, psl].unsqueeze(2).broadcast_to([128, 16, 64]), B2[:, :].unsqueeze(1).broadcast_to([128, 16, 64]),
             ALU.mult, ["xdt2", "x2"], [Tn])
        k.stt(Hf, Hf, dtA2[:, 1:2], xr[sl_][:, :], ALU.mult, ALU.add, [Hn, Tn, "x2"], [Hn])
        k.dma("act", Hn + "_st", [(ssms[:, hh, pq * 1024:(pq + 1) * 1024], Hf[32 * hh:32 * hh + 32, :]) for hh in range(4)], [Hn], [])
        k.tt("dve", T3, H3, C2[:, :].unsqueeze(1).broadcast_to([128, 16, 64]), ALU.mult, [Hn, "x2"], [Tn])

        def redy(e, T3=T3, psl=psl):
            return e.tensor_reduce(out=y2s[:, psl], in_=T3, axis=AX.X, op=ALU.add)
        P.op("dve", redy, [Tn], ["y2s"])
    k.tt("dve", y2s[:, :], y2s[:, :], xd2[:, :], ALU.add, ["y2s", "x2"], ["y2s"])
    k.act(z2[:, :], z2[:, :], AF.Silu, ["x2"], ["z2"])
    k.tt("dve", y2s[:, :], y2s[:, :], z2[:, :], ALU.mult, ["y2s", "z2"], ["y2s"])
    k.memset("dve", ss[:, 4:6], 0.0, ["ss"])
    k.act(junk[:, 0:64], y2s[:, :], AF.Square, ["y2s"], ["junk", "ss"], accum_out=ss[:, 4:5])
    for j in range(4):
        k.cp("dve", Em[:, j * 32:(j + 1) * 32], ident_f[0:32, 0:32], ["ident_f"], ["Em"])
    b, bn = bank()
    k.mmg([(b[:, 0:128], Em[:, :], Em[:, :], True, True)], ["Em"], [bn])
    k.cp("dve", Gm[:, :], b[:, 0:128], [bn], ["Gm"])
    b, bn = bank()
    k.mmg([(b[:, 0:2], Gm[:, :], ss[:, 4:6], True, True)], ["Gm", "ss"], [bn])
    k.cp("dve", ss[:, 6:8], b[:, 0:2], [bn], ["ss2"])
    k.ts("pool", sm[:, 4:5], ss[:, 6:7], 1.0 / 256, EPS, ALU.mult, ALU.add, ["ss2"], ["sm"])
    k.tt("pool", sm[:, 4:5], sm[:, 4:5], negh[:, 0:1], ALU.pow, ["sm", "negh"], ["sm"])
    for j in range(2):
        k.ts("dve", ynd[:, j * 64:(j + 1) * 64], y2s[:, :], sm[:, 4:5], None, ALU.mult, None, ["y2s", "sm"], ["ynd"])
    k.trg([(pt[:, 128:256], ynd[:, :], ident_bf[:, :])], ["ynd", "ident_bf"], ["pt"])
    k.cp("dve", ynT2[:, :], pt[:, 128:256], ["pt"], ["ynT2"])

    mrg_s = xn[0:NS, :]
    for half in range(2):
        hs = slice(half * 512, (half + 1) * 512)
        bA, bAn = bank()
        k.mmg([(bA[0:NS, :], ogTs[:, h, :], wbrg[:, h, hs], h == 0, h == 3) for h in range(4)], ["ogTs", "wbrg"], [bAn])
        bBs = []
        for par in range(2):
            pr = slice(64 * par, 64 * par + 64)
            bB, bBn = bank()
            heads = [(g_, hh) for g_ in range(2) for hh in range(4) if hh % 2 == par]
            mms = []
            for i, (g_, hh) in enumerate(heads):
                c0 = hh * 32 + g_
                mms.append((bB[0:NS, :], ynT2[pr, c0:c0 + 31:2], wbrs[pr, 2 * g_ + hh // 2, hs], i == 0, i == len(heads) - 1))
            k.mmg(mms, ["ynT2", "wbrs"], [bBn])
            bBs.append((bB, bBn))
        gts = []
        for (cg, nm) in ((C_GA, "tg"), (C_GB, "rs")):
            bG, bGn = bank()
            c0 = cg + half * 512
            k.mmg([(bG[0:NS, :], xnTs[:, kc, :], w_in_bf[:, kc, c0:c0 + 512], kc == 0, kc == 7) for kc in range(8)],
                  wres(c0, c0 + 512) + ["xnTs"], [bGn])
            tl = stg_tiles[0] if nm == "tg" else stg_tiles[1]
            k.act(tl, bG[0:NS, :], AF.Tanh, [bGn], [nm], scale=0.5)
            gts.append((tl, nm))
        k.stt(gts[0][0], gts[0][0], 1.0, bA[0:NS, :], ALU.add, ALU.mult, ["tg", bAn], ["tg"])
        yB = tmp4[0:NS, :, :].rearrange("p a b -> p (a b)")
        k.cp("act", yB, bBs[1][0][0:NS, :], [bBs[1][1]], ["tmp4"])
        k.tt("dve", yB, yB, bBs[0][0][0:NS, :], ALU.add, ["tmp4", bBs[0][1]], ["tmp4"])
        k.stt(gts[1][0], gts[1][0], 1.0, yB, ALU.add, ALU.mult, ["rs", "tmp4"], ["rs"])
        k.tt("pool", mrg_s[:, hs], gts[0][0], gts[1][0], ALU.add, ["tg", "rs"], ["xn"])

    ptv = pt[:, 0:8 * NS].rearrange("p (kc t) -> p kc t", kc=8)
    k.trg([(ptv[:, kc, :], xn[0:NS, kc * 128:(kc + 1) * 128], ident_bf[0:NS, 0:NS]) for kc in range(8)], ["xn", "ident_bf"], ["pt"])
    k.cp("dve", mTs[:, :, :], ptv, ["pt"], ["mTs"])
    for half in range(2):
        hs = slice(half * 512, (half + 1) * 512)
        b, bn = bank()
        k.mmg([(b[0:NS, :], mTs[:, kc, :], wout[:, kc, hs], kc == 0, kc == 7) for kc in range(8)], ["mTs", "wout0", "wout1"], [bn])
        k.stt(x_s[:, hs], b[0:NS, :], 0.5, x_s[:, hs], ALU.mult, ALU.add, [bn, "xbc_raw"], ["xbc_raw"])
    k.act(junk[0:NS, :], x_s, AF.Square, ["xbc_raw"], ["junk", "ss"], accum_out=ss[0:NS, 2:3])
    k.ts("pool", sm[0:NS, 2:3], ss[0:NS, 2:3], 1.0 / D, EPS, ALU.mult, ALU.add, ["ss"], ["sm"])
    k.tt("pool", sm[0:NS, 2:3], sm[0:NS, 2:3], negh[0:NS, 0:1], ALU.pow, ["sm", "negh"], ["sm"])
    k.stt(x_s, x_s, sm[0:NS, 2:3], fng_rep[0:NS, :], ALU.mult, ALU.mult, ["xbc_raw", "sm", "fng_rep"], ["xbc_raw"])
    k.dma("sp", "xbc_raw", [(ys_o, x_s)], ["xbc_raw"], [])


_CACHE = {}


def _in_maps(inputs):
    f = lambda a: np.ascontiguousarray(np.asarray(a, dtype=np.float32))
    shared = {
        "norm_g": f(inputs["norm_g"][0]), "w_in": f(inputs["w_in"][0]), "w_a2": f(inputs["w_a2"][0]), "b_a2": f(inputs["b_a2"][0]),
        "gla_norm_g": f(inputs["gla_norm_g"][0]), "conv_w": f(inputs["conv_w"][0]), "conv_b": f(inputs["conv_b"][0]),
        "dt_bias": f(inputs["dt_bias"][0]), "a_log": f(inputs["a_log"][0]), "d_skip": f(inputs["d_skip"][0]),
        "ssd_norm_g": f(inputs["ssd_norm_g"][0]), "w_br_gla": f(inputs["w_br_gla"][0]), "w_br_ssd": f(inputs["w_br_ssd"][0]),
        "w_out": f(inputs["w_out"][0]), "final_norm_g": f(inputs["final_norm_g"]),
    }
    maps = []
    for c in range(8):
        m = dict(shared)
        m["xp"] = f(inputs["x_prompt"][c])
        m["xs"] = f(inputs["x_sample"][16 * c:16 * c + 16, 0])
        m["sgla"] = f(inputs["state_gla"][0, 16 * c:16 * c + 16]).reshape(64, 2, 4096)
        m["sssm"] = f(inputs["state_ssm"][0, 16 * c:16 * c + 16]).reshape(32, 4, 4096)
        m["sconv"] = f(inputs["state_conv"][0, 16 * c:16 * c + 16])
        maps.append(m)
    return maps


def kernel(**inputs):
    if "nc" not in _CACHE:
        _CACHE["nc"] = build_program()
    nc = _CACHE["nc"]
    res = run_bass_kernel_spmd(nc, _in_maps(inputs), core_ids=list(range(8)))
    r = res.results
    y_prompt = np.stack([r[c]["yp"] for c in range(8)], 0)
    y_sample = np.concatenate([r[c]["ys"] for c in range(8)], 0).reshape(128, 1, D)
    gla_p = np.stack([r[c]["glap"].reshape(4, 64, 128) for c in range(8)], 0)[None]
    ssm_p = np.stack([r[c]["ssmp"].reshape(8, 64, 64) for c in range(8)], 0)[None]
    conv_p = np.stack([r[c]["convp"] for c in range(8)], 0)[None]
    gla_s = np.concatenate([r[c]["glas"].reshape(16, 4, 64, 128) for c in range(8)], 0)[None]
    ssm_s = np.concatenate([r[c]["ssms"].reshape(16, 8, 64, 64) for c in range(8)], 0)[None]
    conv_s = np.concatenate([r[c]["convs"] for c in range(8)], 0)[None]
    outs = (y_prompt, y_sample, gla_p, ssm_p, conv_p, gla_s, ssm_s, conv_s)
    return tuple(np.ascontiguousarray(o, dtype=np.float32) for o in outs)
```

```python
import numpy as np
from contextlib import ExitStack
import concourse.bass as bass
import concourse.mybir as mybir
from concourse.bass_utils import run_bass_kernel_spmd

F32 = mybir.dt.float32
BF16 = mybir.dt.bfloat16
I32 = mybir.dt.int32
AF = mybir.ActivationFunctionType
ALU = mybir.AluOpType
AX = mybir.AxisListType

ENGS = ("pe", "act", "dve", "pool", "sp")


class Op:
    __slots__ = ("eng", "fn", "waits", "dma_key", "dma_n", "ev")

    def __init__(self, eng, fn):
        self.eng = eng
        self.fn = fn
        self.waits = []
        self.dma_key = None
        self.dma_n = 0
        self.ev = None


class Prog:
    def __init__(self, nc):
        self.nc = nc
        self.ops = {e: [] for e in ENGS}
        self.res = {}
        self.waited = {e: {} for e in ENGS}
        self.dma_count = {}
        self.all_dma_keys = []
        self.nprog = {e: 0 for e in ENGS}
        import os
        self.limit = int(os.environ.get("MK_MAXOPS", "100000000"))
        self.count = 0
        self.log = []

    def _deps(self, reads, writes):
        deps = []
        for r in reads:
            st = self.res.get(r)
            if st and st[0] is not None:
                deps.append(st[0])
            if st and (r.startswith("pb") or r == "pt"):
                deps.extend(st[1])
        for w in writes:
            st = self.res.get(w)
            if st:
                if st[0] is not None:
                    deps.append(st[0])
                deps.extend(st[1])
        return deps

    def _commit(self, ev, reads, writes):
        for r in reads:
            st = self.res.setdefault(r, [None, []])
            st[1].append(ev)
        for w in writes:
            self.res[w] = [ev, []]

    def _add_waits(self, op, deps):
        eng = op.eng
        best = {}
        for (k, v) in deps:
            if k == eng and eng == "pe":
                continue
            if best.get(k, 0) < v:
                best[k] = v
        for k, v in best.items():
            if self.waited[eng].get(k, 0) >= v:
                continue
            self.waited[eng][k] = v
            op.waits.append((k, v))

    def op(self, eng, fn, reads=(), writes=()):
        self.count += 1
        if self.count > self.limit:
            return None
        self.log.append((self.count, eng, tuple(reads), tuple(writes)))
        o = Op(eng, fn)
        deps = self._deps(reads, writes)
        self._add_waits(o, deps)
        self.ops[eng].append(o)
        self.nprog[eng] += 1
        ev = (eng, self.nprog[eng])
        o.ev = ev
        self._commit(ev, reads, writes)
        return o

    def dma(self, queue, key, fn, reads=(), writes=(), n=1):
        self.count += 1
        if self.count > self.limit:
            return None
        self.log.append((self.count, "dma:" + queue, tuple(reads), tuple(writes)))
        o = Op(queue, fn)
        deps = self._deps(reads, writes)
        self._add_waits(o, deps)
        semkey = "dma:" + key
        if semkey not in self.dma_count:
            self.dma_count[semkey] = 0
            self.all_dma_keys.append(semkey)
        self.dma_count[semkey] += 16 * n
        o.dma_key = semkey
        o.dma_n = n
        self.ops[queue].append(o)
        ev = (semkey, self.dma_count[semkey])
        o.ev = ev
        self._commit(ev, reads, writes)
        return o

    def barrier(self):
        evs = []
        for e in ENGS:
            if self.nprog[e]:
                evs.append((e, self.nprog[e]))
        for k in self.all_dma_keys:
            evs.append((k, self.dma_count[k]))
        for e in ENGS:
            o = Op(e, None)
            self._add_waits(o, [ev for ev in evs if not (ev[0] == e)])
            self.ops[e].append(o)
            self.nprog[e] += 1
            o.ev = (e, self.nprog[e])
        self.res = {}

    def finish(self):
        evs = []
        for e in ENGS:
            if e != "sp" and self.nprog[e]:
                evs.append((e, self.nprog[e]))
        for k in self.all_dma_keys:
            evs.append((k, self.dma_count[k]))
        o = Op("sp", None)
        self._add_waits(o, evs)
        self.ops["sp"].append(o)
        self.nprog["sp"] += 1
        o.ev = ("sp", self.nprog["sp"])

    def emit(self, stack):
        nc = self.nc
        sems = {}
        for e in ENGS:
            sems[e] = stack.enter_context(nc.semaphore("s_" + e))
        for i, k in enumerate(self.all_dma_keys):
            sems[k] = stack.enter_context(nc.semaphore("d%d" % i))
        block = stack.enter_context(nc.Block())
        ops = self.ops

        def replay(ename, eng):
            mysem = sems[ename]
            for o in ops[ename]:
                for (k, v) in o.waits:
                    eng.wait_ge(sems[k], v)
                if o.dma_key is not None:
                    instrs = o.fn(eng)
                    assert len(instrs) == o.dma_n, (len(instrs), o.dma_n)
                    for ins in instrs:
                        ins.then_inc(sems[o.dma_key], 16)
                elif o.fn is None:
                    eng.nop().then_inc(mysem, 1)
                else:
                    ins = o.fn(eng)
                    ins.then_inc(mysem, 1)

        @block.tensor
        def _(eng):
            replay("pe", eng)

        @block.scalar
        def _(eng):
            replay("act", eng)

        @block.vector
        def _(eng):
            replay("dve", eng)

        @block.gpsimd
        def _(eng):
            replay("pool", eng)

        @block.sync
        def _(eng):
            replay("sp", eng)


D = 1024
T = 2048
NS = 16
IN_DIM = 4888
C_Q, C_K, C_V, C_G, C_A, C_Z, C_X, C_DT, C_GA, C_GB = 0, 256, 512, 1024, 1536, 1552, 2064, 2832, 2840, 3864
EPS = 1e-6
SC = 256
NSC = T // SC
CH = 128
NEG = -30000.0


class K:
    def __init__(self, nc, P, st):
        self.nc, self.P, self.st = nc, P, st
        self.bank_i = 0

    def sb(self, name, shape, dt):
        return self.st.enter_context(self.nc.sbuf_tensor(name, shape, dt))

    def ps(self, name, shape, dt):
        return self.st.enter_context(self.nc.psum_tensor(name, shape, dt))

    def mmg(self, mms, reads, writes):
        def fn(e, mms=mms):
            ins = None
            for (o, l, r, s0, s1) in mms:
                ins = e.matmul(o, lhsT=l, rhs=r, start=s0, stop=s1)
            return ins
        return self.P.op("pe", fn, reads, writes)

    def trg(self, trs, reads, writes):
        def fn(e, trs=trs):
            ins = None
            for (o, i, idn) in trs:
                ins = e.transpose(o, i, idn)
            return ins
        return self.P.op("pe", fn, reads, writes)

    def act(self, out, in_, func, reads, writes, **kw):
        def fn(e):
            return e.activation(out=out, in_=in_, func=func, **kw)
        return self.P.op("act", fn, reads, writes)

    def tt(self, eng, out, in0, in1, op, reads, writes):
        def fn(e):
            return e.tensor_tensor(out=out, in0=in0, in1=in1, op=op)
        return self.P.op(eng, fn, reads, writes)

    def ts(self, eng, out, in0, s1, s2, op0, op1, reads, writes):
        def fn(e):
            if op1 is None and eng == "pool" and op0 == ALU.mult:
                return e.tensor_scalar(out=out, in0=in0, scalar1=s1, scalar2=0.0, op0=ALU.mult, op1=ALU.add)
            if op1 is None:
                return e.tensor_scalar(out=out, in0=in0, scalar1=s1, scalar2=None, op0=op0)
            return e.tensor_scalar(out=out, in0=in0, scalar1=s1, scalar2=s2, op0=op0, op1=op1)
        return self.P.op(eng, fn, reads, writes)

    def stt(self, out, in0, scalar, in1, op0, op1, reads, writes):
        def fn(e):
            return e.scalar_tensor_tensor(out=out, in0=in0, scalar=scalar, in1=in1, op0=op0, op1=op1)
        return self.P.op("dve", fn, reads, writes)

    def cp(self, eng, out, in_, reads, writes):
        if eng == "act":
            return self.act(out, in_, AF.Identity, reads, writes)
        def fn(e):
            return e.tensor_copy(out=out, in_=in_)
        return self.P.op(eng, fn, reads, writes)

    def memset(self, eng, ap, val, writes):
        def fn(e):
            return e.memset(ap, val)
        return self.P.op(eng, fn, (), writes)

    def dma(self, q, key, pairs, reads, writes, slow=False):
        def fn(e, pairs=pairs):
            if slow:
                return [e.dma_start(out=o, in_=i, allow_slow_non_contiguous=True) for (o, i) in pairs]
            return [e.dma_start(out=o, in_=i) for (o, i) in pairs]
        return self.P.dma(q, key, fn, reads, writes, n=len(pairs))


def bc_rows(ap1d, nparts, n, off=0):
    return bass.AP(ap1d.tensor, off, [[0, nparts], [1, n]])


def build_program(with_sample=True, stop=None):
    nc = bass.Bass("TRN2", target_bir_lowering=False)
    P = Prog(nc)

    def dr(n, s, kind="ExternalInput", dt=F32):
        return nc.dram_tensor(n, s, dt, kind=kind).ap()

    xp = dr("xp", [T, D]); xs_d = dr("xs", [NS, D])
    sgla = dr("sgla", [64, 2, 4096]); sssm = dr("sssm", [32, 4, 4096]); sconv = dr("sconv", [NS, 3, 768])
    norm_g = dr("norm_g", [D]); w_in = dr("w_in", [D, IN_DIM]); w_a2 = dr("w_a2", [16, 256]); b_a2 = dr("b_a2", [256])
    gla_ng = dr("gla_norm_g", [512]); conv_w = dr("conv_w", [4, 768]); conv_b = dr("conv_b", [768])
    dt_bias = dr("dt_bias", [8]); a_log = dr("a_log", [8]); d_skip = dr("d_skip", [8]); ssd_ng = dr("ssd_norm_g", [512])
    w_brg_d = dr("w_br_gla", [512, D]); w_brs_d = dr("w_br_ssd", [512, D]); w_out_d = dr("w_out", [D, D]); fng = dr("final_norm_g", [D])
    yp = dr("yp", [T, D], "ExternalOutput"); ys_o = dr("ys", [NS, D], "ExternalOutput")
    glap = dr("glap", [256, 128], "ExternalOutput"); ssmp = dr("ssmp", [512, 64], "ExternalOutput")
    convp = dr("convp", [3, 768], "ExternalOutput")
    glas = dr("glas", [64, 2, 4096], "ExternalOutput"); ssms = dr("ssms", [32, 4, 4096], "ExternalOutput")
    convs = dr("convs", [NS, 3, 768], "ExternalOutput")

    with ExitStack() as st:
        k = K(nc, P, st)
        sb, ps = k.sb, k.ps
        w_in_bf = sb("w_in_bf", [128, 8, IN_DIM], BF16)
        wbrg = sb("wbrg", [128, 4, D], BF16)
        wbrs = sb("wbrs", [128, 4, D], BF16)
        wout = sb("wout", [128, 8, D], BF16)
        wa2 = sb("wa2", [16, 256], BF16)
        ident_bf = sb("ident_bf", [128, 128], BF16)
        ident_f = sb("ident_f", [128, 128], F32)
        tri_f = sb("tri_f", [128, 128], F32)
        mask_bf = sb("mask_bf", [128, 128], BF16)
        ones_f = sb("ones_f", [128, 128], F32)
        ones_bf = sb("ones_bf", [128, 128], BF16)
        negh = sb("negh", [128, 1], F32)
        stg = sb("stg", [48, 128], F32)
        prm = sb("prm", [128, 48], F32)
        nba2 = sb("nba2", [128, 2], F32)
        dtb_rep = sb("dtb_rep", [128, 8], F32)
        A_rep = sb("A_rep", [128, 8], F32)
        dsk_rep = sb("dsk_rep", [128, 8], F32)
        dskc = sb("dskc", [128, 4], F32)
        fng_rep = sb("fng_rep", [128, D], F32)
        pt = ps("pt", [128, 1024], BF16)
        banks = [ps("pb%d" % i, [128, 512], F32) for i in range(7)]

        held = set()

        pool_i = {}

        def bank(hold=False, pool=(0, 1, 2, 3, 4, 5, 6)):
            n = len(pool)
            i0 = pool_i.get(pool, 0)
            for t in range(n):
                i = pool[(i0 + t) % n]
                if i not in held:
                    pool_i[pool] = (i0 + t + 1) % n
                    if hold:
                        held.add(i)
                    return banks[i], "pb%d" % i
            raise RuntimeError("no free PSUM bank in pool %r" % (pool,))

        def release(bn):
            held.discard(int(bn[2:]))

        def v4(ap, h):
            return ap.rearrange("p (h i) -> p h i", h=h)

        k.memset("pool", ones_f[:], 1.0, ["ones_f"])
        k.memset("pool", ones_bf[:], 1.0, ["ones_bf"])
        k.memset("pool", negh[:], -0.5, ["negh"])
        P.op("pool", lambda e: e.affine_select(out=tri_f[:], in_=ones_f[:], pattern=[[1, 128]], compare_op=ALU.is_ge,
                                               fill=0.0, base=0, channel_multiplier=-1), ["ones_f"], ["tri_f"])
        P.op("pool", lambda e: e.affine_select(out=ident_f[:], in_=ones_f[:], pattern=[[1, 128]], compare_op=ALU.is_equal,
                                               fill=0.0, base=0, channel_multiplier=-1), ["ones_f"], ["ident_f"])
        k.cp("pool", mask_bf[:], tri_f[:], ["tri_f"], ["mask_bf"])
        k.cp("pool", ident_bf[:], ident_f[:], ["ident_f"], ["ident_bf"])

        lvl = stop[1] if (stop is not None and stop[0] == 0) else 99
        ng = prm[:, 0:8]
        cb = prm[:, 10:16]

        def cw(w, cc):
            return prm[:, 24 + w * 6 + cc: 25 + w * 6 + cc]

        xt = [sb("xt%d" % i, [128, D], F32) for i in range(2)]
        xr = [sb("xr%d" % i, [128, D], F32) for i in range(2)]
        xn = sb("xn", [128, D], BF16)
        scrB = sb("scrB", [128, 1024], BF16)
        ATm = scrB[:, 0:512].rearrange("p (h i) -> p h i", h=4)
        Mg = scrB[:, 512:1024].rearrange("p (h i) -> p h i", h=4)
        ss = sb("ss", [128, 8], F32)
        xnT = [sb("xnT%d" % i, [128, 8, SC], BF16) for i in range(2)]
        alr = sb("alr", [16, SC], BF16)
        bpos = sb("bpos", [128, 2, SC], F32)
        eb = sb("eb", [128, 2, SC], F32)
        enb = sb("enb", [128, 2, SC], F32)
        ebl = [sb("ebl%d" % i, [128, 2, 2], F32) for i in range(2)]
        qtT = [sb("qtT%d" % i, [128, 2, SC], BF16) for i in range(2)]
        ktT = [sb("ktT%d" % i, [128, 2, SC], BF16) for i in range(2)]
        gs = [sb("gs%d" % i, [128, 4, SC], BF16) for i in range(2)]
        zs = [sb("zs%d" % i, [128, 4, SC], BF16) for i in range(2)]
        xbc_raw = sb("xbc_raw", [128, 6, SC + 3], F32)
        cacc = sb("cacc", [128, SC], F32)
        xsT = [sb("xsT%d" % i, [128, 4, SC], F32) for i in range(2)]
        BT = [sb("BT%d" % i, [128, SC], BF16) for i in range(2)]
        CT = [sb("CT%d" % i, [128, SC], BF16) for i in range(2)]
        vtok = sb("vtok", [128, 2, 512], BF16)
        dtt = [sb("dtt%d" % i, [128, 2, 8], F32) for i in range(2)]
        latok = [sb("latok%d" % i, [128, 2, 8], F32) for i in range(2)]
        ktok = sb("ktok", [128, 256], BF16)
        S_f = sb("S_f", [128, 2, 128], F32)
        S_bf = sb("S_bf", [128, 2, 128], BF16)
        hT_f = sb("hT_f", [128, 4, 64], F32)
        hT_bf = sb("hT_bf", [128, 4, 64], BF16)
        tmp4 = sb("tmp4", [128, 4, 128], F32)
        diff = tmp4
        y1 = tmp4
        sq = sb("sq", [128, 4, 128], BF16)
        rs = sb("rs", [128, 4, 128], F32)
        expcum = rs
        ccol = sb("ccol", [128, 8], F32)
        decay = sb("decay", [128, 8, 128], BF16)
        scm = sb("scm", [128, 2, 128], BF16)
        CsT = sb("CsT", [128, 4, 128], BF16)
        xdt = sb("xdt", [128, 8, 64], BF16)
        xw = sb("xw", [128, 8, 64], BF16)
        Btok = sb("Btok", [128, 128], BF16)
        ogT = sb("ogT", [128, 4, SC], BF16)
        ygT = sb("ygT", [128, 4, SC], BF16)
        tg = sb("tg", [128, 2, SC], F32)
        mrgT = sb("mrgT", [128, 8, SC], BF16)
        sm = sb("sm", [128, 64], F32)

        w_in_v = w_in.rearrange("(kc p) c -> p kc c", p=128)
        col_blocks = [(C_A, C_A + 16), (C_DT, C_DT + 8), (C_K, C_K + 256), (C_Q, C_Q + 256), (C_X, C_X + 384), (C_X + 384, C_X + 768),
                      (C_G, C_G + 512), (C_Z, C_Z + 512), (C_V, C_V + 512),
                      (C_GA, C_GA + 512), (C_GA + 512, C_GA + 1024), (C_GB, C_GB + 512), (C_GB + 512, C_GB + 1024)]
        if lvl >= 2:
            k.dma("pool", "wa2", [(wa2[:, :], w_a2)], [], ["wa2"])
        wthr = [0]

        def wdma(key, out_ap, in_ap):
            k.dma("pool", key, [(out_ap, in_ap)], [], [key, "wthr%d" % (wthr[0] % 6)])
            wthr[0] += 1

        N_EARLY = 9
        if lvl >= 2:
            for (c0, c1) in col_blocks[:N_EARLY]:
                wdma("w_in_%d" % c0, w_in_bf[:, :, c0:c1], w_in_v[:, :, c0:c1])

        def wres(c0, c1):
            return ["w_in_%d" % a for (a, b) in col_blocks if a < c1 and b > c0]

        k.memset("dve", S_f[:], 0.0, ["S_f"])
        k.memset("dve", S_bf[:], 0.0, ["S_bf"])
        k.memset("dve", hT_f[:], 0.0, ["hT_f"])
        k.memset("dve", hT_bf[:], 0.0, ["hT_bf"])
        k.memset("dve", xbc_raw[:], 0.0, ["xbc_raw"])

        def load_x(tile_idx):
            slot = tile_idx % 2
            k.dma("sp", "xt%d" % slot, [(xt[slot][:], xp[tile_idx * 128:(tile_idx + 1) * 128, :])], [], ["xt%d" % slot])

        if stop is None or stop[0] > 0:
            load_x(0)
            load_x(1)
        if lvl >= 1:
            k.dma("sp", "stg", [
                (stg[0:8, :], norm_g.rearrange("(a p) -> a p", p=128)),
                (stg[8:10, :], b_a2.rearrange("(a p) -> a p", p=128)),
                (stg[10:16, :], conv_b.rearrange("(a p) -> a p", p=128)),
                (stg[16:20, :], gla_ng.rearrange("(a p) -> a p", p=128)),
                (stg[20:24, :], ssd_ng.rearrange("(a p) -> a p", p=128)),
                (stg[24:48, :], conv_w.rearrange("w (a p) -> (w a) p", p=128)),
            ], [], ["stg"])
            k.dma("sp", "prm2", [
                (dtb_rep[:], bc_rows(dt_bias, 128, 8)),
                (A_rep[:], bc_rows(a_log, 128, 8)),
                (dsk_rep[:], bc_rows(d_skip, 128, 8)),
                (fng_rep[:], bc_rows(fng, 128, D)),
            ], [], ["dtb_rep", "A_rep", "dsk_rep", "fng_rep"])
            b0, b0n = bank()
            k.trg([(b0[:, 0:48], stg[:, :], ident_f[0:48, 0:48])], ["stg", "ident_f"], [b0n])
            k.cp("dve", prm[:], b0[:, 0:48], [b0n], ["prm"])
            k.ts("dve", nba2[:], prm[:, 8:10], -1.0, None, ALU.mult, None, ["prm"], ["nba2"])
            k.act(A_rep[:], A_rep[:], AF.Exp, ["A_rep"], ["A_rep"])
            k.ts("dve", A_rep[:], A_rep[:], -1.0, None, ALU.mult, None, ["A_rep"], ["A_rep"])
            dsk2 = dsk_rep[:, :].rearrange("p (c two) -> p c two", two=2)
            k.cp("dve", dskc[0:64, :], dsk2[0:64, :, 0], ["dsk_rep"], ["dskc"])
            k.cp("dve", dskc[64:128, :], dsk2[64:128, :, 1], ["dsk_rep", "dskc"], ["dskc"])
        def rstd_from_ss(col, n, out_col):
            rn = "ss%d" % col
            k.ts("pool", sm[:, out_col:out_col + 1], ss[:, col:col + 1], 1.0 / n, EPS, ALU.mult, ALU.add, [rn], ["sm%d" % out_col])
            k.tt("pool", sm[:, out_col:out_col + 1], sm[:, out_col:out_col + 1], negh[:, 0:1], ALU.pow, ["sm%d" % out_col, "negh"],
                 ["sm%d" % out_col])

        def staged_weight(dst3, src2d, nkc, scale_cols, const_scale, name):
            for kc in range(nkc):
                slot = kc % 2
                k.dma("sp", "xr%d" % slot, [(xr[slot][:], src2d[kc * 128:(kc + 1) * 128, :])], [], ["xr%d" % slot])
                if scale_cols is not None:
                    k.act(dst3[:, kc, :], xr[slot][:], AF.Identity, ["xr%d" % slot, "prm"], [name], scale=scale_cols[:, kc:kc + 1])
                else:
                    k.act(dst3[:, kc, :], xr[slot][:], AF.Identity, ["xr%d" % slot], [name], scale=const_scale)

        POOL_A = (0, 1, 2)
        POOL_B = (3, 4, 5, 6)

        def stage1a(s):
            pb = s % 2
            xs_ = xnT[pb]
            xsn = "xnT%d" % pb
            sfx = str(pb)
            for ti in range(2):
                tile_idx = 2 * s + ti
                slot = tile_idx % 2
                xtn = "xt%d" % slot
                k.act(xn[:], xt[slot][:], AF.Square, [xtn], ["xn", "ss0"], accum_out=ss[:, 0:1])
                yield
                rstd_from_ss(0, D, 0)
                k.ts("pool", xn[:], xt[slot][:], sm[:, 0:1], None, ALU.mult, None, [xtn, "sm0"], ["xn"])
                yield
                bxt, bxtn = bank(pool=POOL_A)
                ptv = bxt[:, :].bitcast(BF16).rearrange("p (kc t) -> p kc t", kc=8)
                k.trg([(ptv[:, kc, :], xn[:, kc * 128:(kc + 1) * 128], ident_bf[:]) for kc in range(8)], ["xn", "ident_bf"], [bxtn])
                k.tt("dve", xs_[:, :, ti * 128:(ti + 1) * 128], ptv, ng.unsqueeze(2).broadcast_to([128, 8, 128]), ALU.mult,
                     [bxtn, "prm"], [xsn])
                if tile_idx + 2 < 2 * NSC:
                    load_x(tile_idx + 2)
                yield

            def proj_fm(c0, m):
                b, bn = bank(pool=POOL_A)
                k.mmg([(b[0:m, 0:SC], w_in_bf[:, kc, c0:c0 + m], xs_[:, kc, :], kc == 0, kc == 7) for kc in range(8)],
                      wres(c0, c0 + m) + [xsn], [bn])
                return b, bn

            b, bn = proj_fm(C_A, 16)
            k.cp("act", alr[:, :], b[0:16, 0:SC], [bn], ["alr"])
            yield
            b, bn = bank(pool=POOL_A)
            bv = v4(b[:, :], 2)
            k.mmg([(bv[:, cc, :], wa2[:, cc * 128:(cc + 1) * 128], alr[:, :], True, True) for cc in range(2)], ["wa2", "alr"], [bn])
            yield
            for cc in range(2):
                k.act(eb[:, cc, :], bv[:, cc, :], AF.Exp, [bn, "nba2"], ["eb"], scale=-1.0, bias=nba2[:, cc:cc + 1])
            k.act(eb[:, :, :], eb[:, :, :], AF.Ln, ["eb"], ["eb"], bias=1.0)
            yield
            b, bn = bank(pool=POOL_A)
            bdt = b[:, 0:16].rearrange("p (t h) -> p t h", t=2)
            for ti in range(2):
                k.mmg([(bdt[:, ti, :], xs_[:, kc, ti * 128:(ti + 1) * 128], w_in_bf[:, kc, C_DT:C_DT + 8], kc == 0, kc == 7)
                       for kc in range(8)], wres(C_DT, C_DT + 8) + [xsn], [bn])
            dtn, lan = "dtt" + sfx, "latok" + sfx
            k.tt("dve", dtt[pb][:, :, :], bdt, dtb_rep[:, :].unsqueeze(1).broadcast_to([128, 2, 8]), ALU.add, [bn, "dtb_rep"], [dtn])
            yield
            k.act(dtt[pb][:, :, :], dtt[pb][:, :, :], AF.Exp, [dtn], [dtn])
            k.act(dtt[pb][:, :, :], dtt[pb][:, :, :], AF.Ln, [dtn], [dtn], bias=1.0)
            yield
            k.tt("dve", latok[pb][:, :, :], dtt[pb][:, :, :], A_rep[:, :].unsqueeze(1).broadcast_to([128, 2, 8]), ALU.mult,
                 [dtn, "A_rep"], [lan])
            for cc in range(2):
                for ci in range(2):
                    sl = slice(ci * 128, (ci + 1) * 128)

                    def fn(e, cc=cc, sl=sl):
                        return e.tensor_tensor_scan(out=bpos[:, cc, sl], data0=ones_f[:, :], data1=eb[:, cc, sl], initial=0.0,
                                                    op0=ALU.mult, op1=ALU.add)
                    P.op("dve", fn, ["eb", "ones_f"], ["bpos"])
            k.act(eb[:, :, :], bpos[:, :, :], AF.Exp, ["bpos"], ["eb"], scale=-1.0 / 16)
            k.act(enb[:, :, :], bpos[:, :, :], AF.Exp, ["bpos"], ["enb"], scale=1.0 / 16)
            yield
            k.cp("pool", ebl[pb][:, :, :], eb[:, :, 127:SC:128], ["eb"], ["ebl" + sfx])
            yield
            for cc in range(2):
                b, bn = proj_fm(C_K + cc * 128, 128)
                yield
                k.tt("dve", ktT[pb][:, cc, :], b[:, 0:SC], enb[:, cc, :], ALU.mult, [bn, "enb"], ["ktT" + sfx])
                yield
            for cc in range(2):
                b, bn = proj_fm(C_Q + cc * 128, 128)
                yield
                k.stt(qtT[pb][:, cc, :], b[:, 0:SC], 0.125, eb[:, cc, :], ALU.mult, ALU.mult, [bn, "eb"], ["qtT" + sfx])
                yield


        def stage1b(s):
            pb = s % 2
            xs_ = xnT[pb]
            xsn = "xnT%d" % pb
            sfx = str(pb)
            def proj_fm(c0, m):
                b, bn = bank(pool=POOL_A)
                k.mmg([(b[0:m, 0:SC], w_in_bf[:, kc, c0:c0 + m], xs_[:, kc, :], kc == 0, kc == 7) for kc in range(8)],
                      wres(c0, c0 + m) + [xsn], [bn])
                return b, bn

            for cc in range(6):
                b, bn = proj_fm(C_X + cc * 128, 128)
                yield
                k.cp("act", xbc_raw[:, cc, 3:SC + 3], b[:, 0:SC], [bn], ["xbc_raw"])
                yield
                k.ts("dve", cacc[:, :], xbc_raw[:, cc, 0:SC], cw(0, cc), cb[:, cc:cc + 1], ALU.mult, ALU.add, ["xbc_raw", "prm"], ["cacc"])
                for w in range(1, 4):
                    k.stt(cacc[:, :], xbc_raw[:, cc, w:w + SC], cw(w, cc), cacc[:, :], ALU.mult, ALU.add, ["xbc_raw", "prm", "cacc"], ["cacc"])
                yield
                if cc < 4:
                    k.act(xsT[pb][:, cc, :], cacc[:, :], AF.Silu, ["cacc"], ["xsT" + sfx])
                elif cc == 4:
                    k.act(BT[pb][:, :], cacc[:, :], AF.Silu, ["cacc"], ["BT" + sfx])
                else:
                    k.act(CT[pb][:, :], cacc[:, :], AF.Silu, ["cacc"], ["CT" + sfx])
                yield
            k.cp("pool", xbc_raw[:, :, 0:3], xbc_raw[:, :, SC:SC + 3], ["xbc_raw"], ["xbc_raw"])

        def stage1gz(s):
            pb = s % 2
            xs_ = xnT[pb]
            xsn = "xnT%d" % pb
            sfx = str(pb)
            def proj_fm(c0, m):
                b, bn = bank(pool=POOL_A)
                k.mmg([(b[0:m, 0:SC], w_in_bf[:, kc, c0:c0 + m], xs_[:, kc, :], kc == 0, kc == 7) for kc in range(8)],
                      wres(c0, c0 + m) + [xsn], [bn])
                return b, bn

            for cc in range(4):
                b, bn = proj_fm(C_G + cc * 128, 128)
                k.cp("act", gs[pb][:, cc, :], b[:, 0:SC], [bn], ["gs" + sfx])
                yield
            for cc in range(4):
                b, bn = proj_fm(C_Z + cc * 128, 128)
                k.cp("act", zs[pb][:, cc, :], b[:, 0:SC], [bn], ["zs" + sfx])
                yield


        def silu_gz(s):
            pb = s % 2
            sfx = str(pb)
            k.act(gs[pb][:, :, :], gs[pb][:, :, :], AF.Silu, ["gs" + sfx], ["gs" + sfx])
            k.act(zs[pb][:, :, :], zs[pb][:, :, :], AF.Silu, ["zs" + sfx], ["zs" + sfx])

        def stage2(s):
            pb = s % 2
            sfx = str(pb)
            xs_ = xnT[pb]
            xsn = "xnT" + sfx
            qn, kn, gn, zn, xsn_, Bn, Cn, dtn, lan = ("qtT" + sfx, "ktT" + sfx, "gs" + sfx, "zs" + sfx, "xsT" + sfx, "BT" + sfx,
                                                        "CT" + sfx, "dtt" + sfx, "latok" + sfx)
            q_, k_, g_, z_, x_, B_, C_, dt_, la_ = qtT[pb], ktT[pb], gs[pb], zs[pb], xsT[pb], BT[pb], CT[pb], dtt[pb], latok[pb]
            Mgs = [Mg, ATm]
            Mgn = ["Mg", "ATm"]
            t1 = tmp4
            y1_ = tg[:, :, :].rearrange("p a (b c) -> p (a b) c", c=128)
            diffs = [tmp4[:, :, :], y1_]
            diffn = ["tmp4", "tg"]
            def chunk_gen(ci):
                sl = slice(ci * 128, (ci + 1) * 128)
                vn = "vtok%d" % ci
                b, bn = bank(pool=POOL_B)
                k.mmg([(b[:, :], xs_[:, kc, sl], w_in_bf[:, kc, C_V:C_V + 512], kc == 0, kc == 7) for kc in range(8)],
                      wres(C_V, C_V + 512) + [xsn], [bn])
                k.cp("act", vtok[:, ci, :], b[:, :], [bn], [vn])
                ptk = pt[:, 0:256]
                k.trg([(ptk[:, cc * 128:(cc + 1) * 128], k_[:, cc, sl], ident_bf[:]) for cc in range(2)], [kn, "ident_bf"], ["pt"])
                k.cp("act", ktok[:, :], ptk, ["pt"], ["ktok"])
                bc_, bcn = bank(pool=POOL_B)
                k.mmg([(bc_[:, 0:8], tri_f[:, :], la_[:, ci, :], True, True)], ["tri_f", lan], [bcn])
                k.ts("dve", ccol[:, :], bc_[:, 0:8], -1.0, None, ALU.mult, None, [bcn], ["ccol"])
                yield
                for g in range(2):
                    pr = slice(64 * g, 64 * g + 64)
                    bcu, bcun = bank(pool=POOL_B)
                    bcuv = v4(bcu[:, :], 4)
                    k.mmg([(bcuv[:, hh, :], la_[:, ci, 4 * g + hh:4 * g + hh + 1].broadcast_to([128, 128]), tri_f[:, :], True, True)
                           for hh in range(4)], [lan, "tri_f"], [bcun])
                    for hh in range(4):
                        k.ts("dve", diffs[g][:, hh, :], bcuv[:, hh, :], ccol[:, 4 * g + hh:4 * g + hh + 1], 0.0, ALU.add, ALU.min,
                             [bcun, "ccol"], [diffn[g]])
                    k.act(decay[:, 4 * g:4 * g + 4, :], diffs[g], AF.Exp, [diffn[g]], ["decay%d" % g])
                    k.act(expcum[pr, :, :], bcuv[pr, :, :], AF.Exp, [bcun], ["rs%d" % g])
                yield
                bx, bxn = bank(pool=POOL_B)
                k.trg([(bx[:, cc * 128:(cc + 1) * 128], x_[:, cc, sl], ident_f[:]) for cc in range(4)], [xsn_, "ident_f"], [bxn])
                k.tt("dve", xdt[:, :, :], bx[:, :].rearrange("p (h q) -> p h q", h=8),
                     dt_[:, ci, :].unsqueeze(2).broadcast_to([128, 8, 64]), ALU.mult, [bxn, dtn], ["xdt"])
                ptb = pt[:, 256:384]
                k.trg([(ptb, B_[:, sl], ident_bf[:])], [Bn, "ident_bf"], ["pt"])
                k.cp("act", Btok[:, :], ptb, ["pt"], ["Btok"])
                for g in range(2):
                    pr = slice(64 * g, 64 * g + 64)
                    bsc, bscn = bank(pool=POOL_B)
                    k.mmg([(bsc[:, 0:128], B_[pr, sl], C_[pr, sl], True, True)], [Bn, Cn], [bscn])
                    k.tt("dve", scm[:, g, :], bsc[:, 0:128], mask_bf[:, :], ALU.mult, [bscn, "mask_bf"], ["scm%d" % g])
                yield
                ATv = ATm.rearrange("p (cc hh) i -> p hh cc i", hh=2)
                for hh in range(2):
                    pr = slice(64 * hh, 64 * hh + 64)
                    ba, ban = bank(pool=POOL_B)
                    bav = ba[:, 0:256].rearrange("p (c i) -> p c i", c=2)
                    k.mmg([(bav[:, cc, :], k_[pr, cc, sl], q_[pr, cc, sl], True, True) for cc in range(2)], [kn, qn], [ban])
                    k.tt("dve", ATv[:, hh, :, :], bav, mask_bf[:, :].unsqueeze(1).broadcast_to([128, 2, 128]), ALU.mult,
                         [ban, "mask_bf"], ["ATm"])
                yield
                for hh in range(2):
                    pr = slice(64 * hh, 64 * hh + 64)
                    bo, bon = bank(pool=POOL_B)
                    bov = bo[:, 0:256].rearrange("p (c i) -> p c i", c=2)
                    mms = []
                    for cc in range(2):
                        h = 2 * cc + hh
                        mms.append((bov[:, cc, :], vtok[:, ci, h * 128:(h + 1) * 128], ATm[:, h, :], True, False))
                        mms.append((bov[:, cc, :], S_bf[pr, cc, :], q_[pr, cc, sl], False, True))
                    k.mmg(mms, [vn, "ATm", "S_bf", qn], [bon])
                    k.act(sq[:, 2 * hh:2 * hh + 2, :], bov, AF.Square, [bon], ["sq"])
                    for cc in range(2):
                        k.tt("dve", t1[:, 2 * hh + cc, :], g_[:, 2 * cc + hh, sl], bov[:, cc, :], ALU.mult, [bon, gn], ["tmp4"])
                bk, bkn = bank(pool=POOL_B)
                bkv = bk[:, 0:256].rearrange("p (c e) -> p c e", c=2)
                mms = []
                for hh in range(2):
                    for cc in range(2):
                        h = 2 * cc + hh
                        mms.append((bkv[64 * hh:64 * hh + 64, cc, :], ktok[:, cc * 128 + hh * 64:cc * 128 + hh * 64 + 64],
                                    vtok[:, ci, h * 128:(h + 1) * 128], True, True))
                k.mmg(mms, ["ktok", vn], [bkn])
                k.tt("dve", S_f[:, :, :], S_f[:, :, :], bkv, ALU.add, ["S_f", bkn], ["S_f"])
                k.tt("dve", S_f[:, :, :], S_f[:, :, :], ebl[pb][:, :, ci:ci + 1].broadcast_to([128, 2, 128]), ALU.mult,
                     ["S_f", "ebl" + sfx], ["S_f"])
                k.cp("act", S_bf[:, :, :], S_f[:, :, :], ["S_f"], ["S_bf"])
                bsg, bsgn = bank(hold=True, pool=POOL_B)
                k.mmg([(bsg[:, :], ones_bf[:, :], sq[:, :, :].rearrange("p h i -> p (h i)"), True, True)], ["ones_bf", "sq"], [bsgn])
                k.act(bsg[:, :], bsg[:, :], AF.Ln, [bsgn], [bsgn], scale=1.0 / 128, bias=EPS)
                k.act(bsg[:, :], bsg[:, :], AF.Exp, [bsgn], [bsgn], scale=-0.5)
                yield
                k.tt("dve", CsT[:, :, :], expcum[:, :, :], C_[:, sl].unsqueeze(1).broadcast_to([128, 4, 128]), ALU.mult,
                     ["rs0", "rs1", Cn], ["CsT"])
                for g in range(2):
                    k.tt("dve", Mgs[g], decay[:, 4 * g:4 * g + 4, :], scm[:, g, :].unsqueeze(1).broadcast_to([128, 4, 128]), ALU.mult,
                         ["decay%d" % g, "scm%d" % g], [Mgn[g]])
                k.tt("dve", xw[:, :, :], xdt[:, :, :], decay[:, :, 127:128].broadcast_to([128, 8, 64]), ALU.mult,
                     ["xdt", "decay0", "decay1"], ["xw"])
                k.tt("dve", ogT[:, :, sl].rearrange("p (cc hh) i -> p hh cc i", hh=2),
                     t1[:, :, :].rearrange("p (hh cc) i -> p hh cc i", hh=2),
                     v4(bsg[:, :], 4).rearrange("p (hh cc) i -> p hh cc i", hh=2), ALU.mult, ["tmp4", bsgn], ["ogT"])
                release(bsgn)
                yield
                bh, bhn = bank(hold=True, pool=POOL_B)
                bhv = bh[:, 0:256].rearrange("p (h q) -> p h q", h=4)
                for g in range(2):
                    pr = slice(64 * g, 64 * g + 64)
                    by, byn = bank(pool=POOL_B)
                    byv = by[:, 0:256].rearrange("p (c i) -> p c i", c=2)
                    mms = []
                    for hh in range(4):
                        h = 4 * g + hh
                        po = byv[64 * (h % 2):64 * (h % 2) + 64, hh // 2, :]
                        mms.append((po, xdt[:, h, :], Mgs[g][:, hh, :], True, False))
                        mms.append((po, hT_bf[pr, hh, :], CsT[pr, hh, :], False, True))
                    k.mmg(mms, ["xdt", Mgn[g], "hT_bf", "CsT"], [byn])
                    for c2 in range(2):
                        cc = 2 * g + c2
                        k.stt(y1_[:, cc, :], x_[:, cc, sl], dskc[:, cc:cc + 1], byv[:, c2, :], ALU.mult, ALU.add, [xsn_, "dskc", byn], ["tg"])
                    k.mmg([(bh[pr, 0:256], Btok[:, 64 * g:64 * g + 64], xw[:, 4 * g:4 * g + 4, :].rearrange("p h q -> p (h q)"), True, True)],
                          ["Btok", "xw"], [bhn])
                k.tt("dve", hT_f[:, :, :], hT_f[:, :, :], expcum[:, :, 127:128].broadcast_to([128, 4, 64]), ALU.mult,
                     ["hT_f", "rs0", "rs1"], ["hT_f"])
                k.tt("dve", hT_f[:, :, :], hT_f[:, :, :], bhv, ALU.add, ["hT_f", bhn], ["hT_f"])
                k.cp("act", hT_bf[:, :, :], hT_f[:, :, :], ["hT_f"], ["hT_bf"])
                release(bhn)
                yield
                k.tt("dve", y1_, y1_, z_[:, :, sl], ALU.mult, ["tg", zn], ["tg"])
                k.act(sq[:, :, :], y1_, AF.Square, ["tg"], ["sq"])
                bs, bsn = bank(hold=True, pool=POOL_B)
                bsv = bs[:, 0:256].rearrange("p (g i) -> p g i", g=2)
                mms = []
                for g in range(2):
                    mms.append((bsv[:, g, :], ones_bf[:, :], sq[:, 2 * g, :], True, False))
                    mms.append((bsv[:, g, :], ones_bf[:, :], sq[:, 2 * g + 1, :], False, True))
                k.mmg(mms, ["ones_bf", "sq"], [bsn])
                k.act(bsv, bsv, AF.Ln, [bsn], [bsn], scale=1.0 / 256, bias=EPS)
                k.act(bsv, bsv, AF.Exp, [bsn], [bsn], scale=-0.5)
                k.tt("dve", ygT[:, :, sl].rearrange("p (g t) i -> p g t i", g=2), y1_.rearrange("p (g t) i -> p g t i", g=2),
                     bsv.unsqueeze(2).broadcast_to([128, 2, 2, 128]), ALU.mult, ["tg", bsn], ["ygT"])
                release(bsn)
                yield

            for ci in range(2):
                yield from chunk_gen(ci)

        def stage3_m(s):
            pb = s % 2
            xs_ = xnT[pb]
            xsn = "xnT%d" % pb
            for ti in range(2):
                tile_idx = 2 * s + ti
                k.dma("sp", "xr%d" % ti, [(xr[ti][:], xp[tile_idx * 128:(tile_idx + 1) * 128, :])], [], ["xr%d" % ti])
            for m in range(8):
                ms = slice(m * 128, (m + 1) * 128)
                bg, bgn = bank(pool=POOL_B)
                bgv = v4(bg[:, :], 2)
                mms = [(bgv[:, 0, :], w_in_bf[:, kc, C_GA + m * 128:C_GA + (m + 1) * 128], xs_[:, kc, :], kc == 0, kc == 7) for kc in range(8)]
                mms += [(bgv[:, 1, :], w_in_bf[:, kc, C_GB + m * 128:C_GB + (m + 1) * 128], xs_[:, kc, :], kc == 0, kc == 7) for kc in range(8)]
                k.mmg(mms, wres(C_GA + m * 128, C_GA + (m + 1) * 128) + wres(C_GB + m * 128, C_GB + (m + 1) * 128) + [xsn], [bgn])
                bab, babn = bank(pool=POOL_B)
                babv = v4(bab[:, :], 2)
                mms = [(babv[:, 0, :], wbrg[:, cc, ms], ogT[:, cc, :], cc == 0, cc == 3) for cc in range(4)]
                mms += [(babv[:, 1, :], wbrs[:, cc, ms], ygT[:, cc, :], cc == 0, cc == 3) for cc in range(4)]
                k.mmg(mms, ["wbrg", "wbrs", "ogT", "ygT"], [babn])
                k.act(tg[:, :, :], bgv, AF.Tanh, [bgn], ["tg"], scale=0.5)
                k.stt(tg[:, :, :], tg[:, :, :], 1.0, babv, ALU.add, ALU.mult, ["tg", babn], ["tg"])
                k.tt("pool", mrgT[:, m, :], tg[:, 0, :], tg[:, 1, :], ALU.add, ["tg"], ["mrgT"])
                yield
        def stage3_tail(s):
            for ti in range(2):
                tile_idx = 2 * s + ti
                slot = tile_idx % 2
                xrn = "xr%d" % slot
                for half in range(2):
                    b, bn = bank(pool=POOL_B)
                    hs = slice(half * 512, (half + 1) * 512)
                    k.mmg([(b[:, :], mrgT[:, kc, ti * 128:(ti + 1) * 128], wout[:, kc, hs], kc == 0, kc == 7) for kc in range(8)],
                          ["mrgT", "wout0", "wout1"], [bn])
                    k.stt(xr[slot][:, hs], b[:, :], 0.5, xr[slot][:, hs], ALU.mult, ALU.add, [bn, xrn], [xrn])
                yield
            for ti in range(2):
                tile_idx = 2 * s + ti
                slot = tile_idx % 2
                xrn = "xr%d" % slot
                rows = slice(tile_idx * 128, (tile_idx + 1) * 128)
                k.act(mrgT[:, :, ti * 128:(ti + 1) * 128], xr[slot][:].rearrange("p (a b) -> p a b", a=8), AF.Square, [xrn], ["mrgT", "ss1"],
                      accum_out=ss[:, 1:2])
                rstd_from_ss(1, D, 1)
                yield
                k.stt(xr[slot][:], xr[slot][:], sm[:, 1:2], fng_rep[:], ALU.mult, ALU.mult, [xrn, "sm1", "fng_rep"], [xrn])
                k.dma("pool", xrn + "_st", [(yp[rows, :], xr[slot][:])], [xrn], [])
                yield

        def late_weights():
            for (c0, c1) in col_blocks[N_EARLY:]:
                wdma("w_in_%d" % c0, w_in_bf[:, :, c0:c1], w_in_v[:, :, c0:c1])
            wdma("wbrg", wbrg[:, :, :], w_brg_d.rearrange("(kc p) c -> p kc c", p=128))
            wdma("wbrs", wbrs[:, :, :], w_brs_d.rearrange("(kc p) c -> p kc c", p=128))
            wov = w_out_d.rearrange("(kc p) c -> p kc c", p=128)
            wdma("wout0", wout[:, 0:4, :], wov[:, 0:4, :])
            wdma("wout1", wout[:, 4:8, :], wov[:, 4:8, :])

        def run(gen):
            for _ in gen:
                pass

        def chain(*gens):
            for gg in gens:
                yield from gg

        def interleave(gp, gf, rp=2, rf=1):
            dp = df = False
            while not (dp and df):
                for _ in range(rp):
                    if not dp:
                        try:
                            next(gp)
                        except StopIteration:
                            dp = True
                for _ in range(rf):
                    if not df:
                        try:
                            next(gf)
                        except StopIteration:
                            df = True

        nsc = NSC if stop is None else stop[0]
        if nsc > 0 and (stop is None or stop[1] >= 1):
            g1a = stage1a(0)
            for _ in range(6):
                next(g1a)
            interleave(g1a, chain(stage1b(0), stage1gz(0)), 2, 1)
            silu_gz(0)
            late_weights()
            def fold_gains():
                for cc in range(4):
                    k.ts("pool", wbrg[:, cc, :], wbrg[:, cc, :], prm[:, 16 + cc:17 + cc], None, ALU.mult, None, ["wbrg", "prm"], ["wbrg"])
                    k.ts("pool", wbrs[:, cc, :], wbrs[:, cc, :], prm[:, 20 + cc:21 + cc], None, ALU.mult, None, ["wbrs", "prm"], ["wbrs"])

            tail_prev = iter(())
            for s in range(nsc):
                nxt = s + 1 < nsc
                if stop is None or stop[1] >= 2:
                    interleave(stage2(s), chain(tail_prev, chain(stage1a(s + 1), stage1gz(s + 1)) if nxt else iter(())), 1, 3)
                tail_prev = iter(())
                if stop is None or stop[1] >= 3:
                    if s == 0:
                        fold_gains()
                    if nxt:
                        silu_gz(s + 1)
                    interleave(stage3_m(s), stage1b(s + 1) if nxt else iter(()), 1, 3)
                    tail_prev = stage3_tail(s)
            run(tail_prev)

        if lvl >= 3:
            k.dma("sp", "S_f", [(glap.rearrange("(cc p) e -> p cc e", p=128), S_f[:, :, :])], ["S_f"], [])
            bt_, btn = bank()
            btv = bt_[0:64, :].rearrange("p (hh g n) -> p hh g n", hh=4, g=2)
            k.trg([(bt_[0:64, hh * 128:(hh + 1) * 128], hT_f[:, hh, :], ident_f[:, :]) for hh in range(4)], ["hT_f", "ident_f"], [btn])
            hout = tmp4[0:64, :, :].rearrange("p a b -> p (a b)").rearrange("p (h n) -> p h n", h=8)
            k.cp("dve", hout.rearrange("p (g hh) n -> p hh g n", g=2), btv, [btn], ["tmp4"])
            k.dma("sp", "tmp4", [(ssmp.rearrange("(h p) n -> p h n", p=64), hout)], ["tmp4"], [])
        if lvl >= 4:
            bcv, bcvn = bank()
            k.mmg([(bcv[0:4, cc * 128:(cc + 1) * 128], xbc_raw[:, cc, 0:4], ident_f[:], True, True) for cc in range(4)], ["xbc_raw", "ident_f"], [bcvn])
            bcw, bcwn = bank()
            k.mmg([(bcw[0:4, (cc - 4) * 128:(cc - 3) * 128], xbc_raw[:, cc, 0:4], ident_f[:], True, True) for cc in (4, 5)], ["xbc_raw", "ident_f"], [bcwn])
            cst = diff[0:3, :, :].rearrange("p a b -> p (a b)")
            cst2 = expcum[0:3, 0:2, :].rearrange("p a b -> p (a b)")
            k.cp("dve", cst, bcv[0:3, :], [bcvn], ["tmp4"])
            k.cp("dve", cst2, bcw[0:3, 0:256], [bcwn], ["rs"])
            k.dma("sp", "tmp4", [(convp[:, 0:512], cst)], ["tmp4"], [])
            k.dma("sp", "rs", [(convp[:, 512:768], cst2)], ["rs"], [])

        if with_sample:
            sample_phase(nc, P, k, locals())

        P.finish()
        P.emit(st)
    return nc


def sample_phase(nc, P, k, L):
    g = lambda n: L[n]
    bank, release = g("bank"), g("release")
    xt, xr, xn, ss, sm, pt = g("xt"), g("xr"), g("xn"), g("ss"), g("sm"), g("pt")
    junk = g("scrB")
    ident_f, ident_bf, ones_f, negh = g("ident_f"), g("ident_bf"), g("ones_f"), g("negh")
    w_in_bf, wbrg, wbrs, wout, wa2 = g("w_in_bf"), g("wbrg"), g("wbrs"), g("wout"), g("wa2")
    prm, dtb_rep, A_rep, dsk_rep, fng_rep = g("prm"), g("dtb_rep"), g("A_rep"), g("dsk_rep"), g("fng_rep")
    tg, rs, tmp4, alr, cacc, bpos, eb, enb, xbc_raw, Btok = (g(n) for n in
        ("tg", "rs", "tmp4", "alr", "cacc", "bpos", "eb", "enb", "xbc_raw", "Btok"))
    xsT = g("xsT")[0]
    xs_d, sgla, sssm, sconv, b_a2 = g("xs_d"), g("sgla"), g("sssm"), g("sconv"), g("b_a2")
    ys_o, glas, ssms, convs = g("ys_o"), g("glas"), g("ssms"), g("convs")
    wres = g("wres")
    ng = prm[:, 0:8]
    cb = prm[:, 10:16]
    cw = g("cw")

    def dscr(n, shape):
        return nc.dram_tensor(n, shape, F32, kind="Internal").ap()
    d_q, d_k, d_a = dscr("d_q", [NS, 256]), dscr("d_k", [NS, 256]), dscr("d_a", [NS, 256])
    d_v, d_g, d_z = dscr("d_v", [NS, 512]), dscr("d_g", [NS, 512]), dscr("d_z", [NS, 512])
    d_xs, d_xd = dscr("d_xs", [NS, 512]), dscr("d_xd", [NS, 512])
    d_B, d_C = dscr("d_B", [NS, 128]), dscr("d_C", [NS, 128])
    d_dt, d_dA = dscr("d_dt", [NS, 8]), dscr("d_dA", [NS, 8])

    def dap(t, off, pat):
        return bass.AP(t.tensor, off, pat)

    fx = g("xsT")[1][:, :, :].rearrange("p a b -> p (a b)")
    q1, k1, a1 = fx[:, 0:32], fx[:, 32:64], fx[:, 64:96]
    v1 = fx[:, 96:224]
    g3 = fx[0:64, 224:352]
    x2, xd2, z2, B2, C2 = fx[:, 352:416], fx[:, 416:480], fx[:, 480:544], fx[:, 544:608], fx[:, 608:672]
    dtA2 = fx[:, 672:674]
    xdt2, y2s = fx[:, 674:738], fx[:, 738:802]
    Pm = fx[:, 802:866]
    Gm = fx[:, 866:994]
    Em = g("S_f")[0:32, :, :].rearrange("p a b -> p (a b)")[:, 0:128]
    bq = g("qtT")[0][:, :, :].rearrange("p a b -> p (a b)")
    xnTs = bq[:, 0:128].rearrange("p (k s) -> p k s", k=8)
    ynd = bq[:, 128:256]
    ynT2 = bq[:, 256:384]
    ogTs = bq[:, 384:448].rearrange("p (h s) -> p h s", h=4)
    mTs = g("ktT")[0][:, 0, 0:128].rearrange("p (k s) -> p k s", k=8)

    P.barrier()

    xnTl = g("xnT")
    Sslot = [xt[0][:, :], xt[1][:, :],
             xnTl[0][:, :, :].rearrange("p a b -> p (a b)").bitcast(F32), xnTl[1][:, :, :].rearrange("p a b -> p (a b)").bitcast(F32)]
    Sname = ["xt0", "xt1", "xnT0", "xnT1"]
    for dq in range(4):
        k.dma("sp", Sname[dq], [(Sslot[dq][64 * dhi:64 * dhi + 64, :], sgla[:, dhi, dq * 1024:(dq + 1) * 1024]) for dhi in range(2)],
              [], [Sname[dq]])

    x_s = xbc_raw[0:NS, :, :].rearrange("p a b -> p (a b)")[:, 0:D]
    stg_tiles = [tg[0:NS, :, :].rearrange("p a b -> p (a b)"), rs[0:NS, 0:4, :].rearrange("p a b -> p (a b)")]
    stg_names = ["tg", "rs"]
    ustage = cacc
    u_s = xsT[0:NS, :, :].rearrange("p a b -> p (a b)")[:, 0:768]
    xn_s = xn[0:NS, :]

    k.dma("sp", "xbc_raw", [(x_s, xs_d)], [], ["xbc_raw"])
    k.act(junk[0:NS, :], x_s, AF.Square, ["xbc_raw"], ["junk", "ss"], accum_out=ss[0:NS, 2:3])
    k.ts("pool", sm[0:NS, 2:3], ss[0:NS, 2:3], 1.0 / D, EPS, ALU.mult, ALU.add, ["ss"], ["sm"])
    k.tt("pool", sm[0:NS, 2:3], sm[0:NS, 2:3], negh[0:NS, 0:1], ALU.pow, ["sm", "negh"], ["sm"])
    k.ts("pool", xn_s, x_s, sm[0:NS, 2:3], None, ALU.mult, None, ["xbc_raw", "sm"], ["xn"])
    ptv = pt[:, 0:8 * NS].rearrange("p (kc t) -> p kc t", kc=8)
    k.trg([(ptv[:, kc, :], xn[0:NS, kc * 128:(kc + 1) * 128], ident_bf[0:NS, 0:NS]) for kc in range(8)], ["xn", "ident_bf"], ["pt"])
    k.tt("dve", xnTs[:, :, :], ptv, ng.unsqueeze(2).broadcast_to([128, 8, NS]), ALU.mult, ["pt", "prm"], ["xnTs"])

    si = [0]

    def proj_tm(c0, w):
        b, bn = bank()
        k.mmg([(b[0:NS, 0:w], xnTs[:, kc, :], w_in_bf[:, kc, c0:c0 + w], kc == 0, kc == 7) for kc in range(8)],
              wres(c0, c0 + w) + ["xnTs"], [bn])
        return b, bn

    def proj_to_dram(c0, w, dsts):
        b, bn = proj_tm(c0, w)
        i = si[0] % 2
        si[0] += 1
        k.cp("act", stg_tiles[i][:, 0:w], b[0:NS, 0:w], [bn], [stg_names[i]])
        off = 0
        for (d, dw) in dsts:
            k.dma("sp", d.tensor.name, [(d, stg_tiles[i][:, off:off + dw])], [stg_names[i]], [d.tensor.name])
            off += dw

    proj_to_dram(C_Q, 512, [(d_q, 256), (d_k, 256)])
    proj_to_dram(C_V, 512, [(d_v, 512)])
    proj_to_dram(C_G, 512, [(d_g, 512)])
    proj_to_dram(C_Z, 512, [(d_z, 512)])
    for (c0, w, o) in ((C_X, 512, 0), (C_X + 512, 256, 512)):
        b, bn = proj_tm(c0, w)
        k.cp("act", u_s[:, o:o + w], b[0:NS, 0:w], [bn], ["xsT"])
    k.dma("sp", "xsT", [(convs[:, 2, :], u_s)], ["xsT"], [])
    k.dma("sp", "convs01", [(convs[:, 0:2, :], sconv[:, 1:3, :])], [], [])
    b, bn = proj_tm(C_A, 16)
    k.cp("dve", sm[0:NS, 8:24], b[0:NS, 0:16], [bn], ["sm_a"])
    b, bn = proj_tm(C_DT, 8)
    k.tt("dve", sm[0:NS, 24:32], b[0:NS, 0:8], dtb_rep[0:NS, :], ALU.add, [bn, "dtb_rep"], ["sm_dt"])
    k.act(sm[0:NS, 24:32], sm[0:NS, 24:32], AF.Exp, ["sm_dt"], ["sm_dt"])
    k.act(sm[0:NS, 24:32], sm[0:NS, 24:32], AF.Ln, ["sm_dt"], ["sm_dt"], bias=1.0)
    k.tt("dve", sm[0:NS, 32:40], sm[0:NS, 24:32], A_rep[0:NS, :], ALU.mult, ["sm_dt", "A_rep"], ["sm_dA"])
    k.act(sm[0:NS, 32:40], sm[0:NS, 32:40], AF.Exp, ["sm_dA"], ["sm_dA"])
    k.dma("sp", "d_dt", [(d_dt, sm[0:NS, 24:32])], ["sm_dt"], ["d_dt"])
    k.dma("sp", "d_dA", [(d_dA, sm[0:NS, 32:40])], ["sm_dA"], ["d_dA"])

    b, bn = bank()
    k.trg([(b[0:16, 0:NS], sm[0:NS, 8:24], ident_f[0:NS, 0:NS])], ["sm_a", "ident_f"], [bn])
    k.cp("act", alr[0:16, 0:NS], b[0:16, 0:NS], [bn], ["alr"])
    b, bn = bank()
    k.mmg([(b[0:NS, 0:256], alr[0:16, 0:NS], wa2[0:16, :], True, True)], ["alr", "wa2"], [bn])
    ba2r = cacc[0:NS, 0:256]
    k.dma("sp", "cacc", [(ba2r, bc_rows(b_a2, NS, 256))], [], ["cacc"])
    a_s = bpos[0:NS, 0, :]
    k.tt("dve", a_s, b[0:NS, 0:256], ba2r, ALU.add, [bn, "cacc"], ["bpos"])
    k.act(a_s, a_s, AF.Exp, ["bpos"], ["bpos"], scale=-1.0)
    k.act(a_s, a_s, AF.Ln, ["bpos"], ["bpos"], bias=1.0)
    k.act(a_s, a_s, AF.Exp, ["bpos"], ["bpos"], scale=-1.0 / 16)
    k.dma("sp", "d_a", [(d_a, a_s)], ["bpos"], ["d_a"])

    bufrows = xr[0][0:48, 0:768]
    k.dma("sp", "xr0", [(bufrows, sconv.rearrange("s r c -> (s r) c"))], [], ["xr0"])
    b, bn = bank()
    k.trg([(b[:, cc * 48:(cc + 1) * 48], xr[0][0:48, cc * 128:(cc + 1) * 128], ident_f[0:48, 0:48]) for cc in range(6)],
          ["xr0", "ident_f"], [bn])
    bufT = rs[:, 0:3, :].rearrange("p a b -> p (a b)")[:, 0:288]
    k.cp("dve", bufT, b[:, 0:288], [bn], ["rs"])
    bufT4 = bufT.rearrange("p (c s r) -> p c s r", c=6, s=NS)
    bu, bun = bank()
    buv = bu[:, 0:6 * NS].rearrange("p (c s) -> p c s", c=6)
    for cc in range(6):
        k.mmg([(buv[:, cc, :], w_in_bf[:, kc, C_X + cc * 128:C_X + (cc + 1) * 128], xnTs[:, kc, :], kc == 0, kc == 7) for kc in range(8)],
              wres(C_X + cc * 128, C_X + (cc + 1) * 128) + ["xnTs"], [bun])
    accs = eb[:, 0, 0:6 * NS].rearrange("p (c s) -> p c s", c=6)
    xcT = enb[:, 0, 0:6 * NS].rearrange("p (c s) -> p c s", c=6)
    for cc in range(6):
        k.ts("dve", accs[:, cc, :], bufT4[:, cc, :, 0], cw(0, cc), cb[:, cc:cc + 1], ALU.mult, ALU.add, ["rs", "prm"], ["eb"])
        for w in (1, 2):
            k.stt(accs[:, cc, :], bufT4[:, cc, :, w], cw(w, cc), accs[:, cc, :], ALU.mult, ALU.add, ["rs", "prm", "eb"], ["eb"])
        k.stt(accs[:, cc, :], buv[:, cc, :], cw(3, cc), accs[:, cc, :], ALU.mult, ALU.add, [bun, "prm", "eb"], ["eb"])
    k.act(xcT, accs, AF.Silu, ["eb"], ["enb"])
    bA, bAn = bank()
    k.trg([(bA[0:NS, cc * 128:(cc + 1) * 128], xcT[:, cc, :], ident_f[:, :]) for cc in range(4)], ["enb", "ident_f"], [bAn])
    bB, bBn = bank()
    k.trg([(bB[0:NS, (cc - 4) * 128:(cc - 3) * 128], xcT[:, cc, :], ident_f[:, :]) for cc in (4, 5)], ["enb", "ident_f"], [bBn])
    xcs = xr[1][0:NS, 0:768]
    k.cp("act", xcs[:, 0:512], bA[0:NS, 0:512], [bAn], ["xr1"])
    k.cp("act", xcs[:, 512:768], bB[0:NS, 0:256], [bBn], ["xr1"])
    xd_s = xr[1][0:NS, 768:1024]
    xdfull = tmp4[0:NS, :, :].rearrange("p a b -> p (a b)")
    k.tt("dve", xdfull.rearrange("p (h q) -> p h q", h=8), xcs[:, 0:512].rearrange("p (h q) -> p h q", h=8),
         dsk_rep[0:NS, :].unsqueeze(2).broadcast_to([NS, 8, 64]), ALU.mult, ["xr1", "dsk_rep"], ["tmp4"])
    k.dma("sp", "d_xs", [(d_xs, xcs[:, 0:512])], ["xr1"], ["d_xs"])
    k.dma("sp", "d_B", [(d_B, xcs[:, 512:640])], ["xr1"], ["d_B"])
    k.dma("sp", "d_C", [(d_C, xcs[:, 640:768])], ["xr1"], ["d_C"])
    k.dma("sp", "d_xd", [(d_xd, xdfull)], ["tmp4"], ["d_xd"])

    for dhi in range(2):
        pr = slice(64 * dhi, 64 * dhi + 64)
        k.dma("sp", "q1", [(q1[pr, :], dap(d_q, dhi * 32, [[64, 64], [1, 32]])),
                           (k1[pr, :], dap(d_k, dhi * 32, [[64, 64], [1, 32]])),
                           (a1[pr, :], dap(d_a, dhi * 32, [[64, 64], [1, 32]])),
                           (v1[pr, :], dap(d_v, 0, [[128, 64], [1, 128]]))],
              ["d_q", "d_k", "d_a", "d_v"], ["q1"])
    k.dma("sp", "g3", [(g3[:, :], dap(d_g, 0, [[128, 64], [1, 128]]))], ["d_g"], ["g3"])
    k.memset("dve", Pm[:, :], 0.0, ["Pm"])
    k.cp("dve", Pm[0:64, :], ident_f[0:64, 0:64], ["ident_f", "Pm"], ["Pm"])
    k.cp("dve", Pm[64:128, :], ident_f[64:128, 64:128], ["ident_f", "Pm"], ["Pm"])
    pacc = rs
    for dq in range(4):
        sl_ = dq % 2
        Sn, Tn = Sname[dq], "xr%d" % sl_
        Sf = Sslot[dq]
        S3 = Sf.rearrange("p (d e) -> p d e", d=8)
        T3 = xr[sl_][:, :].rearrange("p (d e) -> p d e", d=8)
        dsl = slice(dq * 8, dq * 8 + 8)
        k.tt("dve", T3, k1[:, dsl].unsqueeze(2).broadcast_to([128, 8, 128]), v1[:, :].unsqueeze(1).broadcast_to([128, 8, 128]),
             ALU.mult, ["q1"], [Tn])
        k.tt("dve", S3, S3, a1[:, dsl].unsqueeze(2).broadcast_to([128, 8, 128]), ALU.mult, [Sn, "q1"], [Sn])
        k.tt("dve", Sf, Sf, xr[sl_][:, :], ALU.add, [Sn, Tn], [Sn])
        k.dma("act", Sn + "_st", [(glas[:, dhi, dq * 1024:(dq + 1) * 1024], Sf[64 * dhi:64 * dhi + 64, :]) for dhi in range(2)], [Sn], [])
        k.tt("dve", T3, S3, q1[:, dsl].unsqueeze(2).broadcast_to([128, 8, 128]), ALU.mult, [Sn, "q1"], [Tn])

        def red(e, T3=T3, dq=dq):
            return e.tensor_reduce(out=pacc[:, dq, :], in_=T3.rearrange("p d e -> p e d"), axis=AX.X, op=ALU.add)
        P.op("dve", red, [Tn], ["rs"])
    po1 = tmp4[:, 0, :]

    def red2(e):
        return e.tensor_reduce(out=po1, in_=pacc[:, :, :].rearrange("p q e -> p e q"), axis=AX.X, op=ALU.add)
    P.op("dve", red2, ["rs"], ["tmp4"])
    b, bn = bank()
    k.mmg([(b[0:64, 0:128], Pm[:, :], po1, True, True)], ["Pm", "tmp4"], [bn])
    o3s = tmp4[0:64, 1, :]
    k.act(o3s, b[0:64, 0:128], AF.Identity, [bn], ["tmp4b"], scale=0.125)
    k.act(junk[0:64, 0:128], o3s, AF.Square, ["tmp4b"], ["junk", "ss"], accum_out=ss[0:64, 3:4])
    k.ts("pool", sm[0:64, 3:4], ss[0:64, 3:4], 1.0 / 128, EPS, ALU.mult, ALU.add, ["ss"], ["sm"])
    k.tt("pool", sm[0:64, 3:4], sm[0:64, 3:4], negh[0:64, 0:1], ALU.pow, ["sm", "negh"], ["sm"])
    k.act(g3[:, :], g3[:, :], AF.Silu, ["g3"], ["g3"])
    k.stt(Btok[0:64, :], o3s, sm[0:64, 3:4], g3[:, :], ALU.mult, ALU.mult, ["tmp4b", "sm", "g3"], ["Btok"])
    k.trg([(pt[:, 0:64], Btok[0:64, :], ident_bf[0:64, 0:64])], ["Btok", "ident_bf"], ["pt"])
    k.cp("dve", ogTs[:, :, :], pt[:, 0:64].rearrange("p (s h) -> p h s", h=4), ["pt"], ["ogTs"])

    for hh in range(4):
        pr = slice(32 * hh, 32 * hh + 32)
        k.dma("sp", "x2", [(x2[pr, :], dap(d_xs, hh * 64, [[256, 32], [1, 64]])),
                           (xd2[pr, :], dap(d_xd, hh * 64, [[256, 32], [1, 64]])),
                           (z2[pr, :], dap(d_z, hh * 64, [[256, 32], [1, 64]])),
                           (B2[pr, :], dap(d_B, 0, [[64, 32], [1, 64]])),
                           (C2[pr, :], dap(d_C, 0, [[64, 32], [1, 64]])),
                           (dtA2[pr, 0:1], dap(d_dt, hh, [[4, 32], [1, 1]])),
                           (dtA2[pr, 1:2], dap(d_dA, hh, [[4, 32], [1, 1]]))],
              ["d_xs", "d_xd", "d_z", "d_B", "d_C", "d_dt", "d_dA"], ["x2"], slow=True)
    k.ts("dve", xdt2[:, :], x2[:, :], dtA2[:, 0:1], None, ALU.mult, None, ["x2"], ["xdt2"])
    for pq in range(4):
        k.dma("sp", Sname[pq], [(Sslot[pq][32 * hh:32 * hh + 32, :], sssm[:, hh, pq * 1024:(pq + 1) * 1024]) for hh in range(4)],
              [], [Sname[pq]])
    for pq in range(4):
        sl_ = pq % 2
        Hn, Tn = Sname[pq], "xr%d" % sl_
        Hf = Sslot[pq]
        H3 = Hf.rearrange("p (q n) -> p q n", q=16)
        T3 = xr[sl_][:, :].rearrange("p (q n) -> p q n", q=16)
        psl = slice(pq * 16, pq * 16 + 16)
        k.tt("dve", T3, xdt2[:, psl].unsqueeze(2).broadcast_to([128, 16, 64]), B2[:, :].unsqueeze(1).broadcast_to([128, 16, 64]),
             ALU.mult, ["xdt2", "x2"], [Tn])
        k.stt(Hf, Hf, dtA2[:, 1:2], xr[sl_][:, :], ALU.mult, ALU.add, [Hn, Tn, "x2"], [Hn])
        k.dma("act", Hn + "_st", [(ssms[:, hh, pq * 1024:(pq + 1) * 1024], Hf[32 * hh:32 * hh + 32, :]) for hh in range(4)], [Hn], [])
        k.tt("dve", T3, H3, C2[:, :].unsqueeze(1).broadcast_to([128, 16, 64]), ALU.mult, [Hn, "x2"], [Tn])

        def redy(e, T3=T3, psl=psl):
            return e.tensor_reduce(out=y2s[:, psl], in_=T3, axis=AX.X, op=ALU.add)
        P.op("dve", redy, [Tn], ["y2s"])
    k.tt("dve", y2s[:, :], y2s[:, :], xd2[:, :], ALU.add, ["y2s", "x2"], ["y2s"])
    k.act(z2[:, :], z2[:, :], AF.Silu, ["x2"], ["z2"])
    k.tt("dve", y2s[:, :], y2s[:, :], z2[:, :], ALU.mult, ["y2s", "z2"], ["y2s"])
    k.memset("dve", ss[:, 4:6], 0.0, ["ss"])
    k.act(junk[:, 0:64], y2s[:, :], AF.Square, ["y2s"], ["junk", "ss"], accum_out=ss[:, 4:5])
    for j in range(4):
        k.cp("dve", Em[:, j * 32:(j + 1) * 32], ident_f[0:32, 0:32], ["ident_f"], ["Em"])
    b, bn = bank()
    k.mmg([(b[:, 0:128], Em[:, :], Em[:, :], True, True)], ["Em"], [bn])
    k.cp("dve", Gm[:, :], b[:, 0:128], [bn], ["Gm"])
    b, bn = bank()
    k.mmg([(b[:, 0:2], Gm[:, :], ss[:, 4:6], True, True)], ["Gm", "ss"], [bn])
    k.cp("dve", ss[:, 6:8], b[:, 0:2], [bn], ["ss2"])
    k.ts("pool", sm[:, 4:5], ss[:, 6:7], 1.0 / 256, EPS, ALU.mult, ALU.add, ["ss2"], ["sm"])
    k.tt("pool", sm[:, 4:5], sm[:, 4:5], negh[:, 0:1], ALU.pow, ["sm", "negh"], ["sm"])
    for j in range(2):
        k.ts("dve", ynd[:, j * 64:(j + 1) * 64], y2s[:, :], sm[:, 4:5], None, ALU.mult, None, ["y2s", "sm"], ["ynd"])
    k.trg([(pt[:, 128:256], ynd[:, :], ident_bf[:, :])], ["ynd", "ident_bf"], ["pt"])
    k.cp("dve", ynT2[:, :], pt[:, 128:256], ["pt"], ["ynT2"])

    mrg_s = xn[0:NS, :]
    for half in range(2):
        hs = slice(half * 512, (half + 1) * 512)
        bA, bAn = bank()
        k.mmg([(bA[0:NS, :], ogTs[:, h, :], wbrg[:, h, hs], h == 0, h == 3) for h in range(4)], ["ogTs", "wbrg"], [bAn])
        bBs = []
        for par in range(2):
            pr = slice(64 * par, 64 * par + 64)
            bB, bBn = bank()
            heads = [(g_, hh) for g_ in range(2) for hh in range(4) if hh % 2 == par]
            mms = []
            for i, (g_, hh) in enumerate(heads):
                c0 = hh * 32 + g_
                mms.append((bB[0:NS, :], ynT2[pr, c0:c0 + 31:2], wbrs[pr, 2 * g_ + hh // 2, hs], i == 0, i == len(heads) - 1))
            k.mmg(mms, ["ynT2", "wbrs"], [bBn])
            bBs.append((bB, bBn))
        gts = []
        for (cg, nm) in ((C_GA, "tg"), (C_GB, "rs")):
            bG, bGn = bank()
            c0 = cg + half * 512
            k.mmg([(bG[0:NS, :], xnTs[:, kc, :], w_in_bf[:, kc, c0:c0 + 512], kc == 0, kc == 7) for kc in range(8)],
                  wres(c0, c0 + 512) + ["xnTs"], [bGn])
            tl = stg_tiles[0] if nm == "tg" else stg_tiles[1]
            k.act(tl, bG[0:NS, :], AF.Tanh, [bGn], [nm], scale=0.5)
            gts.append((tl, nm))
        k.stt(gts[0][0], gts[0][0], 1.0, bA[0:NS, :], ALU.add, ALU.mult, ["tg", bAn], ["tg"])
        yB = tmp4[0:NS, :, :].rearrange("p a b -> p (a b)")
        k.cp("act", yB, bBs[1][0][0:NS, :], [bBs[1][1]], ["tmp4"])
        k.tt("dve", yB, yB, bBs[0][0][0:NS, :], ALU.add, ["tmp4", bBs[0][1]], ["tmp4"])
        k.stt(gts[1][0], gts[1][0], 1.0, yB, ALU.add, ALU.mult, ["rs", "tmp4"], ["rs"])
        k.tt("pool", mrg_s[:, hs], gts[0][0], gts[1][0], ALU.add, ["tg", "rs"], ["xn"])

    ptv = pt[:, 0:8 * NS].rearrange("p (kc t) -> p kc t", kc=8)
    k.trg([(ptv[:, kc, :], xn[0:NS, kc * 128:(kc + 1) * 128], ident_bf[0:NS, 0:NS]) for kc in range(8)], ["xn", "ident_bf"], ["pt"])
    k.cp("dve", mTs[:, :, :], ptv, ["pt"], ["mTs"])
    for half in range(2):
        hs = slice(half * 512, (half + 1) * 512)
        b, bn = bank()
        k.mmg([(b[0:NS, :], mTs[:, kc, :], wout[:, kc, hs], kc == 0, kc == 7) for kc in range(8)], ["mTs", "wout0", "wout1"], [bn])
        k.stt(x_s[:, hs], b[0:NS, :], 0.5, x_s[:, hs], ALU.mult, ALU.add, [bn, "xbc_raw"], ["xbc_raw"])
    k.act(junk[0:NS, :], x_s, AF.Square, ["xbc_raw"], ["junk", "ss"], accum_out=ss[0:NS, 2:3])
    k.ts("pool", sm[0:NS, 2:3], ss[0:NS, 2:3], 1.0 / D, EPS, ALU.mult, ALU.add, ["ss"], ["sm"])
    k.tt("pool", sm[0:NS, 2:3], sm[0:NS, 2:3], negh[0:NS, 0:1], ALU.pow, ["sm", "negh"], ["sm"])
    k.stt(x_s, x_s, sm[0:NS, 2:3], fng_rep[0:NS, :], ALU.mult, ALU.mult, ["xbc_raw", "sm", "fng_rep"], ["xbc_raw"])
    k.dma("sp", "xbc_raw", [(ys_o, x_s)], ["xbc_raw"], [])


_CACHE = {}


def _in_maps(inputs):
    f = lambda a: np.ascontiguousarray(np.asarray(a, dtype=np.float32))
    shared = {
        "norm_g": f(inputs["norm_g"][0]), "w_in": f(inputs["w_in"][0]), "w_a2": f(inputs["w_a2"][0]), "b_a2": f(inputs["b_a2"][0]),
        "gla_norm_g": f(inputs["gla_norm_g"][0]), "conv_w": f(inputs["conv_w"][0]), "conv_b": f(inputs["conv_b"][0]),
        "dt_bias": f(inputs["dt_bias"][0]), "a_log": f(inputs["a_log"][0]), "d_skip": f(inputs["d_skip"][0]),
        "ssd_norm_g": f(inputs["ssd_norm_g"][0]), "w_br_gla": f(inputs["w_br_gla"][0]), "w_br_ssd": f(inputs["w_br_ssd"][0]),
        "w_out": f(inputs["w_out"][0]), "final_norm_g": f(inputs["final_norm_g"]),
    }
    maps = []
    for c in range(8):
        m = dict(shared)
        m["xp"] = f(inputs["x_prompt"][c])
        m["xs"] = f(inputs["x_sample"][16 * c:16 * c + 16, 0])
        m["sgla"] = f(inputs["state_gla"][0, 16 * c:16 * c + 16]).reshape(64, 2, 4096)
        m["sssm"] = f(inputs["state_ssm"][0, 16 * c:16 * c + 16]).reshape(32, 4, 4096)
        m["sconv"] = f(inputs["state_conv"][0, 16 * c:16 * c + 16])
        maps.append(m)
    return maps


def kernel(**inputs):
    if "nc" not in _CACHE:
        _CACHE["nc"] = build_program()
    nc = _CACHE["nc"]
    res = run_bass_kernel_spmd(nc, _in_maps(inputs), core_ids=list(range(8)))
    r = res.results
    y_prompt = np.stack([r[c]["yp"] for c in range(8)], 0)
    y_sample = np.concatenate([r[c]["ys"] for c in range(8)], 0).reshape(128, 1, D)
    gla_p = np.stack([r[c]["glap"].reshape(4, 64, 128) for c in range(8)], 0)[None]
    ssm_p = np.stack([r[c]["ssmp"].reshape(8, 64, 64) for c in range(8)], 0)[None]
    conv_p = np.stack([r[c]["convp"] for c in range(8)], 0)[None]
    gla_s = np.concatenate([r[c]["glas"].reshape(16, 4, 64, 128) for c in range(8)], 0)[None]
    ssm_s = np.concatenate([r[c]["ssms"].reshape(16, 8, 64, 64) for c in range(8)], 0)[None]
    conv_s = np.concatenate([r[c]["convs"] for c in range(8)], 0)[None]
    outs = (y_prompt, y_sample, gla_p, ssm_p, conv_p, gla_s, ssm_s, conv_s)
    return tuple(np.ascontiguousarray(o, dtype=np.float32) for o in outs)
```

```python
import numpy as np
from contextlib import ExitStack
import concourse.bass as bass
import concourse.mybir as mybir
from concourse.bass_utils import run_bass_kernel_spmd

F32 = mybir.dt.float32
BF16 = mybir.dt.bfloat16
I32 = mybir.dt.int32
AF = mybir.ActivationFunctionType
ALU = mybir.AluOpType
AX = mybir.AxisListType

ENGS = ("pe", "act", "dve", "pool", "sp")


class Op:
    __slots__ = ("eng", "fn", "waits", "dma_key", "dma_n", "ev")

    def __init__(self, eng, fn):
        self.eng = eng
        self.fn = fn
        self.waits = []
        self.dma_key = None
        self.dma_n = 0
        self.ev = None


class Prog:
    def __init__(self, nc):
        self.nc = nc
        self.ops = {e: [] for e in ENGS}
        self.res = {}
        self.waited = {e: {} for e in ENGS}
        self.dma_count = {}
        self.all_dma_keys = []
        self.nprog = {e: 0 for e in ENGS}
        import os
        self.limit = int(os.environ.get("MK_MAXOPS", "100000000"))
        self.count = 0
        self.log = []

    def _deps(self, reads, writes):
        deps = []
        for r in reads:
            st = self.res.get(r)
            if st and st[0] is not None:
                deps.append(st[0])
            if st and (r.startswith("pb") or r == "pt"):
                deps.extend(st[1])
        for w in writes:
            st = self.res.get(w)
            if st:
                if st[0] is not None:
                    deps.append(st[0])
                deps.extend(st[1])
        return deps

    def _commit(self, ev, reads, writes):
        for r in reads:
            st = self.res.setdefault(r, [None, []])
            st[1].append(ev)
        for w in writes:
            self.res[w] = [ev, []]

    def _add_waits(self, op, deps):
        eng = op.eng
        best = {}
        for (k, v) in deps:
            if k == eng and eng == "pe":
                continue
            if best.get(k, 0) < v:
                best[k] = v
        for k, v in best.items():
            if self.waited[eng].get(k, 0) >= v:
                continue
            self.waited[eng][k] = v
            op.waits.append((k, v))

    def op(self, eng, fn, reads=(), writes=()):
        self.count += 1
        if self.count > self.limit:
            return None
        self.log.append((self.count, eng, tuple(reads), tuple(writes)))
        o = Op(eng, fn)
        deps = self._deps(reads, writes)
        self._add_waits(o, deps)
        self.ops[eng].append(o)
        self.nprog[eng] += 1
        ev = (eng, self.nprog[eng])
        o.ev = ev
        self._commit(ev, reads, writes)
        return o

    def dma(self, queue, key, fn, reads=(), writes=(), n=1):
        self.count += 1
        if self.count > self.limit:
            return None
        self.log.append((self.count, "dma:" + queue, tuple(reads), tuple(writes)))
        o = Op(queue, fn)
        deps = self._deps(reads, writes)
        self._add_waits(o, deps)
        semkey = "dma:" + key
        if semkey not in self.dma_count:
            self.dma_count[semkey] = 0
            self.all_dma_keys.append(semkey)
        self.dma_count[semkey] += 16 * n
        o.dma_key = semkey
        o.dma_n = n
        self.ops[queue].append(o)
        ev = (semkey, self.dma_count[semkey])
        o.ev = ev
        self._commit(ev, reads, writes)
        return o

    def barrier(self):
        evs = []
        for e in ENGS:
            if self.nprog[e]:
                evs.append((e, self.nprog[e]))
        for k in self.all_dma_keys:
            evs.append((k, self.dma_count[k]))
        for e in ENGS:
            o = Op(e, None)
            self._add_waits(o, [ev for ev in evs if not (ev[0] == e)])
            self.ops[e].append(o)
            self.nprog[e] += 1
            o.ev = (e, self.nprog[e])
        self.res = {}

    def finish(self):
        evs = []
        for e in ENGS:
            if e != "sp" and self.nprog[e]:
                evs.append((e, self.nprog[e]))
        for k in self.all_dma_keys:
            evs.append((k, self.dma_count[k]))
        o = Op("sp", None)
        self._add_waits(o, evs)
        self.ops["sp"].append(o)
        self.nprog["sp"] += 1
        o.ev = ("sp", self.nprog["sp"])

    def emit(self, stack):
        nc = self.nc
        sems = {}
        for e in ENGS:
            sems[e] = stack.enter_context(nc.semaphore("s_" + e))
        for i, k in enumerate(self.all_dma_keys):
            sems[k] = stack.enter_context(nc.semaphore("d%d" % i))
        block = stack.enter_context(nc.Block())
        ops = self.ops

        def replay(ename, eng):
            mysem = sems[ename]
            for o in ops[ename]:
                for (k, v) in o.waits:
                    eng.wait_ge(sems[k], v)
                if o.dma_key is not None:
                    instrs = o.fn(eng)
                    assert len(instrs) == o.dma_n, (len(instrs), o.dma_n)
                    for ins in instrs:
                        ins.then_inc(sems[o.dma_key], 16)
                elif o.fn is None:
                    eng.nop().then_inc(mysem, 1)
                else:
                    ins = o.fn(eng)
                    ins.then_inc(mysem, 1)

        @block.tensor
        def _(eng):
            replay("pe", eng)

        @block.scalar
        def _(eng):
            replay("act", eng)

        @block.vector
        def _(eng):
            replay("dve", eng)

        @block.gpsimd
        def _(eng):
            replay("pool", eng)

        @block.sync
        def _(eng):
            replay("sp", eng)


D = 1024
T = 2048
NS = 16
IN_DIM = 4888
C_Q, C_K, C_V, C_G, C_A, C_Z, C_X, C_DT, C_GA, C_GB = 0, 256, 512, 1024, 1536, 1552, 2064, 2832, 2840, 3864
EPS = 1e-6
SC = 256
NSC = T // SC
CH = 128
NEG = -30000.0


class K:
    def __init__(self, nc, P, st):
        self.nc, self.P, self.st = nc, P, st
        self.bank_i = 0

    def sb(self, name, shape, dt):
        return self.st.enter_context(self.nc.sbuf_tensor(name, shape, dt))

    def ps(self, name, shape, dt):
        return self.st.enter_context(self.nc.psum_tensor(name, shape, dt))

    def mmg(self, mms, reads, writes):
        def fn(e, mms=mms):
            ins = None
            for (o, l, r, s0, s1) in mms:
                ins = e.matmul(o, lhsT=l, rhs=r, start=s0, stop=s1)
            return ins
        return self.P.op("pe", fn, reads, writes)

    def trg(self, trs, reads, writes):
        def fn(e, trs=trs):
            ins = None
            for (o, i, idn) in trs:
                ins = e.transpose(o, i, idn)
            return ins
        return self.P.op("pe", fn, reads, writes)

    def act(self, out, in_, func, reads, writes, **kw):
        def fn(e):
            return e.activation(out=out, in_=in_, func=func, **kw)
        return self.P.op("act", fn, reads, writes)

    def tt(self, eng, out, in0, in1, op, reads, writes):
        def fn(e):
            return e.tensor_tensor(out=out, in0=in0, in1=in1, op=op)
        return self.P.op(eng, fn, reads, writes)

    def ts(self, eng, out, in0, s1, s2, op0, op1, reads, writes):
        def fn(e):
            if op1 is None and eng == "pool" and op0 == ALU.mult:
                return e.tensor_scalar(out=out, in0=in0, scalar1=s1, scalar2=0.0, op0=ALU.mult, op1=ALU.add)
            if op1 is None:
                return e.tensor_scalar(out=out, in0=in0, scalar1=s1, scalar2=None, op0=op0)
            return e.tensor_scalar(out=out, in0=in0, scalar1=s1, scalar2=s2, op0=op0, op1=op1)
        return self.P.op(eng, fn, reads, writes)

    def stt(self, out, in0, scalar, in1, op0, op1, reads, writes):
        def fn(e):
            return e.scalar_tensor_tensor(out=out, in0=in0, scalar=scalar, in1=in1, op0=op0, op1=op1)
        return self.P.op("dve", fn, reads, writes)

    def cp(self, eng, out, in_, reads, writes):
        if eng == "act":
            return self.act(out, in_, AF.Identity, reads, writes)
        def fn(e):
            return e.tensor_copy(out=out, in_=in_)
        return self.P.op(eng, fn, reads, writes)

    def memset(self, eng, ap, val, writes):
        def fn(e):
            return e.memset(ap, val)
        return self.P.op(eng, fn, (), writes)

    def dma(self, q, key, pairs, reads, writes, slow=False):
        def fn(e, pairs=pairs):
            if slow:
                return [e.dma_start(out=o, in_=i, allow_slow_non_contiguous=True) for (o, i) in pairs]
            return [e.dma_start(out=o, in_=i) for (o, i) in pairs]
        return self.P.dma(q, key, fn, reads, writes, n=len(pairs))


def bc_rows(ap1d, nparts, n, off=0):
    return bass.AP(ap1d.tensor, off, [[0, nparts], [1, n]])


def build_program(with_sample=True, stop=None):
    nc = bass.Bass("TRN2", target_bir_lowering=False)
    P = Prog(nc)

    def dr(n, s, kind="ExternalInput", dt=F32):
        return nc.dram_tensor(n, s, dt, kind=kind).ap()

    xp = dr("xp", [T, D]); xs_d = dr("xs", [NS, D])
    sgla = dr("sgla", [64, 2, 4096]); sssm = dr("sssm", [32, 4, 4096]); sconv = dr("sconv", [NS, 3, 768])
    norm_g = dr("norm_g", [D]); w_in = dr("w_in", [D, IN_DIM]); w_a2 = dr("w_a2", [16, 256]); b_a2 = dr("b_a2", [256])
    gla_ng = dr("gla_norm_g", [512]); conv_w = dr("conv_w", [4, 768]); conv_b = dr("conv_b", [768])
    dt_bias = dr("dt_bias", [8]); a_log = dr("a_log", [8]); d_skip = dr("d_skip", [8]); ssd_ng = dr("ssd_norm_g", [512])
    w_brg_d = dr("w_br_gla", [512, D]); w_brs_d = dr("w_br_ssd", [512, D]); w_out_d = dr("w_out", [D, D]); fng = dr("final_norm_g", [D])
    yp = dr("yp", [T, D], "ExternalOutput"); ys_o = dr("ys", [NS, D], "ExternalOutput")
    glap = dr("glap", [256, 128], "ExternalOutput"); ssmp = dr("ssmp", [512, 64], "ExternalOutput")
    convp = dr("convp", [3, 768], "ExternalOutput")
    glas = dr("glas", [64, 2, 4096], "ExternalOutput"); ssms = dr("ssms", [32, 4, 4096], "ExternalOutput")
    convs = dr("convs", [NS, 3, 768], "ExternalOutput")

    with ExitStack() as st:
        k = K(nc, P, st)
        sb, ps = k.sb, k.ps
        w_in_bf = sb("w_in_bf", [128, 8, IN_DIM], BF16)
        wbrg = sb("wbrg", [128, 4, D], BF16)
        wbrs = sb("wbrs", [128, 4, D], BF16)
        wout = sb("wout", [128, 8, D], BF16)
        wa2 = sb("wa2", [16, 256], BF16)
        ident_bf = sb("ident_bf", [128, 128], BF16)
        ident_f = sb("ident_f", [128, 128], F32)
        tri_f = sb("tri_f", [128, 128], F32)
        mask_bf = sb("mask_bf", [128, 128], BF16)
        ones_f = sb("ones_f", [128, 128], F32)
        ones_bf = sb("ones_bf", [128, 128], BF16)
        negh = sb("negh", [128, 1], F32)
        stg = sb("stg", [48, 128], F32)
        prm = sb("prm", [128, 48], F32)
        nba2 = sb("nba2", [128, 2], F32)
        dtb_rep = sb("dtb_rep", [128, 8], F32)
        A_rep = sb("A_rep", [128, 8], F32)
        dsk_rep = sb("dsk_rep", [128, 8], F32)
        dskc = sb("dskc", [128, 4], F32)
        fng_rep = sb("fng_rep", [128, D], F32)
        pt = ps("pt", [128, 1024], BF16)
        banks = [ps("pb%d" % i, [128, 512], F32) for i in range(7)]

        held = set()

        pool_i = {}

        def bank(hold=False, pool=(0, 1, 2, 3, 4, 5, 6)):
            n = len(pool)
            i0 = pool_i.get(pool, 0)
            for t in range(n):
                i = pool[(i0 + t) % n]
                if i not in held:
                    pool_i[pool] = (i0 + t + 1) % n
                    if hold:
                        held.add(i)
                    return banks[i], "pb%d" % i
            raise RuntimeError("no free PSUM bank in pool %r" % (pool,))

        def release(bn):
            held.discard(int(bn[2:]))

        def v4(ap, h):
            return ap.rearrange("p (h i) -> p h i", h=h)

        k.memset("pool", ones_f[:], 1.0, ["ones_f"])
        k.memset("pool", ones_bf[:], 1.0, ["ones_bf"])
        k.memset("pool", negh[:], -0.5, ["negh"])
        P.op("pool", lambda e: e.affine_select(out=tri_f[:], in_=ones_f[:], pattern=[[1, 128]], compare_op=ALU.is_ge,
                                               fill=0.0, base=0, channel_multiplier=-1), ["ones_f"], ["tri_f"])
        P.op("pool", lambda e: e.affine_select(out=ident_f[:], in_=ones_f[:], pattern=[[1, 128]], compare_op=ALU.is_equal,
                                               fill=0.0, base=0, channel_multiplier=-1), ["ones_f"], ["ident_f"])
        k.cp("pool", mask_bf[:], tri_f[:], ["tri_f"], ["mask_bf"])
        k.cp("pool", ident_bf[:], ident_f[:], ["ident_f"], ["ident_bf"])

        lvl = stop[1] if (stop is not None and stop[0] == 0) else 99
        ng = prm[:, 0:8]
        cb = prm[:, 10:16]

        def cw(w, cc):
            return prm[:, 24 + w * 6 + cc: 25 + w * 6 + cc]

        xt = [sb("xt%d" % i, [128, D], F32) for i in range(2)]
        xr = [sb("xr%d" % i, [128, D], F32) for i in range(2)]
        xn = sb("xn", [128, D], BF16)
        scrB = sb("scrB", [128, 1024], BF16)
        ATm = scrB[:, 0:512].rearrange("p (h i) -> p h i", h=4)
        Mg = scrB[:, 512:1024].rearrange("p (h i) -> p h i", h=4)
        ss = sb("ss", [128, 8], F32)
        xnT = [sb("xnT%d" % i, [128, 8, SC], BF16) for i in range(2)]
        alr = sb("alr", [16, SC], BF16)
        bpos = sb("bpos", [128, 2, SC], F32)
        eb = sb("eb", [128, 2, SC], F32)
        enb = sb("enb", [128, 2, SC], F32)
        ebl = [sb("ebl%d" % i, [128, 2, 2], F32) for i in range(2)]
        qtT = [sb("qtT%d" % i, [128, 2, SC], BF16) for i in range(2)]
        ktT = [sb("ktT%d" % i, [128, 2, SC], BF16) for i in range(2)]
        gs = [sb("gs%d" % i, [128, 4, SC], BF16) for i in range(2)]
        zs = [sb("zs%d" % i, [128, 4, SC], BF16) for i in range(2)]
        xbc_raw = sb("xbc_raw", [128, 6, SC + 3], F32)
        cacc = sb("cacc", [128, SC], F32)
        xsT = [sb("xsT%d" % i, [128, 4, SC], F32) for i in range(2)]
        BT = [sb("BT%d" % i, [128, SC], BF16) for i in range(2)]
        CT = [sb("CT%d" % i, [128, SC], BF16) for i in range(2)]
        vtok = sb("vtok", [128, 2, 512], BF16)
        dtt = [sb("dtt%d" % i, [128, 2, 8], F32) for i in range(2)]
        latok = [sb("latok%d" % i, [128, 2, 8], F32) for i in range(2)]
        ktok = sb("ktok", [128, 256], BF16)
        S_f = sb("S_f", [128, 2, 128], F32)
        S_bf = sb("S_bf", [128, 2, 128], BF16)
        hT_f = sb("hT_f", [128, 4, 64], F32)
        hT_bf = sb("hT_bf", [128, 4, 64], BF16)
        tmp4 = sb("tmp4", [128, 4, 128], F32)
        diff = tmp4
        y1 = tmp4
        sq = sb("sq", [128, 4, 128], BF16)
        rs = sb("rs", [128, 4, 128], F32)
        expcum = rs
        ccol = sb("ccol", [128, 8], F32)
        decay = sb("decay", [128, 8, 128], BF16)
        scm = sb("scm", [128, 2, 128], BF16)
        CsT = sb("CsT", [128, 4, 128], BF16)
        xdt = sb("xdt", [128, 8, 64], BF16)
        xw = sb("xw", [128, 8, 64], BF16)
        Btok = sb("Btok", [128, 128], BF16)
        ogT = sb("ogT", [128, 4, SC], BF16)
        ygT = sb("ygT", [128, 4, SC], BF16)
        tg = sb("tg", [128, 2, SC], F32)
        mrgT = sb("mrgT", [128, 8, SC], BF16)
        sm = sb("sm", [128, 64], F32)

        w_in_v = w_in.rearrange("(kc p) c -> p kc c", p=128)
        col_blocks = [(C_A, C_A + 16), (C_DT, C_DT + 8), (C_K, C_K + 256), (C_Q, C_Q + 256), (C_X, C_X + 384), (C_X + 384, C_X + 768),
                      (C_G, C_G + 512), (C_Z, C_Z + 512), (C_V, C_V + 512),
                      (C_GA, C_GA + 512), (C_GA + 512, C_GA + 1024), (C_GB, C_GB + 512), (C_GB + 512, C_GB + 1024)]
        if lvl >= 2:
            k.dma("pool", "wa2", [(wa2[:, :], w_a2)], [], ["wa2"])
        wthr = [0]

        def wdma(key, out_ap, in_ap):
            k.dma("pool", key, [(out_ap, in_ap)], [], [key, "wthr%d" % (wthr[0] % 6)])
            wthr[0] += 1

        N_EARLY = 9
        if lvl >= 2:
            for (c0, c1) in col_blocks[:N_EARLY]:
                wdma("w_in_%d" % c0, w_in_bf[:, :, c0:c1], w_in_v[:, :, c0:c1])

        def wres(c0, c1):
            return ["w_in_%d" % a for (a, b) in col_blocks if a < c1 and b > c0]

        k.memset("dve", S_f[:], 0.0, ["S_f"])
        k.memset("dve", S_bf[:], 0.0, ["S_bf"])
        k.memset("dve", hT_f[:], 0.0, ["hT_f"])
        k.memset("dve", hT_bf[:], 0.0, ["hT_bf"])
        k.memset("dve", xbc_raw[:], 0.0, ["xbc_raw"])

        def load_x(tile_idx):
            slot = tile_idx % 2
            k.dma("sp", "xt%d" % slot, [(xt[slot][:], xp[tile_idx * 128:(tile_idx + 1) * 128, :])], [], ["xt%d" % slot])

        if stop is None or stop[0] > 0:
            load_x(0)
            load_x(1)
        if lvl >= 1:
            k.dma("sp", "stg", [
                (stg[0:8, :], norm_g.rearrange("(a p) -> a p", p=128)),
                (stg[8:10, :], b_a2.rearrange("(a p) -> a p", p=128)),
                (stg[10:16, :], conv_b.rearrange("(a p) -> a p", p=128)),
                (stg[16:20, :], gla_ng.rearrange("(a p) -> a p", p=128)),
                (stg[20:24, :], ssd_ng.rearrange("(a p) -> a p", p=128)),
                (stg[24:48, :], conv_w.rearrange("w (a p) -> (w a) p", p=128)),
            ], [], ["stg"])
            k.dma("sp", "prm2", [
                (dtb_rep[:], bc_rows(dt_bias, 128, 8)),
                (A_rep[:], bc_rows(a_log, 128, 8)),
                (dsk_rep[:], bc_rows(d_skip, 128, 8)),
                (fng_rep[:], bc_rows(fng, 128, D)),
            ], [], ["dtb_rep", "A_rep", "dsk_rep", "fng_rep"])
            b0, b0n = bank()
            k.trg([(b0[:, 0:48], stg[:, :], ident_f[0:48, 0:48])], ["stg", "ident_f"], [b0n])
            k.cp("dve", prm[:], b0[:, 0:48], [b0n], ["prm"])
            k.ts("dve", nba2[:], prm[:, 8:10], -1.0, None, ALU.mult, None, ["prm"], ["nba2"])
            k.act(A_rep[:], A_rep[:], AF.Exp, ["A_rep"], ["A_rep"])
            k.ts("dve", A_rep[:], A_rep[:], -1.0, None, ALU.mult, None, ["A_rep"], ["A_rep"])
            dsk2 = dsk_rep[:, :].rearrange("p (c two) -> p c two", two=2)
            k.cp("dve", dskc[0:64, :], dsk2[0:64, :, 0], ["dsk_rep"], ["dskc"])
            k.cp("dve", dskc[64:128, :], dsk2[64:128, :, 1], ["dsk_rep", "dskc"], ["dskc"])
        def rstd_from_ss(col, n, out_col):
            rn = "ss%d" % col
            k.ts("pool", sm[:, out_col:out_col + 1], ss[:, col:col + 1], 1.0 / n, EPS, ALU.mult, ALU.add, [rn], ["sm%d" % out_col])
            k.tt("pool", sm[:, out_col:out_col + 1], sm[:, out_col:out_col + 1], negh[:, 0:1], ALU.pow, ["sm%d" % out_col, "negh"],
                 ["sm%d" % out_col])

        def staged_weight(dst3, src2d, nkc, scale_cols, const_scale, name):
            for kc in range(nkc):
                slot = kc % 2
                k.dma("sp", "xr%d" % slot, [(xr[slot][:], src2d[kc * 128:(kc + 1) * 128, :])], [], ["xr%d" % slot])
                if scale_cols is not None:
                    k.act(dst3[:, kc, :], xr[slot][:], AF.Identity, ["xr%d" % slot, "prm"], [name], scale=scale_cols[:, kc:kc + 1])
                else:
                    k.act(dst3[:, kc, :], xr[slot][:], AF.Identity, ["xr%d" % slot], [name], scale=const_scale)

        POOL_A = (0, 1, 2)
        POOL_B = (3, 4, 5, 6)

        def stage1a(s):
            pb = s % 2
            xs_ = xnT[pb]
            xsn = "xnT%d" % pb
            sfx = str(pb)
            for ti in range(2):
                tile_idx = 2 * s + ti
                slot = tile_idx % 2
                xtn = "xt%d" % slot
                k.act(xn[:], xt[slot][:], AF.Square, [xtn], ["xn", "ss0"], accum_out=ss[:, 0:1])
                yield
                rstd_from_ss(0, D, 0)
                k.ts("pool", xn[:], xt[slot][:], sm[:, 0:1], None, ALU.mult, None, [xtn, "sm0"], ["xn"])
                yield
                bxt, bxtn = bank(pool=POOL_A)
                ptv = bxt[:, :].bitcast(BF16).rearrange("p (kc t) -> p kc t", kc=8)
                k.trg([(ptv[:, kc, :], xn[:, kc * 128:(kc + 1) * 128], ident_bf[:]) for kc in range(8)], ["xn", "ident_bf"], [bxtn])
                k.tt("dve", xs_[:, :, ti * 128:(ti + 1) * 128], ptv, ng.unsqueeze(2).broadcast_to([128, 8, 128]), ALU.mult,
                     [bxtn, "prm"], [xsn])
                if tile_idx + 2 < 2 * NSC:
                    load_x(tile_idx + 2)
                yield

            def proj_fm(c0, m):
                b, bn = bank(pool=POOL_A)
                k.mmg([(b[0:m, 0:SC], w_in_bf[:, kc, c0:c0 + m], xs_[:, kc, :], kc == 0, kc == 7) for kc in range(8)],
                      wres(c0, c0 + m) + [xsn], [bn])
                return b, bn

            b, bn = proj_fm(C_A, 16)
            k.cp("act", alr[:, :], b[0:16, 0:SC], [bn], ["alr"])
            yield
            b, bn = bank(pool=POOL_A)
            bv = v4(b[:, :], 2)
            k.mmg([(bv[:, cc, :], wa2[:, cc * 128:(cc + 1) * 128], alr[:, :], True, True) for cc in range(2)], ["wa2", "alr"], [bn])
            yield
            for cc in range(2):
                k.act(eb[:, cc, :], bv[:, cc, :], AF.Exp, [bn, "nba2"], ["eb"], scale=-1.0, bias=nba2[:, cc:cc + 1])
            k.act(eb[:, :, :], eb[:, :, :], AF.Ln, ["eb"], ["eb"], bias=1.0)
            yield
            b, bn = bank(pool=POOL_A)
            bdt = b[:, 0:16].rearrange("p (t h) -> p t h", t=2)
            for ti in range(2):
                k.mmg([(bdt[:, ti, :], xs_[:, kc, ti * 128:(ti + 1) * 128], w_in_bf[:, kc, C_DT:C_DT + 8], kc == 0, kc == 7)
                       for kc in range(8)], wres(C_DT, C_DT + 8) + [xsn], [bn])
            dtn, lan = "dtt" + sfx, "latok" + sfx
            k.tt("dve", dtt[pb][:, :, :], bdt, dtb_rep[:, :].unsqueeze(1).broadcast_to([128, 2, 8]), ALU.add, [bn, "dtb_rep"], [dtn])
            yield
            k.act(dtt[pb][:, :, :], dtt[pb][:, :, :], AF.Exp, [dtn], [dtn])
            k.act(dtt[pb][:, :, :], dtt[pb][:, :, :], AF.Ln, [dtn], [dtn], bias=1.0)
            yield
            k.tt("dve", latok[pb][:, :, :], dtt[pb][:, :, :], A_rep[:, :].unsqueeze(1).broadcast_to([128, 2, 8]), ALU.mult,
                 [dtn, "A_rep"], [lan])
            for cc in range(2):
                for ci in range(2):
                    sl = slice(ci * 128, (ci + 1) * 128)

                    def fn(e, cc=cc, sl=sl):
                        return e.tensor_tensor_scan(out=bpos[:, cc, sl], data0=ones_f[:, :], data1=eb[:, cc, sl], initial=0.0,
                                                    op0=ALU.mult, op1=ALU.add)
                    P.op("dve", fn, ["eb", "ones_f"], ["bpos"])
            k.act(eb[:, :, :], bpos[:, :, :], AF.Exp, ["bpos"], ["eb"], scale=-1.0 / 16)
            k.act(enb[:, :, :], bpos[:, :, :], AF.Exp, ["bpos"], ["enb"], scale=1.0 / 16)
            yield
            k.cp("pool", ebl[pb][:, :, :], eb[:, :, 127:SC:128], ["eb"], ["ebl" + sfx])
            yield
            for cc in range(2):
                b, bn = proj_fm(C_K + cc * 128, 128)
                yield
                k.tt("dve", ktT[pb][:, cc, :], b[:, 0:SC], enb[:, cc, :], ALU.mult, [bn, "enb"], ["ktT" + sfx])
                yield
            for cc in range(2):
                b, bn = proj_fm(C_Q + cc * 128, 128)
                yield
                k.stt(qtT[pb][:, cc, :], b[:, 0:SC], 0.125, eb[:, cc, :], ALU.mult, ALU.mult, [bn, "eb"], ["qtT" + sfx])
                yield


        def stage1b(s):
            pb = s % 2
            xs_ = xnT[pb]
            xsn = "xnT%d" % pb
            sfx = str(pb)
            def proj_fm(c0, m):
                b, bn = bank(pool=POOL_A)
                k.mmg([(b[0:m, 0:SC], w_in_bf[:, kc, c0:c0 + m], xs_[:, kc, :], kc == 0, kc == 7) for kc in range(8)],
                      wres(c0, c0 + m) + [xsn], [bn])
                return b, bn

            for cc in range(6):
                b, bn = proj_fm(C_X + cc * 128, 128)
                yield
                k.cp("act", xbc_raw[:, cc, 3:SC + 3], b[:, 0:SC], [bn], ["xbc_raw"])
                yield
                k.ts("dve", cacc[:, :], xbc_raw[:, cc, 0:SC], cw(0, cc), cb[:, cc:cc + 1], ALU.mult, ALU.add, ["xbc_raw", "prm"], ["cacc"])
                for w in range(1, 4):
                    k.stt(cacc[:, :], xbc_raw[:, cc, w:w + SC], cw(w, cc), cacc[:, :], ALU.mult, ALU.add, ["xbc_raw", "prm", "cacc"], ["cacc"])
                yield
                if cc < 4:
                    k.act(xsT[pb][:, cc, :], cacc[:, :], AF.Silu, ["cacc"], ["xsT" + sfx])
                elif cc == 4:
                    k.act(BT[pb][:, :], cacc[:, :], AF.Silu, ["cacc"], ["BT" + sfx])
                else:
                    k.act(CT[pb][:, :], cacc[:, :], AF.Silu, ["cacc"], ["CT" + sfx])
                yield
            k.cp("pool", xbc_raw[:, :, 0:3], xbc_raw[:, :, SC:SC + 3], ["xbc_raw"], ["xbc_raw"])

        def stage1gz(s):
            pb = s % 2
            xs_ = xnT[pb]
            xsn = "xnT%d" % pb
            sfx = str(pb)
            def proj_fm(c0, m):
                b, bn = bank(pool=POOL_A)
                k.mmg([(b[0:m, 0:SC], w_in_bf[:, kc, c0:c0 + m], xs_[:, kc, :], kc == 0, kc == 7) for kc in range(8)],
                      wres(c0, c0 + m) + [xsn], [bn])
                return b, bn

            for cc in range(4):
                b, bn = proj_fm(C_G + cc * 128, 128)
                yield
                k.cp("act", gs[pb][:, cc, :], b[:, 0:SC], [bn], ["gs" + sfx])
                yield
            for cc in range(4):
                b, bn = proj_fm(C_Z + cc * 128, 128)
                yield
                k.cp("act", zs[pb][:, cc, :], b[:, 0:SC], [bn], ["zs" + sfx])
                yield


        def silu_gz(s):
            pb = s % 2
            sfx = str(pb)
            k.act(gs[pb][:, :, :], gs[pb][:, :, :], AF.Silu, ["gs" + sfx], ["gs" + sfx])
            k.act(zs[pb][:, :, :], zs[pb][:, :, :], AF.Silu, ["zs" + sfx], ["zs" + sfx])

        def stage2(s):
            pb = s % 2
            sfx = str(pb)
            xs_ = xnT[pb]
            xsn = "xnT" + sfx
            qn, kn, gn, zn, xsn_, Bn, Cn, dtn, lan = ("qtT" + sfx, "ktT" + sfx, "gs" + sfx, "zs" + sfx, "xsT" + sfx, "BT" + sfx,
                                                        "CT" + sfx, "dtt" + sfx, "latok" + sfx)
            q_, k_, g_, z_, x_, B_, C_, dt_, la_ = qtT[pb], ktT[pb], gs[pb], zs[pb], xsT[pb], BT[pb], CT[pb], dtt[pb], latok[pb]
            Mgs = [Mg, ATm]
            Mgn = ["Mg", "ATm"]
            t1 = tmp4
            y1_ = tg[:, :, :].rearrange("p a (b c) -> p (a b) c", c=128)
            diffs = [tmp4[:, :, :], y1_]
            diffn = ["tmp4", "tg"]
            def chunk_gen(ci):
                sl = slice(ci * 128, (ci + 1) * 128)
                vn = "vtok%d" % ci
                b, bn = bank(pool=POOL_B)
                k.mmg([(b[:, :], xs_[:, kc, sl], w_in_bf[:, kc, C_V:C_V + 512], kc == 0, kc == 7) for kc in range(8)],
                      wres(C_V, C_V + 512) + [xsn], [bn])
                k.cp("act", vtok[:, ci, :], b[:, :], [bn], [vn])
                ptk = pt[:, 0:256]
                k.trg([(ptk[:, cc * 128:(cc + 1) * 128], k_[:, cc, sl], ident_bf[:]) for cc in range(2)], [kn, "ident_bf"], ["pt"])
                k.cp("act", ktok[:, :], ptk, ["pt"], ["ktok"])
                bc_, bcn = bank(pool=POOL_B)
                k.mmg([(bc_[:, 0:8], tri_f[:, :], la_[:, ci, :], True, True)], ["tri_f", lan], [bcn])
                k.ts("dve", ccol[:, :], bc_[:, 0:8], -1.0, None, ALU.mult, None, [bcn], ["ccol"])
                yield
                for g in range(2):
                    pr = slice(64 * g, 64 * g + 64)
                    bcu, bcun = bank(pool=POOL_B)
                    bcuv = v4(bcu[:, :], 4)
                    k.mmg([(bcuv[:, hh, :], la_[:, ci, 4 * g + hh:4 * g + hh + 1].broadcast_to([128, 128]), tri_f[:, :], True, True)
                           for hh in range(4)], [lan, "tri_f"], [bcun])
                    for hh in range(4):
                        k.ts("dve", diffs[g][:, hh, :], bcuv[:, hh, :], ccol[:, 4 * g + hh:4 * g + hh + 1], 0.0, ALU.add, ALU.min,
                             [bcun, "ccol"], [diffn[g]])
                    k.act(decay[:, 4 * g:4 * g + 4, :], diffs[g], AF.Exp, [diffn[g]], ["decay%d" % g])
                    k.act(expcum[pr, :, :], bcuv[pr, :, :], AF.Exp, [bcun], ["rs%d" % g])
                yield
                bx, bxn = bank(pool=POOL_B)
                k.trg([(bx[:, cc * 128:(cc + 1) * 128], x_[:, cc, sl], ident_f[:]) for cc in range(4)], [xsn_, "ident_f"], [bxn])
                k.tt("dve", xdt[:, :, :], bx[:, :].rearrange("p (h q) -> p h q", h=8),
                     dt_[:, ci, :].unsqueeze(2).broadcast_to([128, 8, 64]), ALU.mult, [bxn, dtn], ["xdt"])
                ptb = pt[:, 256:384]
                k.trg([(ptb, B_[:, sl], ident_bf[:])], [Bn, "ident_bf"], ["pt"])
                k.cp("act", Btok[:, :], ptb, ["pt"], ["Btok"])
                for g in range(2):
                    pr = slice(64 * g, 64 * g + 64)
                    bsc, bscn = bank(pool=POOL_B)
                    k.mmg([(bsc[:, 0:128], B_[pr, sl], C_[pr, sl], True, True)], [Bn, Cn], [bscn])
                    k.tt("dve", scm[:, g, :], bsc[:, 0:128], mask_bf[:, :], ALU.mult, [bscn, "mask_bf"], ["scm%d" % g])
                yield
                ATv = ATm.rearrange("p (cc hh) i -> p hh cc i", hh=2)
                for hh in range(2):
                    pr = slice(64 * hh, 64 * hh + 64)
                    ba, ban = bank(pool=POOL_B)
                    bav = ba[:, 0:256].rearrange("p (c i) -> p c i", c=2)
                    k.mmg([(bav[:, cc, :], k_[pr, cc, sl], q_[pr, cc, sl], True, True) for cc in range(2)], [kn, qn], [ban])
                    k.tt("dve", ATv[:, hh, :, :], bav, mask_bf[:, :].unsqueeze(1).broadcast_to([128, 2, 128]), ALU.mult,
                         [ban, "mask_bf"], ["ATm"])
                yield
                for hh in range(2):
                    pr = slice(64 * hh, 64 * hh + 64)
                    bo, bon = bank(pool=POOL_B)
                    bov = bo[:, 0:256].rearrange("p (c i) -> p c i", c=2)
                    mms = []
                    for cc in range(2):
                        h = 2 * cc + hh
                        mms.append((bov[:, cc, :], vtok[:, ci, h * 128:(h + 1) * 128], ATm[:, h, :], True, False))
                        mms.append((bov[:, cc, :], S_bf[pr, cc, :], q_[pr, cc, sl], False, True))
                    k.mmg(mms, [vn, "ATm", "S_bf", qn], [bon])
                    k.act(sq[:, 2 * hh:2 * hh + 2, :], bov, AF.Square, [bon], ["sq"])
                    for cc in range(2):
                        k.tt("dve", t1[:, 2 * hh + cc, :], g_[:, 2 * cc + hh, sl], bov[:, cc, :], ALU.mult, [bon, gn], ["tmp4"])
                bk, bkn = bank(pool=POOL_B)
                bkv = bk[:, 0:256].rearrange("p (c e) -> p c e", c=2)
                mms = []
                for hh in range(2):
                    for cc in range(2):
                        h = 2 * cc + hh
                        mms.append((bkv[64 * hh:64 * hh + 64, cc, :], ktok[:, cc * 128 + hh * 64:cc * 128 + hh * 64 + 64],
                                    vtok[:, ci, h * 128:(h + 1) * 128], True, True))
                k.mmg(mms, ["ktok", vn], [bkn])
                k.tt("dve", S_f[:, :, :], S_f[:, :, :], bkv, ALU.add, ["S_f", bkn], ["S_f"])
                k.tt("dve", S_f[:, :, :], S_f[:, :, :], ebl[pb][:, :, ci:ci + 1].broadcast_to([128, 2, 128]), ALU.mult,
                     ["S_f", "ebl" + sfx], ["S_f"])
                k.cp("act", S_bf[:, :, :], S_f[:, :, :], ["S_f"], ["S_bf"])
                bsg, bsgn = bank(hold=True, pool=POOL_B)
                k.mmg([(bsg[:, :], ones_bf[:, :], sq[:, :, :].rearrange("p h i -> p (h i)"), True, True)], ["ones_bf", "sq"], [bsgn])
                k.act(bsg[:, :], bsg[:, :], AF.Ln, [bsgn], [bsgn], scale=1.0 / 128, bias=EPS)
                k.act(bsg[:, :], bsg[:, :], AF.Exp, [bsgn], [bsgn], scale=-0.5)
                yield
                k.tt("dve", CsT[:, :, :], expcum[:, :, :], C_[:, sl].unsqueeze(1).broadcast_to([128, 4, 128]), ALU.mult,
                     ["rs0", "rs1", Cn], ["CsT"])
                for g in range(2):
                    k.tt("dve", Mgs[g], decay[:, 4 * g:4 * g + 4, :], scm[:, g, :].unsqueeze(1).broadcast_to([128, 4, 128]), ALU.mult,
                         ["decay%d" % g, "scm%d" % g], [Mgn[g]])
                k.tt("dve", xw[:, :, :], xdt[:, :, :], decay[:, :, 127:128].broadcast_to([128, 8, 64]), ALU.mult,
                     ["xdt", "decay0", "decay1"], ["xw"])
                k.tt("dve", ogT[:, :, sl].rearrange("p (cc hh) i -> p hh cc i", hh=2),
                     t1[:, :, :].rearrange("p (hh cc) i -> p hh cc i", hh=2),
                     v4(bsg[:, :], 4).rearrange("p (hh cc) i -> p hh cc i", hh=2), ALU.mult, ["tmp4", bsgn], ["ogT"])
                release(bsgn)
                yield
                bh, bhn = bank(hold=True, pool=POOL_B)
                bhv = bh[:, 0:256].rearrange("p (h q) -> p h q", h=4)
                for g in range(2):
                    pr = slice(64 * g, 64 * g + 64)
                    by, byn = bank(pool=POOL_B)
                    byv = by[:, 0:256].rearrange("p (c i) -> p c i", c=2)
                    mms = []
                    for hh in range(4):
                        h = 4 * g + hh
                        po = byv[64 * (h % 2):64 * (h % 2) + 64, hh // 2, :]
                        mms.append((po, xdt[:, h, :], Mgs[g][:, hh, :], True, False))
                        mms.append((po, hT_bf[pr, hh, :], CsT[pr, hh, :], False, True))
                    k.mmg(mms, ["xdt", Mgn[g], "hT_bf", "CsT"], [byn])
                    for c2 in range(2):
                        cc = 2 * g + c2
                        k.stt(y1_[:, cc, :], x_[:, cc, sl], dskc[:, cc:cc + 1], byv[:, c2, :], ALU.mult, ALU.add, [xsn_, "dskc", byn], ["tg"])
                    k.mmg([(bh[pr, 0:256], Btok[:, 64 * g:64 * g + 64], xw[:, 4 * g:4 * g + 4, :].rearrange("p h q -> p (h q)"), True, True)],
                          ["Btok", "xw"], [bhn])
                k.tt("dve", hT_f[:, :, :], hT_f[:, :, :], expcum[:, :, 127:128].broadcast_to([128, 4, 64]), ALU.mult,
                     ["hT_f", "rs0", "rs1"], ["hT_f"])
                k.tt("dve", hT_f[:, :, :], hT_f[:, :, :], bhv, ALU.add, ["hT_f", bhn], ["hT_f"])
                k.cp("act", hT_bf[:, :, :], hT_f[:, :, :], ["hT_f"], ["hT_bf"])
                release(bhn)
                yield
                k.tt("dve", y1_, y1_, z_[:, :, sl], ALU.mult, ["tg", zn], ["tg"])
                k.act(sq[:, :, :], y1_, AF.Square, ["tg"], ["sq"])
                bs, bsn = bank(hold=True, pool=POOL_B)
                bsv = bs[:, 0:256].rearrange("p (g i) -> p g i", g=2)
                mms = []
                for g in range(2):
                    mms.append((bsv[:, g, :], ones_bf[:, :], sq[:, 2 * g, :], True, False))
                    mms.append((bsv[:, g, :], ones_bf[:, :], sq[:, 2 * g + 1, :], False, True))
                k.mmg(mms, ["ones_bf", "sq"], [bsn])
                k.act(bsv, bsv, AF.Ln, [bsn], [bsn], scale=1.0 / 256, bias=EPS)
                k.act(bsv, bsv, AF.Exp, [bsn], [bsn], scale=-0.5)
                k.tt("dve", ygT[:, :, sl].rearrange("p (g t) i -> p g t i", g=2), y1_.rearrange("p (g t) i -> p g t i", g=2),
                     bsv.unsqueeze(2).broadcast_to([128, 2, 2, 128]), ALU.mult, ["tg", bsn], ["ygT"])
                release(bsn)
                yield

            for ci in range(2):
                yield from chunk_gen(ci)

        def stage3_m(s):
            pb = s % 2
            xs_ = xnT[pb]
            xsn = "xnT%d" % pb
            for ti in range(2):
                tile_idx = 2 * s + ti
                k.dma("sp", "xr%d" % ti, [(xr[ti][:], xp[tile_idx * 128:(tile_idx + 1) * 128, :])], [], ["xr%d" % ti])
            for m in range(8):
                ms = slice(m * 128, (m + 1) * 128)
                bg, bgn = bank(pool=POOL_B)
                bgv = v4(bg[:, :], 2)
                mms = [(bgv[:, 0, :], w_in_bf[:, kc, C_GA + m * 128:C_GA + (m + 1) * 128], xs_[:, kc, :], kc == 0, kc == 7) for kc in range(8)]
                mms += [(bgv[:, 1, :], w_in_bf[:, kc, C_GB + m * 128:C_GB + (m + 1) * 128], xs_[:, kc, :], kc == 0, kc == 7) for kc in range(8)]
                k.mmg(mms, wres(C_GA + m * 128, C_GA + (m + 1) * 128) + wres(C_GB + m * 128, C_GB + (m + 1) * 128) + [xsn], [bgn])
                bab, babn = bank(pool=POOL_B)
                babv = v4(bab[:, :], 2)
                mms = [(babv[:, 0, :], wbrg[:, cc, ms], ogT[:, cc, :], cc == 0, cc == 3) for cc in range(4)]
                mms += [(babv[:, 1, :], wbrs[:, cc, ms], ygT[:, cc, :], cc == 0, cc == 3) for cc in range(4)]
                k.mmg(mms, ["wbrg", "wbrs", "ogT", "ygT"], [babn])
                k.act(tg[:, :, :], bgv, AF.Tanh, [bgn], ["tg"], scale=0.5)
                k.stt(tg[:, :, :], tg[:, :, :], 1.0, babv, ALU.add, ALU.mult, ["tg", babn], ["tg"])
                k.tt("pool", mrgT[:, m, :], tg[:, 0, :], tg[:, 1, :], ALU.add, ["tg"], ["mrgT"])
                yield
        def stage3_tail(s):
            for ti in range(2):
                tile_idx = 2 * s + ti
                slot = tile_idx % 2
                xrn = "xr%d" % slot
                for half in range(2):
                    b, bn = bank(pool=POOL_B)
                    hs = slice(half * 512, (half + 1) * 512)
                    k.mmg([(b[:, :], mrgT[:, kc, ti * 128:(ti + 1) * 128], wout[:, kc, hs], kc == 0, kc == 7) for kc in range(8)],
                          ["mrgT", "wout0", "wout1"], [bn])
                    k.stt(xr[slot][:, hs], b[:, :], 0.5, xr[slot][:, hs], ALU.mult, ALU.add, [bn, xrn], [xrn])
                yield
            for ti in range(2):
                tile_idx = 2 * s + ti
                slot = tile_idx % 2
                xrn = "xr%d" % slot
                rows = slice(tile_idx * 128, (tile_idx + 1) * 128)
                k.act(mrgT[:, :, ti * 128:(ti + 1) * 128], xr[slot][:].rearrange("p (a b) -> p a b", a=8), AF.Square, [xrn], ["mrgT", "ss1"],
                      accum_out=ss[:, 1:2])
                rstd_from_ss(1, D, 1)
                yield
                k.stt(xr[slot][:], xr[slot][:], sm[:, 1:2], fng_rep[:], ALU.mult, ALU.mult, [xrn, "sm1", "fng_rep"], [xrn])
                k.dma("pool", xrn + "_st", [(yp[rows, :], xr[slot][:])], [xrn], [])
                yield

        def late_weights():
            for (c0, c1) in col_blocks[N_EARLY:]:
                wdma("w_in_%d" % c0, w_in_bf[:, :, c0:c1], w_in_v[:, :, c0:c1])
            wdma("wbrg", wbrg[:, :, :], w_brg_d.rearrange("(kc p) c -> p kc c", p=128))
            wdma("wbrs", wbrs[:, :, :], w_brs_d.rearrange("(kc p) c -> p kc c", p=128))
            wov = w_out_d.rearrange("(kc p) c -> p kc c", p=128)
            wdma("wout0", wout[:, 0:4, :], wov[:, 0:4, :])
            wdma("wout1", wout[:, 4:8, :], wov[:, 4:8, :])

        def run(gen):
            for _ in gen:
                pass

        def chain(*gens):
            for gg in gens:
                yield from gg

        def interleave(gp, gf, rp=2, rf=1):
            dp = df = False
            while not (dp and df):
                for _ in range(rp):
                    if not dp:
                        try:
                            next(gp)
                        except StopIteration:
                            dp = True
                for _ in range(rf):
                    if not df:
                        try:
                            next(gf)
                        except StopIteration:
                            df = True

        nsc = NSC if stop is None else stop[0]
        if nsc > 0 and (stop is None or stop[1] >= 1):
            g1a = stage1a(0)
            for _ in range(6):
                next(g1a)
            interleave(g1a, chain(stage1b(0), stage1gz(0)), 2, 1)
            silu_gz(0)
            late_weights()
            def fold_gains():
                for cc in range(4):
                    k.ts("pool", wbrg[:, cc, :], wbrg[:, cc, :], prm[:, 16 + cc:17 + cc], None, ALU.mult, None, ["wbrg", "prm"], ["wbrg"])
                    k.ts("pool", wbrs[:, cc, :], wbrs[:, cc, :], prm[:, 20 + cc:21 + cc], None, ALU.mult, None, ["wbrs", "prm"], ["wbrs"])

            tail_prev = iter(())
            for s in range(nsc):
                nxt = s + 1 < nsc
                if stop is None or stop[1] >= 2:
                    interleave(stage2(s), chain(tail_prev, chain(stage1a(s + 1), stage1gz(s + 1)) if nxt else iter(())), 2, 5)
                tail_prev = iter(())
                if stop is None or stop[1] >= 3:
                    if s == 0:
                        fold_gains()
                    if nxt:
                        silu_gz(s + 1)
                    interleave(stage3_m(s), stage1b(s + 1) if nxt else iter(()), 1, 3)
                    tail_prev = stage3_tail(s)
            run(tail_prev)

        if lvl >= 3:
            k.dma("sp", "S_f", [(glap.rearrange("(cc p) e -> p cc e", p=128), S_f[:, :, :])], ["S_f"], [])
            bt_, btn = bank()
            btv = bt_[0:64, :].rearrange("p (hh g n) -> p hh g n", hh=4, g=2)
            k.trg([(bt_[0:64, hh * 128:(hh + 1) * 128], hT_f[:, hh, :], ident_f[:, :]) for hh in range(4)], ["hT_f", "ident_f"], [btn])
            hout = tmp4[0:64, :, :].rearrange("p a b -> p (a b)").rearrange("p (h n) -> p h n", h=8)
            k.cp("dve", hout.rearrange("p (g hh) n -> p hh g n", g=2), btv, [btn], ["tmp4"])
            k.dma("sp", "tmp4", [(ssmp.rearrange("(h p) n -> p h n", p=64), hout)], ["tmp4"], [])
        if lvl >= 4:
            bcv, bcvn = bank()
            k.mmg([(bcv[0:4, cc * 128:(cc + 1) * 128], xbc_raw[:, cc, 0:4], ident_f[:], True, True) for cc in range(4)], ["xbc_raw", "ident_f"], [bcvn])
            bcw, bcwn = bank()
            k.mmg([(bcw[0:4, (cc - 4) * 128:(cc - 3) * 128], xbc_raw[:, cc, 0:4], ident_f[:], True, True) for cc in (4, 5)], ["xbc_raw", "ident_f"], [bcwn])
            cst = diff[0:3, :, :].rearrange("p a b -> p (a b)")
            cst2 = expcum[0:3, 0:2, :].rearrange("p a b -> p (a b)")
            k.cp("dve", cst, bcv[0:3, :], [bcvn], ["tmp4"])
            k.cp("dve", cst2, bcw[0:3, 0:256], [bcwn], ["rs"])
            k.dma("sp", "tmp4", [(convp[:, 0:512], cst)], ["tmp4"], [])
            k.dma("sp", "rs", [(convp[:, 512:768], cst2)], ["rs"], [])

        if with_sample:
            sample_phase(nc, P, k, locals())

        P.finish()
        P.emit(st)
    return nc


def sample_phase(nc, P, k, L):
    g = lambda n: L[n]
    bank, release = g("bank"), g("release")
    xt, xr, xn, ss, sm, pt = g("xt"), g("xr"), g("xn"), g("ss"), g("sm"), g("pt")
    junk = g("scrB")
    ident_f, ident_bf, ones_f, negh = g("ident_f"), g("ident_bf"), g("ones_f"), g("negh")
    w_in_bf, wbrg, wbrs, wout, wa2 = g("w_in_bf"), g("wbrg"), g("wbrs"), g("wout"), g("wa2")
    prm, dtb_rep, A_rep, dsk_rep, fng_rep = g("prm"), g("dtb_rep"), g("A_rep"), g("dsk_rep"), g("fng_rep")
    tg, rs, tmp4, alr, cacc, bpos, eb, enb, xbc_raw, Btok = (g(n) for n in
        ("tg", "rs", "tmp4", "alr", "cacc", "bpos", "eb", "enb", "xbc_raw", "Btok"))
    xsT = g("xsT")[0]
    xs_d, sgla, sssm, sconv, b_a2 = g("xs_d"), g("sgla"), g("sssm"), g("sconv"), g("b_a2")
    ys_o, glas, ssms, convs = g("ys_o"), g("glas"), g("ssms"), g("convs")
    wres = g("wres")
    ng = prm[:, 0:8]
    cb = prm[:, 10:16]
    cw = g("cw")

    def dscr(n, shape):
        return nc.dram_tensor(n, shape, F32, kind="Internal").ap()
    d_q, d_k, d_a = dscr("d_q", [NS, 256]), dscr("d_k", [NS, 256]), dscr("d_a", [NS, 256])
    d_v, d_g, d_z = dscr("d_v", [NS, 512]), dscr("d_g", [NS, 512]), dscr("d_z", [NS, 512])
    d_xs, d_xd = dscr("d_xs", [NS, 512]), dscr("d_xd", [NS, 512])
    d_B, d_C = dscr("d_B", [NS, 128]), dscr("d_C", [NS, 128])
    d_dt, d_dA = dscr("d_dt", [NS, 8]), dscr("d_dA", [NS, 8])

    def dap(t, off, pat):
        return bass.AP(t.tensor, off, pat)

    fx = g("xsT")[1][:, :, :].rearrange("p a b -> p (a b)")
    q1, k1, a1 = fx[:, 0:32], fx[:, 32:64], fx[:, 64:96]
    v1 = fx[:, 96:224]
    g3 = fx[0:64, 224:352]
    x2, xd2, z2, B2, C2 = fx[:, 352:416], fx[:, 416:480], fx[:, 480:544], fx[:, 544:608], fx[:, 608:672]
    dtA2 = fx[:, 672:674]
    xdt2, y2s = fx[:, 674:738], fx[:, 738:802]
    Pm = fx[:, 802:866]
    Gm = fx[:, 866:994]
    Em = g("S_f")[0:32, :, :].rearrange("p a b -> p (a b)")[:, 0:128]
    bq = g("qtT")[0][:, :, :].rearrange("p a b -> p (a b)")
    xnTs = bq[:, 0:128].rearrange("p (k s) -> p k s", k=8)
    ynd = bq[:, 128:256]
    ynT2 = bq[:, 256:384]
    ogTs = bq[:, 384:448].rearrange("p (h s) -> p h s", h=4)
    mTs = g("ktT")[0][:, 0, 0:128].rearrange("p (k s) -> p k s", k=8)

    P.barrier()

    xnTl = g("xnT")
    Sslot = [xt[0][:, :], xt[1][:, :],
             xnTl[0][:, :, :].rearrange("p a b -> p (a b)").bitcast(F32), xnTl[1][:, :, :].rearrange("p a b -> p (a b)").bitcast(F32)]
    Sname = ["xt0", "xt1", "xnT0", "xnT1"]
    for dq in range(4):
        k.dma("sp", Sname[dq], [(Sslot[dq][64 * dhi:64 * dhi + 64, :], sgla[:, dhi, dq * 1024:(dq + 1) * 1024]) for dhi in range(2)],
              [], [Sname[dq]])

    x_s = xbc_raw[0:NS, :, :].rearrange("p a b -> p (a b)")[:, 0:D]
    stg_tiles = [tg[0:NS, :, :].rearrange("p a b -> p (a b)"), rs[0:NS, 0:4, :].rearrange("p a b -> p (a b)")]
    stg_names = ["tg", "rs"]
    ustage = cacc
    u_s = xsT[0:NS, :, :].rearrange("p a b -> p (a b)")[:, 0:768]
    xn_s = xn[0:NS, :]

    k.dma("sp", "xbc_raw", [(x_s, xs_d)], [], ["xbc_raw"])
    k.act(junk[0:NS, :], x_s, AF.Square, ["xbc_raw"], ["junk", "ss"], accum_out=ss[0:NS, 2:3])
    k.ts("pool", sm[0:NS, 2:3], ss[0:NS, 2:3], 1.0 / D, EPS, ALU.mult, ALU.add, ["ss"], ["sm"])
    k.tt("pool", sm[0:NS, 2:3], sm[0:NS, 2:3], negh[0:NS, 0:1], ALU.pow, ["sm", "negh"], ["sm"])
    k.ts("pool", xn_s, x_s, sm[0:NS, 2:3], None, ALU.mult, None, ["xbc_raw", "sm"], ["xn"])
    ptv = pt[:, 0:8 * NS].rearrange("p (kc t) -> p kc t", kc=8)
    k.trg([(ptv[:, kc, :], xn[0:NS, kc * 128:(kc + 1) * 128], ident_bf[0:NS, 0:NS]) for kc in range(8)], ["xn", "ident_bf"], ["pt"])
    k.tt("dve", xnTs[:, :, :], ptv, ng.unsqueeze(2).broadcast_to([128, 8, NS]), ALU.mult, ["pt", "prm"], ["xnTs"])

    si = [0]

    def proj_tm(c0, w):
        b, bn = bank()
        k.mmg([(b[0:NS, 0:w], xnTs[:, kc, :], w_in_bf[:, kc, c0:c0 + w], kc == 0, kc == 7) for kc in range(8)],
              wres(c0, c0 + w) + ["xnTs"], [bn])
        return b, bn

    def proj_to_dram(c0, w, dsts):
        b, bn = proj_tm(c0, w)
        i = si[0] % 2
        si[0] += 1
        k.cp("act", stg_tiles[i][:, 0:w], b[0:NS, 0:w], [bn], [stg_names[i]])
        off = 0
        for (d, dw) in dsts:
            k.dma("sp", d.tensor.name, [(d, stg_tiles[i][:, off:off + dw])], [stg_names[i]], [d.tensor.name])
            off += dw

    proj_to_dram(C_Q, 512, [(d_q, 256), (d_k, 256)])
    proj_to_dram(C_V, 512, [(d_v, 512)])
    proj_to_dram(C_G, 512, [(d_g, 512)])
    proj_to_dram(C_Z, 512, [(d_z, 512)])
    for (c0, w, o) in ((C_X, 512, 0), (C_X + 512, 256, 512)):
        b, bn = proj_tm(c0, w)
        k.cp("act", u_s[:, o:o + w], b[0:NS, 0:w], [bn], ["xsT"])
    k.dma("sp", "xsT", [(convs[:, 2, :], u_s)], ["xsT"], [])
    k.dma("sp", "convs01", [(convs[:, 0:2, :], sconv[:, 1:3, :])], [], [])
    b, bn = proj_tm(C_A, 16)
    k.cp("dve", sm[0:NS, 8:24], b[0:NS, 0:16], [bn], ["sm_a"])
    b, bn = proj_tm(C_DT, 8)
    k.tt("dve", sm[0:NS, 24:32], b[0:NS, 0:8], dtb_rep[0:NS, :], ALU.add, [bn, "dtb_rep"], ["sm_dt"])
    k.act(sm[0:NS, 24:32], sm[0:NS, 24:32], AF.Exp, ["sm_dt"], ["sm_dt"])
    k.act(sm[0:NS, 24:32], sm[0:NS, 24:32], AF.Ln, ["sm_dt"], ["sm_dt"], bias=1.0)
    k.tt("dve", sm[0:NS, 32:40], sm[0:NS, 24:32], A_rep[0:NS, :], ALU.mult, ["sm_dt", "A_rep"], ["sm_dA"])
    k.act(sm[0:NS, 32:40], sm[0:NS, 32:40], AF.Exp, ["sm_dA"], ["sm_dA"])
    k.dma("sp", "d_dt", [(d_dt, sm[0:NS, 24:32])], ["sm_dt"], ["d_dt"])
    k.dma("sp", "d_dA", [(d_dA, sm[0:NS, 32:40])], ["sm_dA"], ["d_dA"])

    b, bn = bank()
    k.trg([(b[0:16, 0:NS], sm[0:NS, 8:24], ident_f[0:NS, 0:NS])], ["sm_a", "ident_f"], [bn])
    k.cp("act", alr[0:16, 0:NS], b[0:16, 0:NS], [bn], ["alr"])
    b, bn = bank()
    k.mmg([(b[0:NS, 0:256], alr[0:16, 0:NS], wa2[0:16, :], True, True)], ["alr", "wa2"], [bn])
    ba2r = cacc[0:NS, 0:256]
    k.dma("sp", "cacc", [(ba2r, bc_rows(b_a2, NS, 256))], [], ["cacc"])
    a_s = bpos[0:NS, 0, :]
    k.tt("dve", a_s, b[0:NS, 0:256], ba2r, ALU.add, [bn, "cacc"], ["bpos"])
    k.act(a_s, a_s, AF.Exp, ["bpos"], ["bpos"], scale=-1.0)
    k.act(a_s, a_s, AF.Ln, ["bpos"], ["bpos"], bias=1.0)
    k.act(a_s, a_s, AF.Exp, ["bpos"], ["bpos"], scale=-1.0 / 16)
    k.dma("sp", "d_a", [(d_a, a_s)], ["bpos"], ["d_a"])

    bufrows = xr[0][0:48, 0:768]
    k.dma("sp", "xr0", [(bufrows, sconv.rearrange("s r c -> (s r) c"))], [], ["xr0"])
    b, bn = bank()
    k.trg([(b[:, cc * 48:(cc + 1) * 48], xr[0][0:48, cc * 128:(cc + 1) * 128], ident_f[0:48, 0:48]) for cc in range(6)],
          ["xr0", "ident_f"], [bn])
    bufT = rs[:, 0:3, :].rearrange("p a b -> p (a b)")[:, 0:288]
    k.cp("dve", bufT, b[:, 0:288], [bn], ["rs"])
    bufT4 = bufT.rearrange("p (c s r) -> p c s r", c=6, s=NS)
    bu, bun = bank()
    buv = bu[:, 0:6 * NS].rearrange("p (c s) -> p c s", c=6)
    for cc in range(6):
        k.mmg([(buv[:, cc, :], w_in_bf[:, kc, C_X + cc * 128:C_X + (cc + 1) * 128], xnTs[:, kc, :], kc == 0, kc == 7) for kc in range(8)],
              wres(C_X + cc * 128, C_X + (cc + 1) * 128) + ["xnTs"], [bun])
    accs = eb[:, 0, 0:6 * NS].rearrange("p (c s) -> p c s", c=6)
    xcT = enb[:, 0, 0:6 * NS].rearrange("p (c s) -> p c s", c=6)
    for cc in range(6):
        k.ts("dve", accs[:, cc, :], bufT4[:, cc, :, 0], cw(0, cc), cb[:, cc:cc + 1], ALU.mult, ALU.add, ["rs", "prm"], ["eb"])
        for w in (1, 2):
            k.stt(accs[:, cc, :], bufT4[:, cc, :, w], cw(w, cc), accs[:, cc, :], ALU.mult, ALU.add, ["rs", "prm", "eb"], ["eb"])
        k.stt(accs[:, cc, :], buv[:, cc, :], cw(3, cc), accs[:, cc, :], ALU.mult, ALU.add, [bun, "prm", "eb"], ["eb"])
    k.act(xcT, accs, AF.Silu, ["eb"], ["enb"])
    bA, bAn = bank()
    k.trg([(bA[0:NS, cc * 128:(cc + 1) * 128], xcT[:, cc, :], ident_f[:, :]) for cc in range(4)], ["enb", "ident_f"], [bAn])
    bB, bBn = bank()
    k.trg([(bB[0:NS, (cc - 4) * 128:(cc - 3) * 128], xcT[:, cc, :], ident_f[:, :]) for cc in (4, 5)], ["enb", "ident_f"], [bBn])
    xcs = xr[1][0:NS, 0:768]
    k.cp("act", xcs[:, 0:512], bA[0:NS, 0:512], [bAn], ["xr1"])
    k.cp("act", xcs[:, 512:768], bB[0:NS, 0:256], [bBn], ["xr1"])
    xd_s = xr[1][0:NS, 768:1024]
    xdfull = tmp4[0:NS, :, :].rearrange("p a b -> p (a b)")
    k.tt("dve", xdfull.rearrange("p (h q) -> p h q", h=8), xcs[:, 0:512].rearrange("p (h q) -> p h q", h=8),
         dsk_rep[0:NS, :].unsqueeze(2).broadcast_to([NS, 8, 64]), ALU.mult, ["xr1", "dsk_rep"], ["tmp4"])
    k.dma("sp", "d_xs", [(d_xs, xcs[:, 0:512])], ["xr1"], ["d_xs"])
    k.dma("sp", "d_B", [(d_B, xcs[:, 512:640])], ["xr1"], ["d_B"])
    k.dma("sp", "d_C", [(d_C, xcs[:, 640:768])], ["xr1"], ["d_C"])
    k.dma("sp", "d_xd", [(d_xd, xdfull)], ["tmp4"], ["d_xd"])

    for dhi in range(2):
        pr = slice(64 * dhi, 64 * dhi + 64)
        k.dma("sp", "q1", [(q1[pr, :], dap(d_q, dhi * 32, [[64, 64], [1, 32]])),
                           (k1[pr, :], dap(d_k, dhi * 32, [[64, 64], [1, 32]])),
                           (a1[pr, :], dap(d_a, dhi * 32, [[64, 64], [1, 32]])),
                           (v1[pr, :], dap(d_v, 0, [[128, 64], [1, 128]]))],
              ["d_q", "d_k", "d_a", "d_v"], ["q1"])
    k.dma("sp", "g3", [(g3[:, :], dap(d_g, 0, [[128, 64], [1, 128]]))], ["d_g"], ["g3"])
    k.memset("dve", Pm[:, :], 0.0, ["Pm"])
    k.cp("dve", Pm[0:64, :], ident_f[0:64, 0:64], ["ident_f", "Pm"], ["Pm"])
    k.cp("dve", Pm[64:128, :], ident_f[64:128, 64:128], ["ident_f", "Pm"], ["Pm"])
    pacc = rs
    for dq in range(4):
        sl_ = dq % 2
        Sn, Tn = Sname[dq], "xr%d" % sl_
        Sf = Sslot[dq]
        S3 = Sf.rearrange("p (d e) -> p d e", d=8)
        T3 = xr[sl_][:, :].rearrange("p (d e) -> p d e", d=8)
        dsl = slice(dq * 8, dq * 8 + 8)
        k.tt("dve", T3, k1[:, dsl].unsqueeze(2).broadcast_to([128, 8, 128]), v1[:, :].unsqueeze(1).broadcast_to([128, 8, 128]),
             ALU.mult, ["q1"], [Tn])
        k.tt("dve", S3, S3, a1[:, dsl].unsqueeze(2).broadcast_to([128, 8, 128]), ALU.mult, [Sn, "q1"], [Sn])
        k.tt("dve", Sf, Sf, xr[sl_][:, :], ALU.add, [Sn, Tn], [Sn])
        k.dma("act", Sn + "_st", [(glas[:, dhi, dq * 1024:(dq + 1) * 1024], Sf[64 * dhi:64 * dhi + 64, :]) for dhi in range(2)], [Sn], [])
        k.tt("dve", T3, S3, q1[:, dsl].unsqueeze(2).broadcast_to([128, 8, 128]), ALU.mult, [Sn, "q1"], [Tn])

        def red(e, T3=T3, dq=dq):
            return e.tensor_reduce(out=pacc[:, dq, :], in_=T3.rearrange("p d e -> p e d"), axis=AX.X, op=ALU.add)
        P.op("dve", red, [Tn], ["rs"])
    po1 = tmp4[:, 0, :]

    def red2(e):
        return e.tensor_reduce(out=po1, in_=pacc[:, :, :].rearrange("p q e -> p e q"), axis=AX.X, op=ALU.add)
    P.op("dve", red2, ["rs"], ["tmp4"])
    b, bn = bank()
    k.mmg([(b[0:64, 0:128], Pm[:, :], po1, True, True)], ["Pm", "tmp4"], [bn])
    o3s = tmp4[0:64, 1, :]
    k.act(o3s, b[0:64, 0:128], AF.Identity, [bn], ["tmp4b"], scale=0.125)
    k.act(junk[0:64, 0:128], o3s, AF.Square, ["tmp4b"], ["junk", "ss"], accum_out=ss[0:64, 3:4])
    k.ts("pool", sm[0:64, 3:4], ss[0:64, 3:4], 1.0 / 128, EPS, ALU.mult, ALU.add, ["ss"], ["sm"])
    k.tt("pool", sm[0:64, 3:4], sm[0:64, 3:4], negh[0:64, 0:1], ALU.pow, ["sm", "negh"], ["sm"])
    k.act(g3[:, :], g3[:, :], AF.Silu, ["g3"], ["g3"])
    k.stt(Btok[0:64, :], o3s, sm[0:64, 3:4], g3[:, :], ALU.mult, ALU.mult, ["tmp4b", "sm", "g3"], ["Btok"])
    k.trg([(pt[:, 0:64], Btok[0:64, :], ident_bf[0:64, 0:64])], ["Btok", "ident_bf"], ["pt"])
    k.cp("dve", ogTs[:, :, :], pt[:, 0:64].rearrange("p (s h) -> p h s", h=4), ["pt"], ["ogTs"])

    for hh in range(4):
        pr = slice(32 * hh, 32 * hh + 32)
        k.dma("sp", "x2", [(x2[pr, :], dap(d_xs, hh * 64, [[256, 32], [1, 64]])),
                           (xd2[pr, :], dap(d_xd, hh * 64, [[256, 32], [1, 64]])),
                           (z2[pr, :], dap(d_z, hh * 64, [[256, 32], [1, 64]])),
                           (B2[pr, :], dap(d_B, 0, [[64, 32], [1, 64]])),
                           (C2[pr, :], dap(d_C, 0, [[64, 32], [1, 64]])),
                           (dtA2[pr, 0:1], dap(d_dt, hh, [[4, 32], [1, 1]])),
                           (dtA2[pr, 1:2], dap(d_dA, hh, [[4, 32], [1, 1]]))],
              ["d_xs", "d_xd", "d_z", "d_B", "d_C", "d_dt", "d_dA"], ["x2"], slow=True)
    k.ts("dve", xdt2[:, :], x2[:, :], dtA2[:, 0:1], None, ALU.mult, None, ["x2"], ["xdt2"])
    for pq in range(4):
        k.dma("sp", Sname[pq], [(Sslot[pq][32 * hh:32 * hh + 32, :], sssm[:, hh, pq * 1024:(pq + 1) * 1024]) for hh in range(4)],
              [], [Sname[pq]])
    for pq in range(4):
        sl_ = pq % 2
        Hn, Tn = Sname[pq], "xr%d" % sl_
        Hf = Sslot[pq]
        H3 = Hf.rearrange("p (q n) -> p q n", q=16)
        T3 = xr[sl_][:, :].rearrange("p (q n) -> p q n", q=16)
        psl = slice(pq * 16, pq * 16 + 16)
        k.tt("dve", T3, xdt2[:, psl].unsqueeze(2).broadcast_to([128, 16, 64]), B2[:, :].unsqueeze(1).broadcast_to([128, 16, 64]),
             ALU.mult, ["xdt2", "x2"], [Tn])
        k.stt(Hf, Hf, dtA2[:, 1:2], xr[sl_][:, :], ALU.mult, ALU.add, [Hn, Tn, "x2"], [Hn])
        k.dma("act", Hn + "_st", [(ssms[:, hh, pq * 1024:(pq + 1) * 1024], Hf[32 * hh:32 * hh + 32, :]) for hh in range(4)], [Hn], [])
        k.tt("dve", T3, H3, C2[:, :].unsqueeze(1).broadcast_to([128, 16, 64]), ALU.mult, [Hn, "x2"], [Tn])

        def redy(e, T3=T3, psl=psl):
            return e.tensor_reduce(out=y2s[:, psl], in_=T3, axis=AX.X, op=ALU.add)
        P.op("dve", redy, [Tn], ["y2s"])
    k.tt("dve", y2s[:, :], y2s[:, :], xd2[:, :], ALU.add, ["y2s", "x2"], ["y2s"])
    k.act(z2[:, :], z2[:, :], AF.Silu, ["x2"], ["z2"])
    k.tt("dve", y2s[:, :], y2s[:, :], z2[:, :], ALU.mult, ["y2s", "z2"], ["y2s"])
    k.memset("dve", ss[:, 4:6], 0.0, ["ss"])
    k.act(junk[:, 0:64], y2s[:, :], AF.Square, ["y2s"], ["junk", "ss"], accum_out=ss[:, 4:5])
    for j in range(4):
        k.cp("dve", Em[:, j * 32:(j + 1) * 32], ident_f[0:32, 0:32], ["ident_f"], ["Em"])
    b, bn = bank()
    k.mmg([(b[:, 0:128], Em[:, :], Em[:, :], True, True)], ["Em"], [bn])
    k.cp("dve", Gm[:, :], b[:, 0:128], [bn], ["Gm"])
    b, bn = bank()
    k.mmg([(b[:, 0:2], Gm[:, :], ss[:, 4:6], True, True)], ["Gm", "ss"], [bn])
    k.cp("dve", ss[:, 6:8], b[:, 0:2], [bn], ["ss2"])
    k.ts("pool", sm[:, 4:5], ss[:, 6:7], 1.0 / 256, EPS, ALU.mult, ALU.add, ["ss2"], ["sm"])
    k.tt("pool", sm[:, 4:5], sm[:, 4:5], negh[:, 0:1], ALU.pow, ["sm", "negh"], ["sm"])
    for j in range(2):
        k.ts("dve", ynd[:, j * 64:(j + 1) * 64], y2s[:, :], sm[:, 4:5], None, ALU.mult, None, ["y2s", "sm"], ["ynd"])
    k.trg([(pt[:, 128:256], ynd[:, :], ident_bf[:, :])], ["ynd", "ident_bf"], ["pt"])
    k.cp("dve", ynT2[:, :], pt[:, 128:256], ["pt"], ["ynT2"])

    mrg_s = xn[0:NS, :]
    for half in range(2):
        hs = slice(half * 512, (half + 1) * 512)
        bA, bAn = bank()
        k.mmg([(bA[0:NS, :], ogTs[:, h, :], wbrg[:, h, hs], h == 0, h == 3) for h in range(4)], ["ogTs", "wbrg"], [bAn])
        bBs = []
        for par in range(2):
            pr = slice(64 * par, 64 * par + 64)
            bB, bBn = bank()
            heads = [(g_, hh) for g_ in range(2) for hh in range(4) if hh % 2 == par]
            mms = []
            for i, (g_, hh) in enumerate(heads):
                c0 = hh * 32 + g_
                mms.append((bB[0:NS, :], ynT2[pr, c0:c0 + 31:2], wbrs[pr, 2 * g_ + hh // 2, hs], i == 0, i == len(heads) - 1))
            k.mmg(mms, ["ynT2", "wbrs"], [bBn])
            bBs.append((bB, bBn))
        gts = []
        for (cg, nm) in ((C_GA, "tg"), (C_GB, "rs")):
            bG, bGn = bank()
            c0 = cg + half * 512
            k.mmg([(bG[0:NS, :], xnTs[:, kc, :], w_in_bf[:, kc, c0:c0 + 512], kc == 0, kc == 7) for kc in range(8)],
                  wres(c0, c0 + 512) + ["xnTs"], [bGn])
            tl = stg_tiles[0] if nm == "tg" else stg_tiles[1]
            k.act(tl, bG[0:NS, :], AF.Tanh, [bGn], [nm], scale=0.5)
            gts.append((tl, nm))
        k.stt(gts[0][0], gts[0][0], 1.0, bA[0:NS, :], ALU.add, ALU.mult, ["tg", bAn], ["tg"])
        yB = tmp4[0:NS, :, :].rearrange("p a b -> p (a b)")
        k.cp("act", yB, bBs[1][0][0:NS, :], [bBs[1][1]], ["tmp4"])
        k.tt("dve", yB, yB, bBs[0][0][0:NS, :], ALU.add, ["tmp4", bBs[0][1]], ["tmp4"])
        k.stt(gts[1][0], gts[1][0], 1.0, yB, ALU.add, ALU.mult, ["rs", "tmp4"], ["rs"])
        k.tt("pool", mrg_s[:, hs], gts[0][0], gts[1][0], ALU.add, ["tg", "rs"], ["xn"])

    ptv = pt[:, 0:8 * NS].rearrange("p (kc t) -> p kc t", kc=8)
    k.trg([(ptv[:, kc, :], xn[0:NS, kc * 128:(kc + 1) * 128], ident_bf[0:NS, 0:NS]) for kc in range(8)], ["xn", "ident_bf"], ["pt"])
    k.cp("dve", mTs[:, :, :], ptv, ["pt"], ["mTs"])
    for half in range(2):
        hs = slice(half * 512, (half + 1) * 512)
        b, bn = bank()
        k.mmg([(b[0:NS, :], mTs[:, kc, :], wout[:, kc, hs], kc == 0, kc == 7) for kc in range(8)], ["mTs", "wout0", "wout1"], [bn])
        k.stt(x_s[:, hs], b[0:NS, :], 0.5, x_s[:, hs], ALU.mult, ALU.add, [bn, "xbc_raw"], ["xbc_raw"])
    k.act(junk[0:NS, :], x_s, AF.Square, ["xbc_raw"], ["junk", "ss"], accum_out=ss[0:NS, 2:3])
    k.ts("pool", sm[0:NS, 2:3], ss[0:NS, 2:3], 1.0 / D, EPS, ALU.mult, ALU.add, ["ss"], ["sm"])
    k.tt("pool", sm[0:NS, 2:3], sm[0:NS, 2:3], negh[0:NS, 0:1], ALU.pow, ["sm", "negh"], ["sm"])
    k.stt(x_s, x_s, sm[0:NS, 2:3], fng_rep[0:NS, :], ALU.mult, ALU.mult, ["xbc_raw", "sm", "fng_rep"], ["xbc_raw"])
    k.dma("sp", "xbc_raw", [(ys_o, x_s)], ["xbc_raw"], [])


_CACHE = {}


def _in_maps(inputs):
    f = lambda a: np.ascontiguousarray(np.asarray(a, dtype=np.float32))
    shared = {
        "norm_g": f(inputs["norm_g"][0]), "w_in": f(inputs["w_in"][0]), "w_a2": f(inputs["w_a2"][0]), "b_a2": f(inputs["b_a2"][0]),
        "gla_norm_g": f(inputs["gla_norm_g"][0]), "conv_w": f(inputs["conv_w"][0]), "conv_b": f(inputs["conv_b"][0]),
        "dt_bias": f(inputs["dt_bias"][0]), "a_log": f(inputs["a_log"][0]), "d_skip": f(inputs["d_skip"][0]),
        "ssd_norm_g": f(inputs["ssd_norm_g"][0]), "w_br_gla": f(inputs["w_br_gla"][0]), "w_br_ssd": f(inputs["w_br_ssd"][0]),
        "w_out": f(inputs["w_out"][0]), "final_norm_g": f(inputs["final_norm_g"]),
    }
    maps = []
    for c in range(8):
        m = dict(shared)
        m["xp"] = f(inputs["x_prompt"][c])
        m["xs"] = f(inputs["x_sample"][16 * c:16 * c + 16, 0])
        m["sgla"] = f(inputs["state_gla"][0, 16 * c:16 * c + 16]).reshape(64, 2, 4096)
        m["sssm"] = f(inputs["state_ssm"][0, 16 * c:16 * c + 16]).reshape(32, 4, 4096)
        m["sconv"] = f(inputs["state_conv"][0, 16 * c:16 * c + 16])
        maps.append(m)
    return maps


def kernel(**inputs):
    if "nc" not in _CACHE:
        _CACHE["nc"] = build_program()
    nc = _CACHE["nc"]
    res = run_bass_kernel_spmd(nc, _in_maps(inputs), core_ids=list(range(8)))
    r = res.results
    y_prompt = np.stack([r[c]["yp"] for c in range(8)], 0)
    y_sample = np.concatenate([r[c]["ys"] for c in range(8)], 0).reshape(128, 1, D)
    gla_p = np.stack([r[c]["glap"].reshape(4, 64, 128) for c in range(8)], 0)[None]
    ssm_p = np.stack([r[c]["ssmp"].reshape(8, 64, 64) for c in range(8)], 0)[None]
    conv_p = np.stack([r[c]["convp"] for c in range(8)], 0)[None]
    gla_s = np.concatenate([r[c]["glas"].reshape(16, 4, 64, 128) for c in range(8)], 0)[None]
    ssm_s = np.concatenate([r[c]["ssms"].reshape(16, 8, 64, 64) for c in range(8)], 0)[None]
    conv_s = np.concatenate([r[c]["convs"] for c in range(8)], 0)[None]
    outs = (y_prompt, y_sample, gla_p, ssm_p, conv_p, gla_s, ssm_s, conv_s)
    return tuple(np.ascontiguousarray(o, dtype=np.float32) for o in outs)
```
